# Optimizing a Trainium2 kernel written in Bass

```python
import jax, jax.numpy as jnp
from jax import lax
import numpy as np

D_MODEL = 1024
BATCH = 2
SEQ = 8192
DEPTH = 2

CTX_LEN = 256
GRID_W = 64
EPS = 1e-6

GLA_HEADS = 4
GLA_DK = 32
GLA_DV = 64
GLA_QK = GLA_HEADS * GLA_DK
W_GLA = GLA_HEADS * GLA_DV
GLA_RANK = 16
GLA_TAU = 16.0
GLA_CHUNK = 64
GLA_GATE_BIAS = 2.0
FFT_GROUPS = 4
FFT_DG = 64
W_FFT = FFT_GROUPS * FFT_DG
W_CONV = 256
CONV_WIDTH = 3
POOL_WINDOWS = (2, 4, 8, 16)
POOL_GROUPS = 4
POOL_DG = 64
W_POOL = POOL_GROUPS * POOL_DG
D_MIX = W_GLA + W_FFT + W_CONV + W_POOL
D_FF = 2816

COL_SIZES = (GLA_QK, W_GLA, GLA_RANK, GLA_RANK, GLA_QK, W_GLA, W_FFT, W_CONV, W_CONV, W_CONV, W_POOL)
D_IN = sum(COL_SIZES)
SPLITS = tuple(int(s) for s in np.cumsum(COL_SIZES)[:-1])
KVA_COLS = GLA_QK + W_GLA + 2 * GLA_RANK

kernel_name = "hybrid_parallel_gla_fnet_conv_pool_dit"


def rmsnorm(x, g):
    xf = x.astype(jnp.float32)
    y = xf * lax.rsqrt(jnp.mean(xf * xf, axis=-1, keepdims=True) + EPS)
    return (y * g.astype(jnp.float32)).astype(x.dtype)


def on_grid(fn, u, grid):
    if not grid:
        return fn(u)
    b, n, ch = u.shape
    rows = n // GRID_W
    return fn(u.reshape(b, rows, GRID_W, ch)).reshape(b, n, ch)


def dwconv3(u, w, bias):
    pad = [(0, 0)] * (u.ndim - 2) + [(1, 1), (0, 0)]
    up = jnp.pad(u, pad)
    return up[..., :-2, :] * w[0] + up[..., 1:-1, :] * w[1] + up[..., 2:, :] * w[2] + bias


def pool_minus_self(u, window):
    n = u.shape[-2]
    t = np.arange(n)
    lo = np.clip(t - window // 2, 0, n - 1)
    hi = np.clip(t + window // 2 - 1, 0, n - 1)
    count = jnp.asarray((hi - lo + 1).astype(np.float32)[:, None])
    uf = u.astype(jnp.float32)
    cs = jnp.cumsum(uf, axis=-2)
    cs = jnp.concatenate([jnp.zeros_like(cs[..., :1, :]), cs], axis=-2)
    total = jnp.take(cs, hi + 1, axis=-2) - jnp.take(cs, lo, axis=-2)
    return (total / count - uf).astype(u.dtype)


def pool_mixer(u, w_pool, scale, grid):
    def f(z):
        return jnp.concatenate(
            [pool_minus_self(z[..., i * POOL_DG:(i + 1) * POOL_DG], w) for i, w in enumerate(POOL_WINDOWS)],
            axis=-1)
    p = on_grid(f, u, grid)
    b, n, _ = u.shape
    y = jnp.einsum('bngc,gcd->bngd', p.reshape(b, n, POOL_GROUPS, POOL_DG), w_pool)
    return y.reshape(b, n, W_POOL) * scale


def fourier_mixer(u, w_f):
    b, n, _ = u.shape
    uf = u.astype(jnp.float32).reshape(b, n, FFT_GROUPS, FFT_DG)
    f = jnp.fft.fft2(uf, axes=(1, 3), norm='ortho').real.astype(u.dtype)
    y = jnp.einsum('bngc,gcd->bngd', f, w_f)
    return y.reshape(b, n, W_FFT)


def gla_kv_decay(p_k, p_v, p_af, p_ab, w_a2, b_a2):
    b, n, _ = p_k.shape
    k = p_k.astype(jnp.float32).reshape(b, n, GLA_HEADS, GLA_DK)
    v = p_v.astype(jnp.float32).reshape(b, n, GLA_HEADS, GLA_DV)
    la_f = jax.nn.log_sigmoid((p_af @ w_a2[0] + b_a2[0]).astype(jnp.float32)) / GLA_TAU
    la_b = jax.nn.log_sigmoid((p_ab @ w_a2[1] + b_a2[1]).astype(jnp.float32)) / GLA_TAU
    return (k, v, la_f.reshape(b, n, GLA_HEADS, GLA_DK), la_b.reshape(b, n, GLA_HEADS, GLA_DK))


def gla_chunked(q, k, v, log_a, h0):
    b, n, h, dk = q.shape
    dv = v.shape[-1]
    nc = n // GLA_CHUNK
    causal = np.tril(np.ones((GLA_CHUNK, GLA_CHUNK), dtype=bool))[None, :, :, None, None]

    def to_chunks(t):
        return t.reshape(b, nc, GLA_CHUNK, h, t.shape[-1]).swapaxes(0, 1)

    def step(state, inp):
        qc, kc, vc, ac = inp
        cum = jnp.cumsum(ac, axis=1)
        o_inter = jnp.einsum('bihk,bhkv->bihv', qc * jnp.exp(cum), state)
        diff = cum[:, :, None] - cum[:, None, :]
        decay = jnp.where(causal, jnp.exp(jnp.minimum(diff, 0.0)), 0.0)
        attn = jnp.einsum('bihk,bjhk,bijhk->bhij', qc, kc, decay)
        o_intra = jnp.einsum('bhij,bjhv->bihv', attn, vc)
        last = cum[:, -1]
        k_dec = kc * jnp.exp(last[:, None] - cum)
        new_state = state * jnp.exp(last)[..., None] + jnp.einsum('bjhk,bjhv->bhkv', k_dec, vc)
        return new_state, o_intra + o_inter

    state, o = lax.scan(step, h0, (to_chunks(q), to_chunks(k), to_chunks(v), to_chunks(log_a)))
    return o.swapaxes(0, 1).reshape(b, n, h, dv), state


def gla_final_state(k, v, log_a):
    cum = jnp.cumsum(log_a, axis=1)
    return jnp.einsum('blhk,blhv->bhkv', k * jnp.exp(cum[:, -1:] - cum), v)


def mixers(p, grid, h0_f, h0_b, w_a2, b_a2, gla_g, fft_w, conv_w, conv_b, pool_w, pool_scale):
    p_k, p_v, p_af, p_ab, p_q, p_g, p_fft, p_h, p_bg, p_cg, p_pool = jnp.split(p, SPLITS, axis=-1)
    b, n, _ = p.shape
    k, v, la_f, la_b = gla_kv_decay(p_k, p_v, p_af, p_ab, w_a2, b_a2)
    q = p_q.astype(jnp.float32).reshape(b, n, GLA_HEADS, GLA_DK) * (GLA_DK ** -0.5)
    o_f, s_f = gla_chunked(q, k, v, la_f, h0_f)
    o_b, s_b = gla_chunked(jnp.flip(q, 1), jnp.flip(k, 1), jnp.flip(v, 1), jnp.flip(la_b, 1), h0_b)
    o = rmsnorm(o_f + jnp.flip(o_b, 1), gla_g)
    o = o * jax.nn.silu(p_g.astype(jnp.float32).reshape(b, n, GLA_HEADS, GLA_DV))
    y_gla = o.reshape(b, n, W_GLA).astype(p.dtype)
    y_fft = fourier_mixer(p_fft, fft_w)
    y_conv = p_bg * on_grid(lambda z: dwconv3(z, conv_w, conv_b), p_cg * p_h, grid)
    y_pool = pool_mixer(p_pool, pool_w, pool_scale, grid)
    return jnp.concatenate([y_gla, y_fft, y_conv, y_pool], axis=-1), (s_f, s_b)


def conv_ffn(h, grid, w_up, cw, cb, w_down):
    a, u = jnp.split(h @ w_up, 2, axis=-1)
    a = on_grid(lambda z: dwconv3(z, cw, cb), a, grid)
    return (jax.nn.silu(a) * u) @ w_down


def setup_inputs(seed: int = 0) -> dict:
    key = jax.random.key(seed)
    ks = jax.random.split(key, 23)
    D = D_MODEL

    def nrm(k, shape, scale):
        return jax.random.normal(k, shape, jnp.float32) * scale

    return {
        'x': nrm(ks[0], (BATCH, SEQ, D), 1.0),
        'c': nrm(ks[1], (BATCH, D), 1.0),
        'ctx': nrm(ks[2], (BATCH, CTX_LEN, D), 1.0),
        'c_ctx': nrm(ks[3], (D,), 1.0),
        'norm1_g': 1.0 + nrm(ks[4], (DEPTH, D), 0.02),
        'norm2_g': 1.0 + nrm(ks[5], (DEPTH, D), 0.02),
        'w_mod': nrm(ks[6], (DEPTH, D, 6 * D), 0.5 * D ** -0.5),
        'b_mod': nrm(ks[7], (DEPTH, 6 * D), 0.02),
        'w_in': nrm(ks[8], (DEPTH, D, D_IN), D ** -0.5),
        'gla_w_a2': nrm(ks[9], (DEPTH, 2, GLA_RANK, GLA_QK), GLA_RANK ** -0.5),
        'gla_b_a2': GLA_GATE_BIAS + nrm(ks[10], (DEPTH, 2, GLA_QK), 0.1),
        'gla_norm_g': 1.0 + nrm(ks[11], (DEPTH, GLA_DV), 0.02),
        'fft_w': nrm(ks[12], (DEPTH, FFT_GROUPS, FFT_DG, FFT_DG), FFT_DG ** -0.5),
        'conv_w': nrm(ks[13], (DEPTH, CONV_WIDTH, W_CONV), CONV_WIDTH ** -0.5),
        'conv_b': nrm(ks[14], (DEPTH, W_CONV), 0.02),
        'pool_w': nrm(ks[15], (DEPTH, POOL_GROUPS, POOL_DG, POOL_DG), POOL_DG ** -0.5),
        'pool_scale': 1.0 + nrm(ks[16], (DEPTH, W_POOL), 0.1),
        'w_out': nrm(ks[17], (DEPTH, D_MIX, D), D_MIX ** -0.5),
        'ffn_w_up': nrm(ks[18], (DEPTH, D, 2 * D_FF), D ** -0.5),
        'ffn_conv_w': nrm(ks[19], (DEPTH, CONV_WIDTH, D_FF), CONV_WIDTH ** -0.5),
        'ffn_conv_b': nrm(ks[20], (DEPTH, D_FF), 0.02),
        'ffn_w_down': nrm(ks[21], (DEPTH, D_FF, D), D_FF ** -0.5),
        'final_norm_g': 1.0 + nrm(ks[22], (D,), 0.02),
    }


def reference(x, c, ctx, c_ctx, norm1_g, norm2_g, w_mod, b_mod, w_in, gla_w_a2, gla_b_a2, gla_norm_g,
              fft_w, conv_w, conv_b, pool_w, pool_scale, w_out, ffn_w_up, ffn_conv_w, ffn_conv_b,
              ffn_w_down, final_norm_g):
    D = D_MODEL
    s_lat = jax.nn.silu(c)
    s_ctx = jax.nn.silu(c_ctx)
    xc = ctx
    for i in range(DEPTH):
        last = i == DEPTH - 1
        if last:
            mod_c = s_ctx @ w_mod[i][:, :2 * D] + b_mod[i][:2 * D]
            csh1, csc1 = jnp.split(mod_c, 2)
            hc = rmsnorm(xc, norm1_g[i]) * (1 + csc1) + csh1
            pc = hc @ w_in[i][:, :KVA_COLS]
            pk, pv, paf, pab = jnp.split(pc, SPLITS[:3], axis=-1)
            k_c, v_c, la_f_c, la_b_c = gla_kv_decay(pk, pv, paf, pab, gla_w_a2[i], gla_b_a2[i])
            h_f = gla_final_state(k_c, v_c, la_f_c)
            h_b = gla_final_state(jnp.flip(k_c, 1), jnp.flip(v_c, 1), jnp.flip(la_b_c, 1))
        else:
            mod_c = s_ctx @ w_mod[i] + b_mod[i]
            csh1, csc1, cg1, csh2, csc2, cg2 = jnp.split(mod_c, 6)
            hc = rmsnorm(xc, norm1_g[i]) * (1 + csc1) + csh1
            zeros = jnp.zeros((xc.shape[0], GLA_HEADS, GLA_DK, GLA_DV), jnp.float32)
            yc, (h_f, h_b) = mixers(hc @ w_in[i], False, zeros, zeros, gla_w_a2[i], gla_b_a2[i],
                                    gla_norm_g[i], fft_w[i], conv_w[i], conv_b[i], pool_w[i], pool_scale[i])
            xc = xc + cg1 * (yc @ w_out[i])
            hc2 = rmsnorm(xc, norm2_g[i]) * (1 + csc2) + csh2
            xc = xc + cg2 * conv_ffn(hc2, False, ffn_w_up[i], ffn_conv_w[i], ffn_conv_b[i], ffn_w_down[i])
        mod = s_lat @ w_mod[i] + b_mod[i]
        sh1, sc1, g1, sh2, sc2, g2 = [m[:, None, :] for m in jnp.split(mod, 6, axis=-1)]
        hx = rmsnorm(x, norm1_g[i]) * (1 + sc1) + sh1
        yx, _ = mixers(hx @ w_in[i], True, h_f, h_b, gla_w_a2[i], gla_b_a2[i], gla_norm_g[i],
                       fft_w[i], conv_w[i], conv_b[i], pool_w[i], pool_scale[i])
        x = x + g1 * (yx @ w_out[i])
        hx2 = rmsnorm(x, norm2_g[i]) * (1 + sc2) + sh2
        x = x + g2 * conv_ffn(hx2, True, ffn_w_up[i], ffn_conv_w[i], ffn_conv_b[i], ffn_w_down[i])
    return rmsnorm(x, final_norm_g)
```

```python
import contextlib
import os
import numpy as np
_LVL = int(os.environ.get('S1_LEVEL', '9'))
_NT1 = int(os.environ.get('S1_TILES', '99'))
import concourse.bass as bass
import concourse.mybir as mybir
from concourse.bass_utils import run_bass_kernel_spmd

F32 = mybir.dt.float32
BF16 = mybir.dt.bfloat16
AF = mybir.ActivationFunctionType
ALU = mybir.AluOpType
AX = mybir.AxisListType

D = 1024
SEQ = 8192
CTX = 256
NT = SEQ + CTX
DEPTH = 2
DFF = 2816
NJ = 22
EPS = 1e-6
COMPUTE = ("pe", "act", "dve", "pool")


class _Op:
    __slots__ = ("eng", "fn", "deps", "seq", "is_dma", "semkey", "cum", "sig", "inc")


class Sched:
    def __init__(self, nc):
        self.nc = nc
        self.ops = []
        self.q = {e: [] for e in ("pe", "act", "dve", "pool", "sp")}
        self.last_w = {}
        self.readers = {}
        self.dma_cnt = {}
        self.last_dma = {}
        self.same_eng_window = 1

    def _deps(self, reads, writes):
        deps = set()
        for r in reads:
            w = self.last_w.get(r)
            if w is not None:
                deps.add(w)
        for r in writes:
            w = self.last_w.get(r)
            if w is not None:
                deps.add(w)
            for x in self.readers.get(r, ()):
                deps.add(x)
        return deps

    def _commit(self, oid, reads, writes):
        o = self.ops[oid]
        for r in reads:
            lst = self.readers.setdefault(r, [])
            if not o.is_dma:
                lst[:] = [x for x in lst if self.ops[x].is_dma or self.ops[x].eng != o.eng]
            lst.append(oid)
        for r in writes:
            self.last_w[r] = oid
            self.readers[r] = []

    def _add(self, o, reads, writes):
        o.deps = self._deps(reads, writes)
        o.seq = len(self.q[o.eng])
        oid = len(self.ops)
        self.ops.append(o)
        self.q[o.eng].append(oid)
        self._commit(oid, reads, writes)
        return oid

    def op(self, eng, fn, reads=(), writes=()):
        o = _Op()
        o.eng = eng
        o.fn = fn
        o.is_dma = False
        return self._add(o, reads, writes)

    def dma(self, eng, out, in_, reads=(), writes=(), semkey=None):
        o = _Op()
        o.eng = eng
        o.is_dma = True
        o.fn = lambda e: e.dma_start(out=out, in_=in_)
        o.semkey = semkey if semkey is not None else writes[0]
        o.inc = 16
        self.dma_cnt[o.semkey] = self.dma_cnt.get(o.semkey, 0) + 16
        o.cum = self.dma_cnt[o.semkey]
        oid = self._add(o, reads, writes)
        self.last_dma[o.semkey] = oid
        return oid

    def cc(self, fn, semkey="cc"):
        o = _Op()
        o.eng = "pool"
        o.is_dma = True
        o.fn = fn
        o.semkey = semkey
        o.inc = 1
        self.dma_cnt[semkey] = self.dma_cnt.get(semkey, 0) + 1
        o.cum = self.dma_cnt[semkey]
        oid = self._add(o, [], [])
        self.last_dma[semkey] = oid
        return oid

    def barrier(self, skip_cc=False):
        lasts = set(q[-1] for q in self.q.values() if q) | set(v for k, v in self.last_dma.items() if not (skip_cc and k == "cc"))
        if skip_cc:
            lasts = set(x for x in lasts if not (self.ops[x].is_dma and self.ops[x].semkey == "cc"))
        for e in self.q:
            o = _Op()
            o.eng = e
            o.fn = None
            o.is_dma = False
            o.deps = set(lasts)
            o.seq = len(self.q[e])
            oid = len(self.ops)
            self.ops.append(o)
            self.q[e].append(oid)

    def emit(self):
        nc = self.nc
        with contextlib.ExitStack() as st:
            esem = {e: st.enter_context(nc.semaphore("s_" + e)) for e in COMPUTE}
            dsem = {}
            for i, k in enumerate(self.dma_cnt):
                dsem[k] = st.enter_context(nc.semaphore("d%d" % i))
            block = st.enter_context(nc.Block())
            ops = self.ops
            win = self.same_eng_window

            def skip(o, po):
                if po.eng != o.eng or o.is_dma or o.fn is None:
                    return False
                if po.eng == "pe":
                    return True
                return o.seq - po.seq > win

            need = set()
            for o in ops:
                for d in o.deps:
                    po = ops[d]
                    if po.is_dma or po.fn is None:
                        continue
                    if skip(o, po):
                        continue
                    need.add(d)
            cnt = {e: 0 for e in COMPUTE}
            for o in ops:
                o.sig = 0
            for e in COMPUTE:
                for oid in self.q[e]:
                    if oid in need:
                        cnt[e] += 1
                        ops[oid].sig = cnt[e]
            self.sig_counts = cnt

            def run(ename, eng):
                waited = {}
                for oid in self.q[ename]:
                    o = ops[oid]
                    for d in sorted(o.deps):
                        po = ops[d]
                        if po.fn is None:
                            continue
                        if po.is_dma:
                            key = ("d", po.semkey)
                            if waited.get(key, 0) >= po.cum:
                                continue
                            waited[key] = po.cum
                            eng.wait_ge(dsem[po.semkey], po.cum)
                        else:
                            if d not in need or skip(o, po):
                                continue
                            key = ("e", po.eng)
                            if waited.get(key, 0) >= po.sig:
                                continue
                            waited[key] = po.sig
                            eng.wait_ge(esem[po.eng], po.sig)
                    if o.fn is None:
                        continue
                    ins = o.fn(eng)
                    if o.is_dma:
                        ins.then_inc(dsem[o.semkey], o.inc)
                    elif o.sig:
                        ins.then_inc(esem[ename], 1)
                if ename == "sp":
                    for k, v in self.dma_cnt.items():
                        eng.wait_ge(dsem[k], v)

            @block.sync
            def _(e):
                run("sp", e)

            @block.tensor
            def _(e):
                run("pe", e)

            @block.scalar
            def _(e):
                run("act", e)

            @block.vector
            def _(e):
                run("dve", e)

            @block.gpsimd
            def _(e):
                run("pool", e)


def _pool_mat(w, n):
    t = np.arange(n)
    lo = np.clip(t - w // 2, 0, n - 1)
    hi = np.clip(t + w // 2 - 1, 0, n - 1)
    cnt = (hi - lo + 1).astype(np.float64)
    m = np.zeros((n, n))
    for tt in range(n):
        m[lo[tt]:hi[tt] + 1, tt] = 1.0 / cnt[tt]
    m -= np.eye(n)
    return m


C16 = {}
C32 = {}


def _consts():
    c16 = {}
    c32 = {}
    a = np.arange(128)
    ang = 2 * np.pi * np.outer(a, a) / 128.0
    C128, S128 = np.cos(ang), np.sin(ang)
    c16["CS1"] = np.concatenate([C128, S128], 1)
    c16["CS2"] = np.concatenate([-S128, C128], 1)
    n2 = np.arange(128) % 64
    tw = 2 * np.pi * np.outer(n2, np.arange(128)) / 8192.0
    c32["TT1"] = np.concatenate([np.cos(tw), np.sin(tw)], 1)
    c32["TT2"] = np.concatenate([np.sin(tw), np.cos(tw)], 1)
    b = np.arange(64)
    a64 = 2 * np.pi * np.outer(b, b) / 64.0
    C64, S64 = np.cos(a64), np.sin(a64)
    bdc = np.zeros((128, 128))
    bds = np.zeros((128, 128))
    for e in range(2):
        bdc[e * 64:(e + 1) * 64, e * 64:(e + 1) * 64] = C64
        bds[e * 64:(e + 1) * 64, e * 64:(e + 1) * 64] = -S64
    c16["BDC"] = bdc
    c16["BDS"] = bds
    cs = np.zeros((128, 128))
    cs[:64, :64] = C64
    cs[:64, 64:] = S64
    c32["C64S64"] = cs
    k = np.arange(256)
    a256 = 2 * np.pi * np.outer(k, k) / 256.0
    c16["C256"] = np.cos(a256).reshape(2, 128, 256).transpose(1, 0, 2).reshape(128, 512)
    c16["NS256"] = (-np.sin(a256)).reshape(2, 128, 256).transpose(1, 0, 2).reshape(128, 512)
    pm = np.zeros((128, 4, 128))
    pmc = np.zeros((128, 4, 2, 256))
    for g, w in enumerate((2, 4, 8, 16)):
        m64 = _pool_mat(w, 64)
        pm[:64, g, :64] = m64
        pm[64:, g, 64:] = m64
        mc = _pool_mat(w, 256)
        pmc[:, g, :, :] = mc.reshape(2, 128, 256).transpose(1, 0, 2)
    c16["PM"] = pm.reshape(128, 512)
    c16["PMC"] = pmc.reshape(128, 2048)
    c16["IDENT"] = np.eye(128)
    j = np.arange(128)
    mf = (j[:, None] <= j[None, :]).astype(np.float64)
    mb = (j[:, None] >= j[None, :]).astype(np.float64)
    c32["MASK"] = np.concatenate([mf, mb], 1)
    pp = np.arange(128)[:, None] // 32
    cc_ = np.arange(256)[None, :] // 64
    c32["MASKBD"] = (pp == cc_).astype(np.float64)
    rm = np.ones((128, 1024))
    rm[:, ::128] = 0.0
    c32["RMASK"] = rm
    return c16, c32


def _pack(d):
    offs = {}
    o = 0
    arrs = []
    for k, v in d.items():
        offs[k] = (o, v.shape[1])
        o += v.shape[1]
        arrs.append(v)
    return offs, np.ascontiguousarray(np.concatenate(arrs, 1).astype(np.float32))


_C16, _C32 = _consts()
OFF16, CST16 = _pack(_C16)
OFF32, CST32 = _pack(_C32)


class _SkipPhase(Exception):
    pass


class _Phase:
    def __init__(self, skip):
        self.skip = skip
        self.st = None

    def __enter__(self):
        if self.skip:
            return None
        self.st = contextlib.ExitStack()
        return self.st.__enter__()

    def __exit__(self, et, ev, tb):
        if self.st is not None:
            self.st.__exit__(et, ev, tb)
        return et is _SkipPhase


def build_program(n_layers=DEPTH, dbg=False, stop_after=None):
    nc = bass.Bass("TRN2", target_bir_lowering=False)

    def din(name, shape):
        return nc.dram_tensor(name, list(shape), F32, kind="ExternalInput")

    xT_in = din("xT", [D, NT])
    c_fm = din("c_fm", [128, 8, 2])
    n1g_d = din("n1g", [DEPTH, 128, 8])
    n2g_d = din("n2g", [DEPTH, 128, 8])
    wmod_d = din("wmod", [DEPTH, 3, 128, 8, 512])
    bmod_d = din("bmod", [DEPTH, 128, 12])
    win_d = din("win", [DEPTH, 128, 8, 2080])
    wa2_d = din("wa2", [DEPTH, 16, 2, 128])
    ba2_d = din("ba2", [DEPTH, 128, 2])
    glag_d = din("glag", [DEPTH, 64, 1])
    fftw_d = din("fftw", [DEPTH, 64, 4, 64])
    convw_d = din("convw", [DEPTH, 128, 2, 3])
    convb_d = din("convb", [DEPTH, 128, 2])
    poolw_d = din("poolw", [DEPTH, 64, 4, 64])
    poolsc_d = din("poolsc", [DEPTH, 64, 4])
    wout_d = din("wout", [DEPTH, 128, 8, 1024])
    wup_d = din("wup", [DEPTH, 128, 8, 2 * DFF])
    fcw_d = din("fcw", [DEPTH, 128, NJ, 3])
    fcb_d = din("fcb", [DEPTH, 128, NJ])
    wdn_d = din("wdn", [DEPTH, 128, NJ, 1024])
    gfin_d = din("gfin", [128, 8])
    cst16_d = din("cst16", list(CST16.shape))
    cst32_d = din("cst32", list(CST32.shape))
    qoff_d = nc.dram_tensor("qoff", [1, 1], mybir.dt.int32, kind="ExternalInput")
    qoff2_d = nc.dram_tensor("qoff2", [1, 1], mybir.dt.int32, kind="ExternalInput")
    QW = SEQ // 4
    outT = nc.dram_tensor("outT", [D, QW], F32, kind="ExternalOutput")

    kw = {"kind": "ExternalOutput"} if dbg else {}
    xs = nc.dram_tensor("xs", [D, NT], F32, **kw)
    QKf = nc.dram_tensor("QKf", [256, NT], BF16, **kw)
    AFB = nc.dram_tensor("AFB", [32, NT], BF16, **kw)
    Vtm = nc.dram_tensor("Vtm", [NT, 256], BF16, **kw)
    Ud = nc.dram_tensor("Ud", [256, NT], BF16, **kw)
    Gs = nc.dram_tensor("Gs", [256, NT], BF16, **kw)
    Ymix = nc.dram_tensor("Ymix", [D, NT], BF16, **kw)
    H2 = nc.dram_tensor("H2", [D, NT], BF16, **kw)
    xq = nc.dram_tensor("xq", [8, 128, QW], F32)
    QKfL = nc.dram_tensor("QKfL", [256, QW], BF16)
    AFBL = nc.dram_tensor("AFBL", [32, QW], BF16)
    VtmL = nc.dram_tensor("VtmL", [QW, 256], BF16)
    UdL = nc.dram_tensor("UdL", [256, QW], BF16)
    QKfG = nc.dram_tensor("QKfG", [4 * 256, QW], BF16)
    AFBG = nc.dram_tensor("AFBG", [4 * 32, QW], BF16)
    VtmG = nc.dram_tensor("VtmG", [SEQ, 256], BF16)
    UdG = nc.dram_tensor("UdG", [4 * 256, QW], BF16)
    YF = nc.dram_tensor("YF", [256, NT], BF16)
    ML = [nc.dram_tensor("ML%d" % i, [128, 24], F32) for i in range(DEPTH)]
    MG = [nc.dram_tensor("MG%d" % i, [4 * 128, 24], F32) for i in range(DEPTH)]
    ApL = nc.dram_tensor("ApL", [128, 2 * 16 * 64], BF16)
    ApG = nc.dram_tensor("ApG", [4 * 128, 2 * 16 * 64], BF16)
    EpL = nc.dram_tensor("EpL", [128, 32], F32)
    EpG = nc.dram_tensor("EpG", [4 * 128, 32], F32)

    def xq_tile_ap(t0, tw):
        return xq.ap()[:, :, t0:t0 + tw].rearrange("c p t -> p c t")
    QKp = nc.dram_tensor("QKp", [4, 128, NT], BF16)
    Sd = nc.dram_tensor("Sd", [128, 2, 67 * 64], BF16)
    PH = ["p0", "s1", "s2a", "s2b", "s3a", "s3b"]

    def skip_phase(l, name):
        if stop_after is None:
            return False
        sl, sn = stop_after
        return (l, PH.index(name)) > (sl, PH.index(sn))

    S = Sched(nc)
    ES = contextlib.ExitStack()
    DRAMK = ("QKf", "Gs", "Ud", "Ymc", "Ymp", "AFB", "Vtm", "Ymg", "Ymf", "Ymfc", "x", "out", "H2", "xq", "QKp", "Sd", "L2", "YF")

    _uid = [0]

    def sb(st, name, shape, dt):
        _uid[0] += 1
        return st.enter_context(nc.sbuf_tensor("sb%d_%s" % (_uid[0], name), list(shape), dt))

    def MM(out, lhsT, rhs, start, stop, R, W):
        S.op("pe", lambda e: e.matmul(out, lhsT=lhsT, rhs=rhs, start=start, stop=stop), R, W)

    def TR(out, in_, ident, R, W):
        S.op("pe", lambda e: e.transpose(out=out, in_=in_, identity=ident), R, W)

    def ACT(out, in_, func, R, W, bias=None, scale=None):
        kw = {}
        if bias is not None:
            kw["bias"] = bias
        if scale is not None:
            kw["scale"] = scale
        S.op("act", lambda e: e.activation(out=out, in_=in_, func=func, **kw), R, W)

    def TT(eng, out, in0, in1, op, R, W):
        S.op(eng, lambda e: e.tensor_tensor(out=out, in0=in0, in1=in1, op=op), R, W)

    def TS(eng, out, in0, s1, s2, op0, op1, R, W):
        if op1 is None:
            S.op(eng, lambda e: e.tensor_scalar(out=out, in0=in0, scalar1=s1, scalar2=None, op0=op0), R, W)
        else:
            S.op(eng, lambda e: e.tensor_scalar(out=out, in0=in0, scalar1=s1, scalar2=s2, op0=op0, op1=op1), R, W)

    def STT(out, in0, scalar, in1, op0, op1, R, W):
        S.op("dve", lambda e: e.scalar_tensor_tensor(out=out, in0=in0, scalar=scalar, in1=in1, op0=op0, op1=op1), R, W)

    def CP(eng, out, in_, R, W):
        if eng == "act":
            S.op(eng, lambda e: e.activation(out=out, in_=in_, func=AF.Copy), R, W)
        else:
            S.op(eng, lambda e: e.tensor_copy(out=out, in_=in_), R, W)

    def MEMSET(eng, ap, val, W):
        S.op(eng, lambda e: e.memset(ap, val), (), W)

    def DMA(q, out, in_, R, W):
        to_dram = not isinstance(W[0], str) and W[0][0] in DRAMK or (isinstance(W[0], str) and W[0] in DRAMK)
        if q == "pool" and out.dtype == in_.dtype:
            q = "sp"
        S.dma(q, out, in_, R, W, semkey="st" if to_dram else None)

    rs = ES.enter_context(nc.sync.register("rs_qoff"))
    S.op("sp", lambda e: e.reg_load(rs, qoff_d.ap()[0:1, 0:1]), (), ())
    _snap = {}

    def qv(e):
        if "v" not in _snap:
            _snap["v"] = e.snap(rs, min_val=0, max_val=SEQ - QW)
        return _snap["v"]

    rs2 = ES.enter_context(nc.sync.register("rs_qoff2"))
    S.op("sp", lambda e: e.reg_load(rs2, qoff2_d.ap()[0:1, 0:1]), (), ())

    def qv2(e):
        if "v2" not in _snap:
            _snap["v2"] = e.snap(rs2, min_val=0, max_val=3 * 1024)
        return _snap["v2"]

    def DYN(out, apfn, W):
        o_ = _Op()
        o_.eng = "sp"
        o_.is_dma = True
        o_.fn = lambda e: e.dma_start(out=out, in_=apfn(e))
        o_.semkey = W[0]
        o_.inc = 16
        S.dma_cnt[o_.semkey] = S.dma_cnt.get(o_.semkey, 0) + 16
        o_.cum = S.dma_cnt[o_.semkey]
        oid = S._add(o_, [], W)
        S.last_dma[o_.semkey] = oid

    def DMA_DYN(out, src_t, t0, tw, W, rows=None):
        r0, r1 = rows if rows is not None else (0, src_t.shape[0])
        DYN(out, lambda e: src_t.ap()[r0:r1, bass.ds(qv(e) + t0, tw)].rearrange("(c p) t -> p c t", p=128), W)

    PSB = [ES.enter_context(nc.psum_tensor("psb%d" % i, [128, 512], F32)) for i in range(7)]
    _pu = [0]

    def psum8(st_, dt, cols):
        _pu[0] += 1
        return st_.enter_context(nc.psum_tensor("ps8_%d" % _pu[0], [128, cols], dt))
    _bank = [0]

    def bank():
        i = _bank[0] % 7
        _bank[0] += 1
        return PSB[i], ("ps", i)

    def K16(name, rows=128, st=None):
        o, n = OFF16[name]
        t = sb(st, "k_" + name + "_%d" % len(S.ops), [rows, n], BF16)
        DMA("pool", t[:, :], cst16_d.ap()[0:rows, o:o + n], (), ["c16"])
        return t[:, :]

    def K32(name, rows=128, st=None, cols=None):
        o, n = OFF32[name]
        if cols is not None:
            n = cols
        t = sb(st, "k_" + name + "_%d" % len(S.ops), [rows, n], F32)
        DMA("sp", t[:, :], cst32_d.ap()[0:rows, o:o + n], (), ["c32"])
        return t[:, :]

    EPSB = sb(ES, "epsb", [128, 1], F32)
    MEMSET("pool", EPSB[:, :], EPS, ["epsb"])
    ones_bf = sb(ES, "ones_bf", [128, 128], BF16)
    MEMSET("pool", ones_bf[:, :], 1.0, ["ones"])
    svec = sb(ES, "svec", [128, 8, 2], F32)
    DMA("sp", svec[:, :, :], c_fm.ap(), (), ["svec"])
    ACT(svec[:, :, :], svec[:, :, :], AF.Silu, ["svec"], ["svec"])
    svec_bf = sb(ES, "svec_bf", [128, 8, 2], BF16)
    CP("dve", svec_bf[:, :, :], svec[:, :, :], ["svec"], ["svec"])
    gfin = sb(ES, "gfin", [128, 8], F32)
    DMA("sp", gfin[:, :], gfin_d.ap(), (), ["gfin"])
    modv_l = [sb(ES, "modv%d" % i, [128, 48, 2], F32) for i in range(DEPTH)]
    bmod_l = [sb(ES, "bmod%d" % i, [128, 12], F32) for i in range(DEPTH)]
    n1g_l = [sb(ES, "n1g%d" % i, [128, 8], F32) for i in range(DEPTH)]
    n2g_l = [sb(ES, "n2g%d" % i, [128, 8], F32) for i in range(DEPTH)]
    A1_l = [sb(ES, "A1_%d" % i, [128, 8, 2], F32) for i in range(DEPTH)]
    A2_l = [sb(ES, "A2_%d" % i, [128, 8, 2], F32) for i in range(DEPTH)]
    modv, A1, A2 = modv_l[0], A1_l[0], A2_l[0]

    class ModCalc:
        def __init__(self, lq, st_):
            self.lq = lq
            self.mk = ("mod", lq)
            self.wm = [sb(st_, "wm%d" % i, [128, 8, 512], BF16) for i in range(2)]
            self.wmf = sb(st_, "wmf", [128, 8, 512], F32)
            self.modq = sb(st_, "modq", [128, 12, 2], F32)
            self.pbm = psum8(st_, F32, 512)
            self.pkm = ("psmod", lq)
            DMA("sp", bmod_l[lq][:, :], bmod_d.ap()[lq], (), [("bmod", lq)])
            DMA("sp", n1g_l[lq][:, :], n1g_d.ap()[lq], (), [("ng", lq)])
            DMA("sp", n2g_l[lq][:, :], n2g_d.ap()[lq], (), [("ng", lq)])
            self.n = 0

        def piece(self):
            i = self.n
            self.n += 1
            w = self.wm[i % 2]
            wk = ("wm", i % 2)
            src_ = wmod_d.ap()[self.lq, i]
            if i % 2 == 0:
                DMA("pool", w[:, :, :], src_, (), [wk])
            else:
                DMA("sp", self.wmf[:, :, :], src_, (), ["wmf"])
                CP("dve", w[:, 0:4, :], self.wmf[:, 0:4, :], ["wmf"], [wk])
                CP("act", w[:, 4:8, :], self.wmf[:, 4:8, :], ["wmf"], [wk])
            for m in range(4):
                col = (i * 4 + m) * 2
                for kc in range(8):
                    MM(self.pbm[:, col:col + 2], w[:, kc, m * 128:(m + 1) * 128], svec_bf[:, kc, :], kc == 0, kc == 7,
                       [wk, "svec"], [self.pkm])

        def pieces(self, k):
            for _ in range(k):
                if self.n < 3:
                    self.piece()

        def finish(self):
            self.pieces(3)
            lq = self.lq
            TT("dve", self.modq[:, :, :], self.pbm[:, 0:24].rearrange("p (c i) -> p c i", i=2),
               bmod_l[lq][:, :].unsqueeze(2).broadcast_to([128, 12, 2]), ALU.add, [self.pkm, ("bmod", lq)], [("modq", lq)])
            DMA("sp", ML[lq].ap(), self.modq[:, :, :].rearrange("p c i -> p (c i)"), [("modq", lq)], [("L2", 8 + lq)])

    def mod_cc(lq):
        S.cc(lambda e: e.collective_compute("AllGather", ALU.bypass, replica_groups=[[0, 1, 2, 3], [4, 5, 6, 7]],
                                            ins=[ML[lq].ap().opt()], outs=[MG[lq].ap().opt()]))

    def mod_load(lq):
        mk = ("mod", lq)
        mv = modv_l[lq]
        for r_ in range(4):
            DMA("sp", mv[:, 12 * r_:12 * r_ + 12, :], MG[lq].ap()[128 * r_:128 * (r_ + 1), :].rearrange("p (c i) -> p c i", i=2), [], [mk])
        for (Aout, ng, kind) in ((A1_l[lq], n1g_l[lq], 1), (A2_l[lq], n2g_l[lq], 4)):
            TS("dve", Aout[:, :, :], mv[:, kind * 8:(kind + 1) * 8, :], 1.0, None, ALU.add, None, [mk], [mk])
            TT("dve", Aout[:, :, :], Aout[:, :, :], ng[:, :].unsqueeze(2).broadcast_to([128, 8, 2]), ALU.mult,
               [mk, ("ng", lq)], [mk])

    def phase0_mod(lq):
        with contextlib.ExitStack() as st_:
            mc = ModCalc(lq, st_)
            mc.finish()
            S.barrier()
            mod_cc(lq)

    wa2 = sb(ES, "wa2", [16, 2, 128], BF16)
    nba2 = sb(ES, "nba2", [128, 2], F32)
    glag = sb(ES, "glag", [64, 1], F32)
    fftw = sb(ES, "fftw", [64, 4, 64], F32)
    ABg = sb(ES, "ABg", [64, 4, 128], BF16)
    convw = sb(ES, "convw", [128, 2, 3], F32)
    convb = sb(ES, "convb", [128, 2], F32)
    poolw = sb(ES, "poolw", [64, 4, 64], BF16)
    poolsc = sb(ES, "poolsc", [64, 4], F32)
    fcw = sb(ES, "fcw", [128, NJ, 3], F32)
    fcb = sb(ES, "fcb", [128, NJ], F32)

    def modcol(kind, c, i):
        return modv[:, kind * 8 + c, i:i + 1]

    def tile_list(tw):
        tl = [(t * tw, tw, 0, 64) for t in range(SEQ // tw)]
        tl.append((SEQ, CTX, 1, CTX))
        return tl

    def xsrc(l):
        return xT_in if l == 0 else xs

    def x_tile_ap(t, t0, tw):
        return t.ap()[:, t0:t0 + tw].rearrange("(c p) t -> p c t", p=128)

    def norm_mod(xt, sq, tmp, rstd, hx, tw, Acol, Bcol, Rx, Wkeys, tag):
        ACT(sq[:, :, 0:tw], xt[:, :, 0:tw], AF.Square, Rx, [tag + "sq"])
        pb, pk = bank()
        for c in range(8):
            MM(pb[:, 0:tw], ones_bf[:, :], sq[:, c, 0:tw], c == 0, c == 7, [tag + "sq", "ones"], [pk])
        ACT(rstd[:, 0:tw], pb[:, 0:tw], AF.Ln, [pk], [tag + "rstd"], bias=EPSB[:, 0:1], scale=1.0 / D)
        ACT(rstd[:, 0:tw], rstd[:, 0:tw], AF.Exp, [tag + "rstd"], [tag + "rstd"], scale=-0.5)
        TT("dve", tmp[:, :, 0:tw], xt[:, :, 0:tw], rstd[:, 0:tw].unsqueeze(1).broadcast_to([128, 8, tw]), ALU.mult,
           Rx + [tag + "rstd"], [tag + "tmp"])
        for c in range(8):
            if Bcol is None:
                if c % 2 == 0:
                    TS("dve", hx[:, c, 0:tw], tmp[:, c, 0:tw], Acol(c), None, ALU.mult, None, [tag + "tmp"], Wkeys)
                else:
                    ACT(hx[:, c, 0:tw], tmp[:, c, 0:tw], AF.Copy, [tag + "tmp"], Wkeys, scale=Acol(c))
            else:
                ACT(hx[:, c, 0:tw], tmp[:, c, 0:tw], AF.Identity, [tag + "tmp", "params"], Wkeys, bias=Bcol(c), scale=Acol(c))

    cs_p = K32("C64S64", 64, ES)

    def load_params_early(lq):
        DMA("pool", wa2[:, :, :], wa2_d.ap()[lq], (), ["params"])
        DMA("sp", nba2[:, :], ba2_d.ap()[lq], (), ["params"])
        DMA("sp", glag[:, :], glag_d.ap()[lq], (), ["params"])
        DMA("sp", fftw[:, :, :], fftw_d.ap()[lq], (), ["params"])
        DMA("sp", convw[:, :, :], convw_d.ap()[lq], (), ["params"])
        DMA("sp", convb[:, :], convb_d.ap()[lq], (), ["params"])
        DMA("pool", poolw[:, :, :], poolw_d.ap()[lq], (), ["params"])
        DMA("sp", poolsc[:, :], poolsc_d.ap()[lq], (), ["params"])
        TS("dve", nba2[:, :], nba2[:, :], -1.0, None, ALU.mult, None, ["params"], ["params"])
        for g in range(4):
            pb, pk = bank()
            rhs = fftw[:, g, :].rearrange("m (dl dh) -> m dh dl", dl=2)
            MM(pb[0:64, 0:64], cs_p[:, 0:64], rhs, True, True, ["c32", "params"], [pk])
            MM(pb[0:64, 64:128], cs_p[:, 64:128], rhs, True, True, ["c32", "params"], [pk])
            CP("dve", ABg[:, g, :], pb[0:64, 0:128], [pk], ["params"])

    def load_params_ffn(lq):
        DMA("sp", fcw[:, :, :], fcw_d.ap()[lq], (), ["params"])
        DMA("sp", fcb[:, :], fcb_d.ap()[lq], (), ["params"])

    phase0_mod(0)
    for l in range(n_layers):
        last = l == DEPTH - 1
        modv, A1, A2 = modv_l[l], A1_l[l], A2_l[l]
        with _Phase(skip_phase(l, 'p0')) as st:
            if st is None:
                raise _SkipPhase()
            if l > 0:
                mod_load(l)
                load_params_ffn(l)
            else:
                load_params_early(0)
                load_params_ffn(0)
            if l == 0:
                S.barrier()
                mod_load(0)
            S.barrier()

        with _Phase(skip_phase(l, 's1')) as st:
            if st is None:
                raise _SkipPhase()
            win = sb(st, "win", [128, 8, 2080], BF16)
            if l + 1 < n_layers:
                DMA("pool", win[:, :, :], win_d.ap()[l], (), ["win"])
            else:
                winf = sb(st, "winf", [128, 4, 2080], F32)
                DMA("pool", win[:, 0:4, :], win_d.ap()[l][:, 0:4, :], (), ["win"])
                DMA("sp", winf[:, :, :], win_d.ap()[l][:, 4:8, :], (), ["winf"])
                CP("dve", win[:, 4:6, :], winf[:, 0:2, :], ["winf"], ["win"])
                CP("act", win[:, 6:8, :], winf[:, 2:4, :], ["winf"], ["win"])
            TW = 512
            xt2 = [sb(st, "xt%d" % i, [128, 8, TW], F32) for i in range(2)]
            sq = sb(st, "sq", [128, 8, TW], BF16)
            tmp = sb(st, "tmp", [128, 8, TW], F32)
            rstd = sb(st, "rstd", [128, TW], F32)
            hx2 = [sb(st, "hx%d" % i, [128, 8, TW], BF16) for i in range(2)]
            stg2 = [sb(st, "stg%d" % i, [128, 8, TW], BF16) for i in range(2)]
            stp2 = [sb(st, "stp%d" % i, [64, 4, TW], BF16) for i in range(2)]
            afb2 = [sb(st, "afb%d" % i, [32, TW], BF16) for i in range(2)]
            vst2 = [sb(st, "vst%d" % i, [128, 4, 256], BF16) for i in range(2)]
            ptm = sb(st, "ptm", [128, 4, 256], BF16)
            hs = sb(st, "hs", [128, TW], F32)
            mcv = sb(st, "mcv", [128, TW], F32)
            acc = sb(st, "acc", [128, TW], F32)
            pooled = sb(st, "pooled", [64, 4, TW], BF16)
            tiles = tile_list(TW)
            tiles = tiles[:QW // TW] + tiles[-1:]
            src = xsrc(l)
            PMk = K16("PM", 128, st)
            PMCk = K16("PMC", 128, st)

            def load_x(ti):
                t0, tw, ci, rl = tiles[ti]
                if ci == 0:
                    if l == 0:
                        DMA_DYN(xt2[ti % 2][:, :, 0:tw], src, t0, tw, [("xt", ti % 2)])
                    else:
                        DMA("sp", xt2[ti % 2][:, :, 0:tw], xq_tile_ap(t0, tw), [], [("xt", ti % 2)])
                    return
                DMA("sp", xt2[ti % 2][:, :, 0:tw], x_tile_ap(src, t0, tw), [("x", t0 // 256 + k) for k in range(tw // 256)],
                    [("xt", ti % 2)])

            if _NT1 < 99:
                tiles = tiles[:_NT1] + tiles[-1:]
            def do_norm(ti):
                t0_, tw_, ci_, rl_ = tiles[ti]
                p_ = ti % 2
                norm_mod(xt2[p_], sq, tmp, rstd, hx2[p_], tw_, lambda c: A1[:, c, ci_:ci_ + 1], lambda c: modcol(0, c, ci_),
                         [("xt", p_)], [("hx", p_)], "n1")

            mc_next = ModCalc(l + 1, st) if (l + 1 < n_layers) else None
            load_x(0)
            if len(tiles) > 1:
                load_x(1)
            do_norm(0)
            for ti, (t0, tw, ci, rl) in enumerate(tiles):
                p = ti % 2
                xt, hx, stg, stp, afb, vst = xt2[p], hx2[p], stg2[p], stp2[p], afb2[p], vst2[p]
                hk = ("hx", p)

                def proj(col0, m, pb, pk, off=0):
                    for kc in range(8):
                        MM(pb[0:m, off:off + tw], win[:, kc, col0:col0 + m], hx[:, kc, 0:tw], kc == 0, kc == 7, ["win", hk], [pk])

                if _LVL < 2:
                    continue
                for slot, col0 in ((0, 416), (1, 0)):
                    pb, pk = bank()
                    proj(col0, 128, pb, pk)
                    CP("act" if slot == 0 else "dve", stg[:, slot, 0:tw], pb[:, 0:tw], [pk], [("stg", p)])
                pb, pk = bank()
                proj(384, 32, pb, pk)
                CP("act", afb[:, 0:tw], pb[0:32, 0:tw], [pk], [("afb", p)])
                for j in range(2):
                    pb, pk = bank()
                    proj(544 + 128 * j, 128, pb, pk)
                    ACT(stg[:, 2 + j, 0:tw], pb[:, 0:tw], AF.Silu, [pk], [("stg", p)])
                for j in range(2):
                    pb, pk = bank()
                    proj(800 + 128 * j, 128, pb, pk)
                    CP("dve", stg[:, 4 + j, 0:tw], pb[:, 0:tw], [pk], [("stg", p)])
                if ti + 1 < len(tiles):
                    do_norm(ti + 1)
                if ti + 2 < len(tiles):
                    load_x(ti + 2)
                for j in range(2 if _LVL >= 3 else 0):
                    pbh, pkh = bank()
                    proj(1056 + 128 * j, 128, pbh, pkh)
                    pbB, pkB = bank()
                    proj(1312 + 128 * j, 128, pbB, pkB)
                    pbC, pkC = bank()
                    proj(1568 + 128 * j, 128, pbC, pkC)
                    CP("act", hs[:, 0:tw], pbh[:, 0:tw], [pkh], ["hs"])
                    TT("dve", mcv[:, 0:tw], pbC[:, 0:tw], hs[:, 0:tw], ALU.mult, [pkC, "hs"], ["mcv"])
                    ACT(acc[:, 0:tw], mcv[:, 0:tw], AF.Identity, ["mcv", "params"], ["acc"], bias=convb[:, j:j + 1],
                        scale=convw[:, j, 1:2])
                    a3 = acc[:, 0:tw].rearrange("p (r l) -> p r l", l=rl)
                    m3 = mcv[:, 0:tw].rearrange("p (r l) -> p r l", l=rl)
                    STT(a3[:, :, 1:rl], m3[:, :, 0:rl - 1], convw[:, j, 0:1], a3[:, :, 1:rl], ALU.mult, ALU.add,
                        ["mcv", "acc", "params"], ["acc"])
                    STT(a3[:, :, 0:rl - 1], m3[:, :, 1:rl], convw[:, j, 2:3], a3[:, :, 0:rl - 1], ALU.mult, ALU.add,
                        ["mcv", "acc", "params"], ["acc"])
                    TT("dve", stg[:, 6 + j, 0:tw], acc[:, 0:tw], pbB[:, 0:tw], ALU.mult, ["acc", pkB], [("stg", p)])
                nsub = tw // 128
                for i in range(nsub if _LVL >= 4 else 0):
                    pb, pk = bank()
                    for kc in range(8):
                        MM(pb[:, 0:256], hx[:, kc, i * 128:(i + 1) * 128], win[:, kc, 128:384], kc == 0, kc == 7, ["win", hk], [pk])
                    CP("act", vst[:, i, :], pb[:, 0:256], [pk], [("vst", p)])
                    pb, pk = bank()
                    for kc in range(8):
                        MM(pb[:, 0:256], hx[:, kc, i * 128:(i + 1) * 128], win[:, kc, 1824:2080], kc == 0, kc == 7,
                           ["win", hk], [pk])
                    CP("dve", ptm[:, i, :], pb[:, 0:256], [pk], ["ptm"])
                for g in range(4 if _LVL >= 5 else 0):
                    pb, pk = bank()
                    if ci == 0:
                        pm = PMk.rearrange("p (g t) -> p g t", g=4)
                        for i in range(nsub):
                            MM(pb[0:64, i * 128:(i + 1) * 128], ptm[:, i, 64 * g:64 * g + 64], pm[:, g, :], True, True,
                               ["ptm", "c16"], [pk])
                    else:
                        pmc = PMCk.rearrange("p (g k t) -> p g k t", g=4, k=2)
                        for kc in range(2):
                            MM(pb[0:64, 0:256], ptm[:, kc, 64 * g:64 * g + 64], pmc[:, g, kc, :], kc == 0, kc == 1,
                               ["ptm", "c16"], [pk])
                    CP("act", pooled[:, g, 0:tw], pb[0:64, 0:tw], [pk], [("pooled", g)])
                    pb2, pk2 = bank()
                    MM(pb2[0:64, 0:tw], poolw[:, g, :], pooled[:, g, 0:tw], True, True, [("pooled", g), "params"], [pk2])
                    TS("dve", stp[:, g, 0:tw], pb2[0:64, 0:tw], poolsc[:, g:g + 1], None, ALU.mult, None, [pk2, "params"],
                       [("stp", p)])
                if mc_next is not None:
                    mc_next.pieces(2 if ti < len(tiles) - 1 else 12)
                if _LVL < 6:
                    continue
                ts_ = slice(t0, t0 + tw)
                DMA("pool", QKf.ap()[:, ts_].rearrange("(s p) t -> p s t", p=128), stg[:, 0:2, 0:tw], [("stg", p)], [("QKf", ti)])
                DMA("pool", Gs.ap()[:, ts_].rearrange("(s p) t -> p s t", p=128), stg[:, 2:4, 0:tw], [("stg", p)], [("Gs", ti)])
                DMA("pool", (UdL if ci == 0 else Ud).ap()[:, ts_].rearrange("(s p) t -> p s t", p=128), stg[:, 4:6, 0:tw], [("stg", p)],
                    [("Ud", ti)])
                DMA("pool", Ymix.ap()[512:768, ts_].rearrange("(s p) t -> p s t", p=128), stg[:, 6:8, 0:tw], [("stg", p)],
                    [("Ymc", ti)])
                DMA("pool", Ymix.ap()[768:1024, ts_].rearrange("(g p) t -> p g t", p=64), stp[:, :, 0:tw], [("stp", p)],
                    [("Ymp", ti)])
                DMA("pool", AFB.ap()[:, ts_], afb[:, 0:tw], [("afb", p)], [("AFB", ti)])
                DMA("pool", Vtm.ap()[ts_, :].rearrange("(i p) c -> p i c", p=128), vst[:, 0:nsub, :], [("vst", p)], [("Vtm", ti)])
            if mc_next is not None:
                mc_next.finish()
            S.barrier()
            S.cc(lambda e: e.collective_compute("AllGather", ALU.bypass, replica_groups=[[0, 1, 2, 3], [4, 5, 6, 7]],
                                                ins=[UdL.ap().opt()], outs=[UdG.ap().opt()]))
            if mc_next is not None:
                mod_cc(l + 1)
            s1_done = True

        if not skip_phase(l, 's1'):
            S.barrier(skip_cc=True)
        NS1 = 17
        allk = lambda nm: [(nm, ti) for ti in range(NS1)]

        def run_fft():
          with contextlib.ExitStack() as st:
            usb1 = sb(st, "usb", [64, NT], BF16)
            usb2 = [usb1, usb1]
            Gsb2 = [sb(st, "Gsb%d" % i, [128, 128, 64], BF16) for i in range(2)]
            Zp2 = [sb(st, "Zp%d" % i, [128, 2, 32, 128], BF16) for i in range(2)]
            m1_2 = [sb(st, "m1_%d" % i, [128, 512], F32) for i in range(2)]
            m2_2 = [sb(st, "m2_%d" % i, [128, 512], F32) for i in range(2)]
            Ysb1 = sb(st, "Ysb", [128, 32, 128], BF16)
            Ysb2 = [Ysb1, Ysb1]
            Gc = sb(st, "Gc", [128, 2, 128], BF16)
            Yc = sb(st, "Yc", [64, 256], BF16)
            CS1, CS2 = K16("CS1", 128, st), K16("CS2", 128, st)
            TT1 = K32("TT1", 128, st).rearrange("p (a k) -> p a k", a=2)
            TT2 = K32("TT2", 128, st).rearrange("p (a k) -> p a k", a=2)
            BDC, BDS = K16("BDC", 128, st), K16("BDS", 128, st)
            C256 = K16("C256", 128, st).rearrange("p (k t) -> p k t", k=2)
            NS256 = K16("NS256", 128, st).rearrange("p (k t) -> p k t", k=2)
            scale = 1.0 / np.sqrt(float(SEQ * 64))
            def fkeys(g):
                return ("usb", 0), ("Gsb", g % 2), ("Zp", g % 2), ("Ysb", 0)

            def f_load(g):
                usb = usb2[g % 2]
                KU, KG, KZ, KY = fkeys(g)
                for r_ in range(4):
                    DMA("sp", usb[:, r_ * QW:(r_ + 1) * QW], UdG.ap()[r_ * 256 + 64 * g:r_ * 256 + 64 * g + 64, :], [], [KU])
                DMA("sp", usb[:, SEQ:NT], Ud.ap()[64 * g:64 * g + 64, SEQ:NT], [], [KU])

            def f_step1(g):
                usb, Gsb = usb2[g % 2], Gsb2[g % 2]
                KU, KG, KZ, KY = fkeys(g)
                for q4 in range(16):
                    pb, pk = bank()
                    for i in range(4):
                        n2_ = q4 * 4 + i
                        MM(pb[:, i * 128:(i + 1) * 128], usb[:, n2_:SEQ:64], ABg[:, g, :], True, True, [KU, "params"], [pk])
                    CP("act" if q4 % 2 == 0 else "dve", Gsb[:, :, q4 * 4:q4 * 4 + 4], pb[:, :].rearrange("p (n c) -> p c n", n=4),
                       [pk], [KG])
                if not last:
                    pb, pk = bank()
                    for j in range(2):
                        MM(pb[:, j * 128:(j + 1) * 128], usb[:, SEQ + 128 * j:SEQ + 128 * (j + 1)], ABg[:, g, :], True, True,
                           [KU, "params"], [pk])
                    CP("act", Gc[:, :, :], pb[:, 0:256].rearrange("p (j c) -> p j c", j=2), [pk], ["Gc"])
                    pb, pk = bank()
                    for j in range(2):
                        MM(pb[0:64, 0:256], Gc[:, j, 0:64], C256[:, j, :], j == 0, False, ["Gc", "c16"], [pk])
                    for j in range(2):
                        MM(pb[0:64, 0:256], Gc[:, j, 64:128], NS256[:, j, :], False, j == 1, ["Gc", "c16"], [pk])
                    ACT(Yc[:, :], pb[0:64, 0:256], AF.Copy, [pk], ["Yc"], scale=1.0 / 128.0)
                    for dl in range(2):
                        r0 = 256 + 64 * g + 32 * dl
                        DMA("pool", Ymix.ap()[r0:r0 + 32, SEQ:SEQ + CTX], Yc[dl:64:2, :], ["Yc"], [("Ymfc", g, dl)])

            def f_stepA(g):
                Gsb, Zp = Gsb2[g % 2], Zp2[g % 2]
                KU, KG, KZ, KY = fkeys(g)
                for dp in range(16):
                    pb, pk = bank()
                    for e_ in range(2):
                        dh = dp * 2 + e_
                        oc = pb[:, e_ * 256:(e_ + 1) * 256]
                        MM(oc, Gsb[:, 2 * dh:2 * dh + 2, :].rearrange("p a n -> p (a n)"), CS1, True, False, [KG, "c16"], [pk])
                        MM(oc, Gsb[:, 64 + 2 * dh:64 + 2 * dh + 2, :].rearrange("p a n -> p (a n)"), CS2, False, True,
                           [KG, "c16"], [pk])
                    p4 = pb[:, :].rearrange("p (e a k) -> p e a k", e=2, a=2)
                    m1, m2 = m1_2[dp % 2], m2_2[dp % 2]
                    k1_, k2_ = ("m1", dp % 2), ("m2", dp % 2)
                    m1v = m1[:, :].rearrange("p (e a k) -> p e a k", e=2, a=2)
                    m2v = m2[:, :].rearrange("p (e a k) -> p e a k", e=2, a=2)
                    TT("dve", m1v, p4, TT1.unsqueeze(1).broadcast_to([128, 2, 2, 128]), ALU.mult, [pk, "c32"], [k1_])
                    TT("dve", m2v, p4, TT2.unsqueeze(1).broadcast_to([128, 2, 2, 128]), ALU.mult, [pk, "c32"], [k2_])
                    TT("dve", Zp[:, 0, 2 * dp:2 * dp + 2, :], m1v[:, :, 0, :], m1v[:, :, 1, :], ALU.subtract, [k1_], [(KZ, 0)])
                    TT("pool", Zp[:, 1, 2 * dp:2 * dp + 2, :], m2v[:, :, 0, :], m2v[:, :, 1, :], ALU.add, [k2_], [(KZ, 1)])

            def f_stepC(g):
                Zp, Ysb = Zp2[g % 2], Ysb2[g % 2]
                KU, KG, KZ, KY = fkeys(g)
                for q8 in range(8):
                    pb, pk = bank()
                    MM(pb[:, :], BDC, Zp[:, 0, 4 * q8:4 * q8 + 4, :].rearrange("p a k -> p (a k)"), True, False, [(KZ, 0), "c16"], [pk])
                    MM(pb[:, :], BDS, Zp[:, 1, 4 * q8:4 * q8 + 4, :].rearrange("p a k -> p (a k)"), False, True, [(KZ, 1), "c16"], [pk])
                    ACT(Ysb[:, 4 * q8:4 * q8 + 4, :], pb[:, :].rearrange("p (a k) -> p a k", a=4), AF.Copy, [pk], [KY], scale=scale)
                for dl in range(2):
                    r0 = 64 * g + 32 * dl
                    dst = bass.AP(YF, r0 * NT, [[128, 64], [NT, 32], [1, 128]])
                    DMA("pool", dst, Ysb[dl * 64:(dl + 1) * 64, :, :], [KY], [("Ymf", g, dl)])

            f_load(0)
            for g in range(4):
                f_step1(g)
                if g + 1 < 4:
                    f_load(g + 1)
                if g > 0:
                    f_stepC(g - 1)
                f_stepA(g)
            f_stepC(3)
            S.barrier()


        with _Phase(skip_phase(l, 's2a')) as st:
            if st is None:
                raise _SkipPhase()
            NCH = 66
            GW = 512
            ident = K16("IDENT", 128, st)
            mask = K32("MASK", 128, st)
            rmask = K32("RMASK", 128, st, cols=GW)
            maskBD = K32("MASKBD", 128, st)
            groups = [(g * GW, GW, 4, [g]) for g in range(4)] + [(SEQ, CTX, 2, [16])]

            def idxS(d, c):
                if d == 0:
                    return c + 2 if c < 64 else (0 if c == 64 else 1)
                return c if c < 64 else (64 if c == 64 else 65)

            def idxA(d, c):
                if d == 0:
                    return c + 3 if c < 64 else (1 if c == 64 else 2)
                return c - 1 if (1 <= c < 64) else (66 if c == 0 else (63 if c == 64 else 64))

            with contextlib.ExitStack() as sa_outer:
                Sall = sb(sa_outer, "Sall", [128, 2, 67, 64], F32)
                Sbf = sb(sa_outer, "Sbf", [128, 2, 67, 64], BF16)
                Eall = sb(sa_outer, "Eall", [128, 2, NCH], F32)
                AL = sb(sa_outer, "AL", [128, 2, 16, 64], F32)
                EL = sb(sa_outer, "EL", [128, 2, 16], F32)
                sa = contextlib.ExitStack()
                qraw2 = [sb(sa, "qraw%d" % i, [128, 2, GW], BF16) for i in range(2)]
                afr2 = [sb(sa, "afr%d" % i, [16, 2, GW], BF16) for i in range(2)]
                vg2 = [sb(sa, "vga%d" % i, [128, 4, 256], BF16) for i in range(2)]
                qkst2 = [sb(sa, "qkst%d" % i, [128, 4, GW], BF16) for i in range(2)]
                ktm2 = [sb(sa, "ktm%d" % i, [128, 4, 2, 128], BF16) for i in range(2)]
                lfb = sb(sa, "lfb", [128, 2, GW], F32)
                cum = sb(sa, "cum", [128, 2, GW], F32)
                tot = sb(sa, "tot", [128, 2, 4], F32)
                eE = sb(sa, "eE", [128, 4, GW], F32)
                tmpA2 = [sb(sa, "tmpA%d" % i, [128, 256], F32) for i in range(2)]
                PST = psum8(sa, BF16, 1024)
                MEMSET("pool", Sall[:, 0, 0, :], 0.0, ["Sall0"])
                MEMSET("pool", Sall[:, 1, 65, :], 0.0, ["Sall1"])
                nA = 0
                def loadA(gi):
                    g0, gw, nchk, s1t = groups[gi]
                    p = gi % 2
                    qraw, afr, vg = qraw2[p], afr2[p], vg2[p]
                    qsrc, ksrc = QKf.ap()[0:128, g0:g0 + gw], QKf.ap()[128:256, g0:g0 + gw]
                    afs, abs_ = AFB.ap()[0:16, g0:g0 + gw], AFB.ap()[16:32, g0:g0 + gw]
                    vsrc = Vtm.ap()[g0:g0 + gw, :]
                    DMA("sp", qraw[:, 0, 0:gw], qsrc, [], [("qraw", p)])
                    DMA("sp", qraw[:, 1, 0:gw], ksrc, [], [("qraw", p)])
                    DMA("sp", afr[:, 0, 0:gw], afs, [], [("afr", p)])
                    DMA("sp", afr[:, 1, 0:gw], abs_, [], [("afr", p)])
                    DMA("sp", vg[:, 0:nchk, :], vsrc.rearrange("(c p) d -> p c d", p=128), [], [("vga", p)])

                def passA1(gi):
                    g0, gw, nchk, s1t = groups[gi]
                    p = gi % 2
                    qraw, afr, vg, qkst, ktm = qraw2[p], afr2[p], vg2[p], qkst2[p], ktm2[p]
                    own = g0 < SEQ
                    c0 = g0 // 128
                    for d in range(2):
                        pb, pk = bank()
                        MM(pb[:, 0:gw], wa2[:, d, :], afr[:, d, 0:gw], True, True, [("afr", p), "params"], [pk])
                        ACT(lfb[:, d, 0:gw], pb[:, 0:gw], AF.Exp, [pk, "params"], [("lfb", d)], bias=nba2[:, d:d + 1], scale=-1.0)
                        ACT(lfb[:, d, 0:gw], lfb[:, d, 0:gw], AF.Ln, [("lfb", d)], [("lfb", d)], bias=1.0, scale=1.0)
                        S.op("dve", (lambda o_, d0, d1: (lambda e: e.tensor_tensor_scan(out=o_, data0=d0, data1=d1, initial=0.0,
                                                                                         op0=ALU.mult, op1=ALU.add)))(
                            cum[:, d, 0:gw], rmask[:, 0:gw], lfb[:, d, 0:gw]), [("lfb", d), "c32"], [("cum", d)])
                        S.op("dve", (lambda o_, i_: (lambda e: e.tensor_reduce(out=o_, in_=i_, axis=AX.X, op=ALU.add)))(
                            tot[:, d, 0:nchk], lfb[:, d, 0:gw].rearrange("p (c t) -> p c t", t=128)), [("lfb", d)], [("tot", d)])
                    TT("dve", cum[:, 1, 0:gw], lfb[:, 1, 0:gw], cum[:, 1, 0:gw], ALU.subtract, [("lfb", 1), ("cum", 1)], [("cum", 1)])
                    TT("dve", cum[:, 1, 0:gw].rearrange("p (c t) -> p c t", t=128), cum[:, 1, 0:gw].rearrange("p (c t) -> p c t", t=128),
                       tot[:, 1, 0:nchk].unsqueeze(2).broadcast_to([128, nchk, 128]), ALU.add, [("cum", 1), ("tot", 1)], [("cum", 1)])
                    c0 = g0 // 128
                    for d in range(2):
                        Edst = EL[:, d, c0:c0 + nchk] if own else Eall[:, d, c0:c0 + nchk]
                        ACT(Edst, tot[:, d, 0:nchk], AF.Exp, [("tot", d)], [("Eall", d)], scale=-1.0 / 16)
                        ACT(eE[:, 2 * d, 0:gw], cum[:, d, 0:gw], AF.Exp, [("cum", d)], [("eE", 2 * d)], scale=-1.0 / 16)
                        ACT(eE[:, 2 * d + 1, 0:gw], cum[:, d, 0:gw], AF.Exp, [("cum", d)], [("eE", 2 * d + 1)], scale=1.0 / 16)
                        STT(qkst[:, 2 * d, 0:gw], qraw[:, 0, 0:gw], 32.0 ** -0.5, eE[:, 2 * d, 0:gw], ALU.mult, ALU.mult,
                            [("qraw", p), ("eE", 2 * d)], [("qkst", p)])
                        TT("pool", qkst[:, 2 * d + 1, 0:gw], qraw[:, 1, 0:gw], eE[:, 2 * d + 1, 0:gw], ALU.mult,
                           [("qraw", p), ("eE", 2 * d + 1)], [("qkst", p)])
                    DMA("pool", QKp.ap()[:, :, g0:g0 + gw].rearrange("v p t -> p v t"), qkst[:, :, 0:gw], [("qkst", p)], [("QKp", gi)])

                def passA2(gi):
                    nonlocal nA
                    g0, gw, nchk, s1t = groups[gi]
                    p = gi % 2
                    qraw, afr, vg, qkst, ktm = qraw2[p], afr2[p], vg2[p], qkst2[p], ktm2[p]
                    own = g0 < SEQ
                    c0 = g0 // 128
                    for cc in range(nchk):
                        for d in range(2):
                            o_ = (cc * 2 + d) * 128
                            TR(PST[:, o_:o_ + 128], qkst[:, 2 * d + 1, cc * 128:(cc + 1) * 128], ident, [("qkst", p), "c16"], ["pst"])
                    CP("act", ktm[:, 0:nchk, :, :], PST[:, 0:nchk * 256].rearrange("p (c d k) -> p c d k", d=2, k=128), ["pst"],
                       [("ktm", p)])
                    for cc in range(nchk):
                        c = c0 + cc
                        pb, pk = bank()
                        for d in range(2):
                            MM(pb[:, d * 256:(d + 1) * 256], ktm[:, cc, d, :], vg[:, cc, :], True, True, [("ktm", p), ("vga", p)], [pk])
                        for d in range(2):
                            tA = tmpA2[nA % 2]
                            tk_ = ("tmpA", nA % 2)
                            nA += 1
                            esc = EL[:, d, c:c + 1] if own else Eall[:, d, c:c + 1]
                            adst = AL[:, d, c, :] if own else Sall[:, d, idxA(d, c), :]
                            STT(tA[:, :], pb[:, d * 256:(d + 1) * 256], esc, maskBD, ALU.mult, ALU.mult,
                                [pk, ("Eall", d), "c32"], [tk_])
                            S.op("dve", (lambda o_, i_: (lambda e: e.tensor_reduce(out=o_, in_=i_, axis=AX.X, op=ALU.add)))(
                                adst, tA[:, :].rearrange("p (h v) -> p v h", h=4)), [tk_], ["Sall%d" % d])

                ng_ = len(groups)
                loadA(0)
                if ng_ > 1:
                    loadA(1)
                passA1(0)
                for gi in range(ng_):
                    if gi + 1 < ng_:
                        passA1(gi + 1)
                    passA2(gi)
                    if gi + 2 < ng_:
                        loadA(gi + 2)
                DMA("pool", ApL.ap(), AL[:, :, :, :].rearrange("p d c v -> p (d c v)"), ["Sall0", "Sall1"], [("L2", 5)])
                DMA("pool", EpL.ap(), EL[:, :, :].rearrange("p d c -> p (d c)"), [("Eall", 0), ("Eall", 1)], [("L2", 6)])
                S.barrier()
                sa.close()
                S.cc(lambda e: e.collective_compute("AllGather", ALU.bypass, replica_groups=[[0, 1, 2, 3], [4, 5, 6, 7]],
                                                    ins=[ApL.ap().opt()], outs=[ApG.ap().opt()]))
                S.cc(lambda e: e.collective_compute("AllGather", ALU.bypass, replica_groups=[[0, 1, 2, 3], [4, 5, 6, 7]],
                                                    ins=[EpL.ap().opt()], outs=[EpG.ap().opt()]))
                run_fft()
                for r_ in range(4):
                    rows = slice(r_ * 128, (r_ + 1) * 128)
                    for d in range(2):
                        DMA("sp", Eall[:, d, 16 * r_:16 * r_ + 16], EpG.ap()[rows, d * 16:(d + 1) * 16], [], [("Eall", d)])
                        src3 = ApG.ap()[rows, d * 1024:(d + 1) * 1024].rearrange("p (c v) -> p c v", v=64)
                        if d == 0:
                            DMA("sp", Sbf[:, 0, 16 * r_ + 3:16 * r_ + 19, :], src3, [], [("Sbf", 0)])
                        elif r_ == 0:
                            DMA("sp", Sbf[:, 1, 66:67, :], src3[:, 0:1, :], [], [("Sbf", 1)])
                            DMA("sp", Sbf[:, 1, 0:15, :], src3[:, 1:16, :], [], [("Sbf", 1)])
                        else:
                            DMA("sp", Sbf[:, 1, 16 * r_ - 1:16 * r_ + 15, :], src3, [], [("Sbf", 1)])
                orderF = [64, 65] + list(range(0, 63))
                orderB = [65, 64] + list(range(63, 0, -1))
                for i in range(len(orderF)):
                    for d, c in ((0, orderF[i]), (1, orderB[i])):
                        asrc = Sall[:, d, idxA(d, c), :] if c >= 64 else Sbf[:, d, idxA(d, c), :]
                        STT(Sall[:, d, idxA(d, c), :], Sall[:, d, idxS(d, c), :], Eall[:, d, c:c + 1], asrc,
                            ALU.mult, ALU.add, ["Sall%d" % d, ("Eall", d), ("Sbf", d)], ["Sall%d" % d])
                CP("act", Sbf[:, 0, :, :], Sall[:, 0, :, :], ["Sall0"], [("Sbf", 0)])
                CP("dve", Sbf[:, 1, :, :], Sall[:, 1, :, :], ["Sall1"], [("Sbf", 1)])
                DMA("pool", Sd.ap(), Sbf[:, :, :, :].rearrange("p d s v -> p d (s v)"), [("Sbf", 0), ("Sbf", 1)], [("Sd", 0)])
                S.barrier()
            with contextlib.ExitStack() as sb_:
                qhO = sb(sb_, "qhO", [32, 4, 4, QW], BF16)
                shO = sb(sb_, "shO", [32, 2, 4, 16 * 64], BF16)
                vbO = sb(sb_, "vbO", [128, 16, 256], BF16)
                gsO = sb(sb_, "gsO", [64, 4, QW], BF16)
                qhC = sb(sb_, "qhC", [32, 4, 4, CTX], BF16)
                shC = sb(sb_, "shC", [32, 2, 4, 2 * 64], BF16)
                vbC = sb(sb_, "vbC", [128, 2, 256], BF16)
                gsC = sb(sb_, "gsC", [64, 4, CTX], BF16)
                for v_ in range(4):
                    DMA("sp", qhO[:, v_, :, :], QKp.ap()[v_, :, 0:QW].rearrange("(h k) t -> k h t", k=32), [], ["qhO"])
                for d in range(2):
                    koff = 2 * 64 if d == 0 else 0
                    DYN(shO[:, d, :, :], (lambda d_, k_: (lambda e: Sd.ap()[:, d_, bass.ds(qv2(e) + k_, 16 * 64)].rearrange("(h k) w -> k h w", k=32)))(d, koff),
                        ["shO"])
                DMA("sp", vbO[:, :, :], Vtm.ap()[0:QW, :].rearrange("(c p) d -> p c d", p=128), [], ["vbO"])
                DMA("sp", gsO[:, :, :], Gs.ap()[:, 0:QW].rearrange("(h p) t -> p h t", p=64), [], ["gsO"])
                if not last:
                    for v_ in range(4):
                        DMA("sp", qhC[:, v_, :, :], QKp.ap()[v_, :, SEQ:NT].rearrange("(h k) t -> k h t", k=32), [], ["qhC"])
                    for d in range(2):
                        s_ = idxS(d, 64)
                        DMA("sp", shC[:, d, :, :], Sd.ap()[:, d, s_ * 64:(s_ + 2) * 64].rearrange("(h k) w -> k h w", k=32), [], ["shC"])
                    DMA("sp", vbC[:, :, :], Vtm.ap()[SEQ:NT, :].rearrange("(c p) d -> p c d", p=128), [], ["vbC"])
                    DMA("sp", gsC[:, :, :], Gs.ap()[:, SEQ:NT].rearrange("(h p) t -> p h t", p=64), [], ["gsC"])
                bufs = {"O": (qhO, shO, vbO, gsO, "qhO", "shO", "vbO", "gsO"), "C": (qhC, shC, vbC, gsC, "qhC", "shC", "vbC", "gsC")}
                items = [("O", cl, cl * 128) for cl in range(16)] + ([] if last else [("C", 0, SEQ), ("C", 1, SEQ + 128)])
                NP = 3
                AT1 = [sb(sb_, "AT1p_%d" % i, [128, 4, 128], F32) for i in range(NP)]
                AT2 = [sb(sb_, "AT2p_%d" % i, [128, 4, 128], F32) for i in range(NP)]
                ATb = [sb(sb_, "ATbp_%d" % i, [128, 4, 128], BF16) for i in range(NP)]
                osb2 = [sb(sb_, "osbp%d" % i, [64, 512], F32) for i in range(NP)]
                osq2 = [sb(sb_, "osqp%d" % i, [64, 512], BF16) for i in range(NP)]
                orst2 = [sb(sb_, "orstp%d" % i, [64, 512], F32) for i in range(NP)]
                yg2 = [sb(sb_, "ygp%d" % i, [64, 4, 128], BF16) for i in range(NP)]
                pbo_of = {}

                def st1(i):
                    bk, cc, t0 = items[i]
                    qh, sh, vg, gsb, kq, ks, kv, kg = bufs[bk]
                    pa = i % NP
                    cs_ = slice(cc * 128, (cc + 1) * 128)
                    pbF, pkF = bank()
                    for h in range(4):
                        MM(pbF[:, h * 128:(h + 1) * 128], qh[:, 1, h, cs_], qh[:, 0, h, cs_], True, True, [kq], [pkF])
                    pbB, pkB = bank()
                    for h in range(4):
                        MM(pbB[:, h * 128:(h + 1) * 128], qh[:, 3, h, cs_], qh[:, 2, h, cs_], True, True, [kq], [pkB])
                    TT("dve", AT1[pa][:, :, :], pbF[:, :].rearrange("p (h i) -> p h i", h=4),
                       mask[:, 0:128].unsqueeze(1).broadcast_to([128, 4, 128]), ALU.mult, [pkF, "c32"], [("AT1", pa)])
                    TT("dve", AT2[pa][:, :, :], pbB[:, :].rearrange("p (h i) -> p h i", h=4),
                       mask[:, 128:256].unsqueeze(1).broadcast_to([128, 4, 128]), ALU.mult, [pkB, "c32"], [("AT2", pa)])
                    TT("dve", ATb[pa][:, :, :], AT1[pa][:, :, :], AT2[pa][:, :, :], ALU.add, [("AT1", pa), ("AT2", pa)], [("ATb", pa)])

                def st2(i):
                    bk, cc, t0 = items[i]
                    qh, sh, vg, gsb, kq, ks, kv, kg = bufs[bk]
                    pa = i % NP
                    cs_ = slice(cc * 128, (cc + 1) * 128)
                    pbo, pko = bank()
                    pbo_of[i] = (pbo, pko)
                    for h in range(4):
                        oc = pbo[0:64, h * 128:(h + 1) * 128]
                        MM(oc, vg[:, cc, 64 * h:64 * h + 64], ATb[pa][:, h, :], True, False, [kv, ("ATb", pa)], [pko])
                        MM(oc, sh[:, 0, h, cc * 64:(cc + 1) * 64], qh[:, 0, h, cs_], False, False, [ks, kq], [pko])
                        MM(oc, sh[:, 1, h, cc * 64:(cc + 1) * 64], qh[:, 2, h, cs_], False, True, [ks, kq], [pko])
                    CP("act", osb2[pa][:, :], pbo[0:64, :], [pko], [("osb", pa)])
                    ACT(osq2[pa][:, :], pbo[0:64, :], AF.Square, [pko], [("osq", pa)])

                def st3(i):
                    bk, cc, t0 = items[i]
                    qh, sh, vg, gsb, kq, ks, kv, kg = bufs[bk]
                    pa = i % NP
                    cs_ = slice(cc * 128, (cc + 1) * 128)
                    osb, osq, orst, yg = osb2[pa], osq2[pa], orst2[pa], yg2[pa]
                    pb, pk = bank()
                    MM(pb[0:64, :], ones_bf[0:64, 0:64], osq[:, :], True, True, [("osq", pa), "ones"], [pk])
                    ACT(orst[:, :], pb[0:64, :], AF.Ln, [pk], [("orst", pa)], bias=EPSB[0:64, 0:1], scale=1.0 / 64)
                    ACT(orst[:, :], orst[:, :], AF.Exp, [("orst", pa)], [("orst", pa)], scale=-0.5)
                    TT("dve", osb[:, :], osb[:, :], orst[:, :], ALU.mult, [("osb", pa), ("orst", pa)], [("osb", pa)])
                    STT(yg[:, :, :], osb[:, :].rearrange("p (h i) -> p h i", h=4), glag[:, 0:1], gsb[:, :, cs_], ALU.mult, ALU.mult,
                        [("osb", pa), kg, "params"], [("yg", pa)])
                    DMA("pool", Ymix.ap()[0:256, t0:t0 + 128].rearrange("(h p) t -> p h t", p=64), yg[:, :, :], [("yg", pa)],
                        [("Ymg", i)])

                n_it = len(items)
                for i in range(n_it + 2):
                    if i < n_it:
                        st1(i)
                    if 0 <= i - 1 < n_it:
                        st2(i - 1)
                    if 0 <= i - 2 < n_it:
                        st3(i - 2)
            S.barrier()

        if stop_after is None and l + 1 < n_layers:
            load_params_early(l + 1)
        pref_w = stop_after is None
        if pref_w:
            sw = contextlib.ExitStack()
            wup_o = sb(sw, "wup_o", [128, 8, 2 * DFF], BF16)
        with _Phase(skip_phase(l, 's3a')) as st:
            if st is None:
                raise _SkipPhase()
            wout = sb(st, "wout", [128, 8, 1024], BF16)
            DMA("pool", wout[:, :, :], wout_d.ap()[l], (), ["wout"])
            if pref_w:
                DMA("pool", wup_o[:, :, :], wup_d.ap()[l], (), ["wup"])
            TW = 512
            xt2 = [sb(st, "xu%d" % i, [128, 8, TW], F32) for i in range(2)]
            ym2 = [sb(st, "ym%d" % i, [128, 8, TW], BF16) for i in range(2)]
            sq = sb(st, "sq3", [128, 8, TW], BF16)
            tmp = sb(st, "tmp3", [128, 8, TW], F32)
            rstd = sb(st, "rstd3", [128, TW], F32)
            h22 = [sb(st, "h2_%d" % i, [128, 8, TW], BF16) for i in range(2)]
            tiles = tile_list(TW)
            tiles = tiles[:QW // TW] + ([] if last else tiles[-1:])
            src = xsrc(l)

            def load3(ti):
                t0, tw, ci, rl = tiles[ti]
                if ci == 0:
                    if l == 0:
                        DMA_DYN(xt2[ti % 2][:, :, 0:tw], src, t0, tw, [("xu", ti % 2)])
                    else:
                        DMA("sp", xt2[ti % 2][:, :, 0:tw], xq_tile_ap(t0, tw), [], [("xu", ti % 2)])
                    ymt = ym2[ti % 2]
                    DMA("sp", ymt[:, 0:2, 0:tw], Ymix.ap()[0:256, t0:t0 + tw].rearrange("(c p) t -> p c t", p=128), [], [("ym", ti % 2)])
                    DMA("sp", ymt[:, 4:8, 0:tw], Ymix.ap()[512:1024, t0:t0 + tw].rearrange("(c p) t -> p c t", p=128), [], [("ym", ti % 2)])
                    DMA_DYN(ymt[:, 2:4, 0:tw], YF, t0, tw, [("ym", ti % 2)])
                    return
                DMA("sp", xt2[ti % 2][:, :, 0:tw], x_tile_ap(src, t0, tw), [("x", t0 // 256 + k) for k in range(tw // 256)],
                    [("xu", ti % 2)])
                DMA("sp", ym2[ti % 2][:, :, 0:tw], Ymix.ap()[:, t0:t0 + tw].rearrange("(c p) t -> p c t", p=128), [], [("ym", ti % 2)])

            load3(0)
            for ti, (t0, tw, ci, rl) in enumerate(tiles):
                if ti + 1 < len(tiles):
                    load3(ti + 1)
                p = ti % 2
                xt, ym, h2 = xt2[p], ym2[p], h22[p]
                xk = ("xu", p)
                for m in range(8):
                    pb, pk = bank()
                    for kc in range(8):
                        MM(pb[:, 0:tw], wout[:, kc, m * 128:(m + 1) * 128], ym[:, kc, 0:tw], kc == 0, kc == 7, ["wout", ("ym", p)], [pk])
                    STT(xt[:, m, 0:tw], pb[:, 0:tw], modcol(2, m, ci), xt[:, m, 0:tw], ALU.mult, ALU.add, [pk, xk, "params"], [xk])
                norm_mod(xt, sq, tmp, rstd, h2, tw, lambda c: A2[:, c, ci:ci + 1], lambda c: modcol(3, c, ci), [xk], [("h2", p)], "n2")
                if ci == 0:
                    DMA("pool", xq_tile_ap(t0, tw), xt[:, :, 0:tw], [xk], [("xq", t0 // 256 + k) for k in range(tw // 256)])
                else:
                    DMA("pool", x_tile_ap(xs, t0, tw), xt[:, :, 0:tw], [xk], [("x", t0 // 256 + k) for k in range(tw // 256)])
                DMA("pool", H2.ap()[:, t0:t0 + tw].rearrange("(c p) t -> p c t", p=128), h2[:, :, 0:tw], [("h2", p)], [("H2", ti)])
            S.barrier()

        with _Phase(skip_phase(l, 's3b')) as st:
            if st is None:
                raise _SkipPhase()
            if pref_w:
                wup = wup_o
            else:
                wup = sb(st, "wup", [128, 8, 2 * DFF], BF16)
                DMA("pool", wup[:, :, :], wup_d.ap()[l], (), ["wup"])
            wdn = sb(st, "wdn", [128, NJ, 1024], BF16)
            DMA("pool", wdn[:, :, :], wdn_d.ap()[l], (), ["wdn"])
            TW = 512
            xt = sb(st, "xv", [128, 8, TW], F32)
            h22 = [sb(st, "hv%d" % i, [128, 8, TW], BF16) for i in range(2)]
            hid = sb(st, "hid", [128, NJ, TW], BF16)
            tcv2 = [sb(st, "tcv%d" % i, [128, TW], F32) for i in range(2)]
            scv2 = [sb(st, "scv%d" % i, [128, TW], F32) for i in range(2)]
            tiles = tile_list(TW)
            tiles = tiles[:QW // TW] + ([] if last else tiles[-1:])
            xk = "xv"

            def xio_ap(t0, tw, ci):
                return xq_tile_ap(t0, tw) if ci == 0 else x_tile_ap(xs, t0, tw)

            def load4(ti):
                t0, tw, ci, rl = tiles[ti]
                DMA("sp", h22[ti % 2][:, :, 0:tw], H2.ap()[:, t0:t0 + tw].rearrange("(c p) t -> p c t", p=128), [], [("hv", ti % 2)])

            load4(0)
            for ti, (t0, tw, ci, rl) in enumerate(tiles):
                if ti + 1 < len(tiles):
                    load4(ti + 1)
                p = ti % 2
                h2 = h22[p]
                hk = ("hv", p)
                DMA("sp", xt[:, :, 0:tw], xio_ap(t0, tw, ci), [], [xk])
                for j in range(NJ):
                    pb, pk = bank()
                    for kc in range(8):
                        MM(pb[:, 0:tw], wup[:, kc, j * 128:(j + 1) * 128], h2[:, kc, 0:tw], kc == 0, kc == 7, ["wup", hk], [pk])
                    pbu, pku = bank()
                    for kc in range(8):
                        MM(pbu[:, 0:tw], wup[:, kc, DFF + j * 128:DFF + (j + 1) * 128], h2[:, kc, 0:tw], kc == 0, kc == 7,
                           ["wup", hk], [pku])
                    tcv, scv = tcv2[j % 2], scv2[j % 2]
                    tk, sk = ("tcv", j % 2), ("scv", j % 2)
                    ACT(tcv[:, 0:tw], pb[:, 0:tw], AF.Identity, [pk, "params"], [tk], bias=fcb[:, j:j + 1], scale=fcw[:, j, 1:2])
                    a3 = tcv[:, 0:tw].rearrange("p (r l) -> p r l", l=rl)
                    p3 = pb[:, 0:tw].rearrange("p (r l) -> p r l", l=rl)
                    STT(a3[:, :, 1:rl], p3[:, :, 0:rl - 1], fcw[:, j, 0:1], a3[:, :, 1:rl], ALU.mult, ALU.add, [pk, tk, "params"], [tk])
                    STT(a3[:, :, 0:rl - 1], p3[:, :, 1:rl], fcw[:, j, 2:3], a3[:, :, 0:rl - 1], ALU.mult, ALU.add, [pk, tk, "params"], [tk])
                    ACT(scv[:, 0:tw], tcv[:, 0:tw], AF.Silu, [tk], [sk])
                    TT("dve", hid[:, j, 0:tw], scv[:, 0:tw], pbu[:, 0:tw], ALU.mult, [sk, pku], [("hid", j)])
                for m in range(8):
                    pb, pk = bank()
                    for j in range(NJ):
                        MM(pb[:, 0:tw], wdn[:, j, m * 128:(m + 1) * 128], hid[:, j, 0:tw], j == 0, j == NJ - 1, ["wdn", ("hid", j)], [pk])
                    STT(xt[:, m, 0:tw], pb[:, 0:tw], modcol(5, m, ci), xt[:, m, 0:tw], ALU.mult, ALU.add, [pk, xk, "params"], [xk])
                DMA("pool", xio_ap(t0, tw, ci), xt[:, :, 0:tw], [xk], [("xq" if ci == 0 else "x", t0 // 256)])
            S.barrier()

        if pref_w:
            sw.close()

    with contextlib.ExitStack() as st:
        TW = 512
        xt2 = [sb(st, "xf%d" % i, [128, 8, TW], F32) for i in range(2)]
        sq = sb(st, "sqf", [128, 8, TW], BF16)
        tmp = sb(st, "tmpf", [128, 8, TW], F32)
        rstd = sb(st, "rstdf", [128, TW], F32)
        ot2 = [sb(st, "of%d" % i, [128, 8, TW], F32) for i in range(2)]
        full = n_layers == DEPTH and stop_after is None
        src = xq if full else (xs if (n_layers > 0 and (stop_after[0], PH.index(stop_after[1])) >= (0, 4)) else xT_in)
        nt = QW // TW

        def loadf(ti):
            DMA("sp", xt2[ti % 2][:, :, :], xq_tile_ap(ti * TW, TW) if full else x_tile_ap(src, ti * TW, TW), [], [("xf", ti % 2)])

        loadf(0)
        for ti in range(nt):
            if ti + 1 < nt:
                loadf(ti + 1)
            p = ti % 2
            norm_mod(xt2[p], sq, tmp, rstd, ot2[p], TW, lambda c: gfin[:, c:c + 1], None, [("xf", p)], [("of", p)], "nf")
            DMA("sp", x_tile_ap(outT, ti * TW, TW), ot2[p][:, :, :], [("of", p)], [("out", ti)])
    S.emit()
    ES.close()
    return nc


def _pm(a, kc):
    n = a.shape[-1]
    return np.ascontiguousarray(a.reshape(kc, 128, n).transpose(1, 0, 2))


def _layout_inputs(inp):
    f = lambda a: np.ascontiguousarray(np.asarray(a, dtype=np.float32))
    L = DEPTH
    shared = {}
    shared["n1g"] = f(np.stack([inp["norm1_g"][l].reshape(8, 128).T for l in range(L)]))
    shared["n2g"] = f(np.stack([inp["norm2_g"][l].reshape(8, 128).T for l in range(L)]))
    wmod_q, bmod_q = [], []
    for r in range(4):
        cs_ = slice(r * 1536, (r + 1) * 1536)
        wmod_q.append(f(np.stack([np.asarray(inp["w_mod"][l])[:, cs_].reshape(8, 128, 3, 512).transpose(2, 1, 0, 3) for l in range(L)])))
        bmod_q.append(f(np.stack([np.asarray(inp["b_mod"][l])[cs_].reshape(12, 128).T for l in range(L)])))
    shared["win"] = f(np.stack([_pm(np.asarray(inp["w_in"][l]), 8) for l in range(L)]))
    shared["wa2"] = f(np.stack([np.asarray(inp["gla_w_a2"][l]).transpose(1, 0, 2) for l in range(L)]))
    shared["ba2"] = f(np.stack([np.asarray(inp["gla_b_a2"][l]).T for l in range(L)]))
    shared["glag"] = f(np.stack([np.asarray(inp["gla_norm_g"][l]).reshape(64, 1) for l in range(L)]))
    shared["fftw"] = f(np.stack([np.asarray(inp["fft_w"][l]).transpose(1, 0, 2) for l in range(L)]))
    shared["convw"] = f(np.stack([np.asarray(inp["conv_w"][l]).reshape(3, 2, 128).transpose(2, 1, 0) for l in range(L)]))
    shared["convb"] = f(np.stack([np.asarray(inp["conv_b"][l]).reshape(2, 128).T for l in range(L)]))
    shared["poolw"] = f(np.stack([np.asarray(inp["pool_w"][l]).transpose(1, 0, 2) for l in range(L)]))
    shared["poolsc"] = f(np.stack([np.asarray(inp["pool_scale"][l]).reshape(4, 64).T for l in range(L)]))
    shared["wout"] = f(np.stack([_pm(np.asarray(inp["w_out"][l]), 8) for l in range(L)]))
    shared["wup"] = f(np.stack([_pm(np.asarray(inp["ffn_w_up"][l]), 8) for l in range(L)]))
    shared["fcw"] = f(np.stack([np.asarray(inp["ffn_conv_w"][l]).reshape(3, NJ, 128).transpose(2, 1, 0) for l in range(L)]))
    shared["fcb"] = f(np.stack([np.asarray(inp["ffn_conv_b"][l]).reshape(NJ, 128).T for l in range(L)]))
    shared["wdn"] = f(np.stack([_pm(np.asarray(inp["ffn_w_down"][l]), NJ) for l in range(L)]))
    shared["gfin"] = f(np.asarray(inp["final_norm_g"]).reshape(8, 128).T)
    shared["cst16"] = CST16
    shared["cst32"] = CST32
    maps = []
    x = np.asarray(inp["x"], dtype=np.float32)
    ctx = np.asarray(inp["ctx"], dtype=np.float32)
    c = np.asarray(inp["c"], dtype=np.float32)
    cc = np.asarray(inp["c_ctx"], dtype=np.float32)
    per_b = []
    for b in range(2):
        xT = np.ascontiguousarray(np.concatenate([x[b], ctx[b]], 0).T)
        cf = np.ascontiguousarray(np.stack([c[b].reshape(8, 128).T, cc.reshape(8, 128).T], -1))
        per_b.append((xT, cf))
    for core in range(8):
        m = dict(shared)
        m["xT"], m["c_fm"] = per_b[core // 4]
        m["wmod"], m["bmod"] = wmod_q[core % 4], bmod_q[core % 4]
        m["qoff"] = np.array([[(core % 4) * (SEQ // 4)]], dtype=np.int32)
        m["qoff2"] = np.array([[(core % 4) * 1024]], dtype=np.int32)
        maps.append(m)
    return maps


_NC = {}


def kernel(**inputs):
    if "nc" not in _NC:
        _NC["nc"] = build_program()
    nc = _NC["nc"]
    maps = _layout_inputs(inputs)
    res = run_bass_kernel_spmd(nc, maps, core_ids=list(range(8)))
    out = np.empty((2, SEQ, D), dtype=np.float32)
    q = SEQ // 4
    for core in range(8):
        b, r = core // 4, core % 4
        oT = res.results[core]["outT"]
        out[b, r * q:(r + 1) * q, :] = oT.T
    return out
```

```python
import contextlib
import os
import numpy as np
_LVL = int(os.environ.get('S1_LEVEL', '9'))
_NT1 = int(os.environ.get('S1_TILES', '99'))
import concourse.bass as bass
import concourse.mybir as mybir
from concourse.bass_utils import run_bass_kernel_spmd

F32 = mybir.dt.float32
BF16 = mybir.dt.bfloat16
AF = mybir.ActivationFunctionType
ALU = mybir.AluOpType
AX = mybir.AxisListType

D = 1024
SEQ = 8192
CTX = 256
NT = SEQ + CTX
DEPTH = 2
DFF = 2816
NJ = 22
EPS = 1e-6
COMPUTE = ("pe", "act", "dve", "pool")


class _Op:
    __slots__ = ("eng", "fn", "deps", "seq", "is_dma", "semkey", "cum", "sig", "inc")


class Sched:
    def __init__(self, nc):
        self.nc = nc
        self.ops = []
        self.q = {e: [] for e in ("pe", "act", "dve", "pool", "sp")}
        self.last_w = {}
        self.readers = {}
        self.dma_cnt = {}
        self.last_dma = {}
        self.same_eng_window = 1

    def _deps(self, reads, writes):
        deps = set()
        for r in reads:
            w = self.last_w.get(r)
            if w is not None:
                deps.add(w)
        for r in writes:
            w = self.last_w.get(r)
            if w is not None:
                deps.add(w)
            for x in self.readers.get(r, ()):
                deps.add(x)
        return deps

    def _commit(self, oid, reads, writes):
        o = self.ops[oid]
        for r in reads:
            lst = self.readers.setdefault(r, [])
            if not o.is_dma:
                lst[:] = [x for x in lst if self.ops[x].is_dma or self.ops[x].eng != o.eng]
            lst.append(oid)
        for r in writes:
            self.last_w[r] = oid
            self.readers[r] = []

    def _add(self, o, reads, writes):
        o.deps = self._deps(reads, writes)
        o.seq = len(self.q[o.eng])
        oid = len(self.ops)
        self.ops.append(o)
        self.q[o.eng].append(oid)
        self._commit(oid, reads, writes)
        return oid

    def op(self, eng, fn, reads=(), writes=()):
        o = _Op()
        o.eng = eng
        o.fn = fn
        o.is_dma = False
        return self._add(o, reads, writes)

    def dma(self, eng, out, in_, reads=(), writes=(), semkey=None):
        o = _Op()
        o.eng = eng
        o.is_dma = True
        o.fn = lambda e: e.dma_start(out=out, in_=in_)
        o.semkey = semkey if semkey is not None else writes[0]
        o.inc = 16
        self.dma_cnt[o.semkey] = self.dma_cnt.get(o.semkey, 0) + 16
        o.cum = self.dma_cnt[o.semkey]
        oid = self._add(o, reads, writes)
        self.last_dma[o.semkey] = oid
        return oid

    def cc(self, fn, semkey="cc"):
        o = _Op()
        o.eng = "pool"
        o.is_dma = True
        o.fn = fn
        o.semkey = semkey
        o.inc = 1
        self.dma_cnt[semkey] = self.dma_cnt.get(semkey, 0) + 1
        o.cum = self.dma_cnt[semkey]
        oid = self._add(o, [], [])
        self.last_dma[semkey] = oid
        return oid

    def barrier(self, skip_cc=False):
        lasts = set(q[-1] for q in self.q.values() if q) | set(v for k, v in self.last_dma.items() if not (skip_cc and k == "cc"))
        if skip_cc:
            lasts = set(x for x in lasts if not (self.ops[x].is_dma and self.ops[x].semkey == "cc"))
        for e in self.q:
            o = _Op()
            o.eng = e
            o.fn = None
            o.is_dma = False
            o.deps = set(lasts)
            o.seq = len(self.q[e])
            oid = len(self.ops)
            self.ops.append(o)
            self.q[e].append(oid)

    def emit(self):
        nc = self.nc
        with contextlib.ExitStack() as st:
            esem = {e: st.enter_context(nc.semaphore("s_" + e)) for e in COMPUTE}
            dsem = {}
            for i, k in enumerate(self.dma_cnt):
                dsem[k] = st.enter_context(nc.semaphore("d%d" % i))
            block = st.enter_context(nc.Block())
            ops = self.ops
            win = self.same_eng_window

            def skip(o, po):
                if po.eng != o.eng or o.is_dma or o.fn is None:
                    return False
                if po.eng == "pe":
                    return True
                return o.seq - po.seq > win

            need = set()
            for o in ops:
                for d in o.deps:
                    po = ops[d]
                    if po.is_dma or po.fn is None:
                        continue
                    if skip(o, po):
                        continue
                    need.add(d)
            cnt = {e: 0 for e in COMPUTE}
            for o in ops:
                o.sig = 0
            for e in COMPUTE:
                for oid in self.q[e]:
                    if oid in need:
                        cnt[e] += 1
                        ops[oid].sig = cnt[e]
            self.sig_counts = cnt

            def run(ename, eng):
                waited = {}
                for oid in self.q[ename]:
                    o = ops[oid]
                    for d in sorted(o.deps):
                        po = ops[d]
                        if po.fn is None:
                            continue
                        if po.is_dma:
                            key = ("d", po.semkey)
                            if waited.get(key, 0) >= po.cum:
                                continue
                            waited[key] = po.cum
                            eng.wait_ge(dsem[po.semkey], po.cum)
                        else:
                            if d not in need or skip(o, po):
                                continue
                            key = ("e", po.eng)
                            if waited.get(key, 0) >= po.sig:
                                continue
                            waited[key] = po.sig
                            eng.wait_ge(esem[po.eng], po.sig)
                    if o.fn is None:
                        continue
                    ins = o.fn(eng)
                    if o.is_dma:
                        ins.then_inc(dsem[o.semkey], o.inc)
                    elif o.sig:
                        ins.then_inc(esem[ename], 1)
                if ename == "sp":
                    for k, v in self.dma_cnt.items():
                        eng.wait_ge(dsem[k], v)

            @block.sync
            def _(e):
                run("sp", e)

            @block.tensor
            def _(e):
                run("pe", e)

            @block.scalar
            def _(e):
                run("act", e)

            @block.vector
            def _(e):
                run("dve", e)

            @block.gpsimd
            def _(e):
                run("pool", e)


def _pool_mat(w, n):
    t = np.arange(n)
    lo = np.clip(t - w // 2, 0, n - 1)
    hi = np.clip(t + w // 2 - 1, 0, n - 1)
    cnt = (hi - lo + 1).astype(np.float64)
    m = np.zeros((n, n))
    for tt in range(n):
        m[lo[tt]:hi[tt] + 1, tt] = 1.0 / cnt[tt]
    m -= np.eye(n)
    return m


C16 = {}
C32 = {}


def _consts():
    c16 = {}
    c32 = {}
    a = np.arange(128)
    ang = 2 * np.pi * np.outer(a, a) / 128.0
    C128, S128 = np.cos(ang), np.sin(ang)
    c16["CS1"] = np.concatenate([C128, S128], 1)
    c16["CS2"] = np.concatenate([-S128, C128], 1)
    n2 = np.arange(128) % 64
    tw = 2 * np.pi * np.outer(n2, np.arange(128)) / 8192.0
    c32["TT1"] = np.concatenate([np.cos(tw), np.sin(tw)], 1)
    c32["TT2"] = np.concatenate([np.sin(tw), np.cos(tw)], 1)
    b = np.arange(64)
    a64 = 2 * np.pi * np.outer(b, b) / 64.0
    C64, S64 = np.cos(a64), np.sin(a64)
    bdc = np.zeros((128, 128))
    bds = np.zeros((128, 128))
    for e in range(2):
        bdc[e * 64:(e + 1) * 64, e * 64:(e + 1) * 64] = C64
        bds[e * 64:(e + 1) * 64, e * 64:(e + 1) * 64] = -S64
    c16["BDC"] = bdc
    c16["BDS"] = bds
    cs = np.zeros((128, 128))
    cs[:64, :64] = C64
    cs[:64, 64:] = S64
    c32["C64S64"] = cs
    k = np.arange(256)
    a256 = 2 * np.pi * np.outer(k, k) / 256.0
    c16["C256"] = np.cos(a256).reshape(2, 128, 256).transpose(1, 0, 2).reshape(128, 512)
    c16["NS256"] = (-np.sin(a256)).reshape(2, 128, 256).transpose(1, 0, 2).reshape(128, 512)
    pm = np.zeros((128, 4, 128))
    pmc = np.zeros((128, 4, 2, 256))
    for g, w in enumerate((2, 4, 8, 16)):
        m64 = _pool_mat(w, 64)
        pm[:64, g, :64] = m64
        pm[64:, g, 64:] = m64
        mc = _pool_mat(w, 256)
        pmc[:, g, :, :] = mc.reshape(2, 128, 256).transpose(1, 0, 2)
    c16["PM"] = pm.reshape(128, 512)
    c16["PMC"] = pmc.reshape(128, 2048)
    c16["IDENT"] = np.eye(128)
    j = np.arange(128)
    mf = (j[:, None] <= j[None, :]).astype(np.float64)
    mb = (j[:, None] >= j[None, :]).astype(np.float64)
    c32["MASK"] = np.concatenate([mf, mb], 1)
    pp = np.arange(128)[:, None] // 32
    cc_ = np.arange(256)[None, :] // 64
    c32["MASKBD"] = (pp == cc_).astype(np.float64)
    rm = np.ones((128, 1024))
    rm[:, ::128] = 0.0
    c32["RMASK"] = rm
    return c16, c32


def _pack(d):
    offs = {}
    o = 0
    arrs = []
    for k, v in d.items():
        offs[k] = (o, v.shape[1])
        o += v.shape[1]
        arrs.append(v)
    return offs, np.ascontiguousarray(np.concatenate(arrs, 1).astype(np.float32))


_C16, _C32 = _consts()
OFF16, CST16 = _pack(_C16)
OFF32, CST32 = _pack(_C32)


class _SkipPhase(Exception):
    pass


class _Phase:
    def __init__(self, skip):
        self.skip = skip
        self.st = None

    def __enter__(self):
        if self.skip:
            return None
        self.st = contextlib.ExitStack()
        return self.st.__enter__()

    def __exit__(self, et, ev, tb):
        if self.st is not None:
            self.st.__exit__(et, ev, tb)
        return et is _SkipPhase


def build_program(n_layers=DEPTH, dbg=False, stop_after=None):
    nc = bass.Bass("TRN2", target_bir_lowering=False)

    def din(name, shape):
        return nc.dram_tensor(name, list(shape), F32, kind="ExternalInput")

    xT_in = din("xT", [D, NT])
    c_fm = din("c_fm", [128, 8, 2])
    n1g_d = din("n1g", [DEPTH, 128, 8])
    n2g_d = din("n2g", [DEPTH, 128, 8])
    wmod_d = din("wmod", [DEPTH, 3, 128, 8, 512])
    bmod_d = din("bmod", [DEPTH, 128, 12])
    win_d = din("win", [DEPTH, 128, 8, 2080])
    wa2_d = din("wa2", [DEPTH, 16, 2, 128])
    ba2_d = din("ba2", [DEPTH, 128, 2])
    glag_d = din("glag", [DEPTH, 64, 1])
    fftw_d = din("fftw", [DEPTH, 64, 4, 64])
    convw_d = din("convw", [DEPTH, 128, 2, 3])
    convb_d = din("convb", [DEPTH, 128, 2])
    poolw_d = din("poolw", [DEPTH, 64, 4, 64])
    poolsc_d = din("poolsc", [DEPTH, 64, 4])
    wout_d = din("wout", [DEPTH, 128, 8, 1024])
    wup_d = din("wup", [DEPTH, 128, 8, 2 * DFF])
    fcw_d = din("fcw", [DEPTH, 128, NJ, 3])
    fcb_d = din("fcb", [DEPTH, 128, NJ])
    wdn_d = din("wdn", [DEPTH, 128, NJ, 1024])
    gfin_d = din("gfin", [128, 8])
    cst16_d = din("cst16", list(CST16.shape))
    cst32_d = din("cst32", list(CST32.shape))
    qoff_d = nc.dram_tensor("qoff", [1, 1], mybir.dt.int32, kind="ExternalInput")
    qoff2_d = nc.dram_tensor("qoff2", [1, 1], mybir.dt.int32, kind="ExternalInput")
    QW = SEQ // 4
    outT = nc.dram_tensor("outT", [D, QW], F32, kind="ExternalOutput")

    kw = {"kind": "ExternalOutput"} if dbg else {}
    xs = nc.dram_tensor("xs", [D, NT], F32, **kw)
    QKf = nc.dram_tensor("QKf", [256, NT], BF16, **kw)
    AFB = nc.dram_tensor("AFB", [32, NT], BF16, **kw)
    Vtm = nc.dram_tensor("Vtm", [NT, 256], BF16, **kw)
    Ud = nc.dram_tensor("Ud", [256, NT], BF16, **kw)
    Gs = nc.dram_tensor("Gs", [256, NT], BF16, **kw)
    Ymix = nc.dram_tensor("Ymix", [D, NT], BF16, **kw)
    H2 = nc.dram_tensor("H2", [D, NT], BF16, **kw)
    xq = nc.dram_tensor("xq", [8, 128, QW], F32)
    QKfL = nc.dram_tensor("QKfL", [256, QW], BF16)
    AFBL = nc.dram_tensor("AFBL", [32, QW], BF16)
    VtmL = nc.dram_tensor("VtmL", [QW, 256], BF16)
    UdL = nc.dram_tensor("UdL", [256, QW], BF16)
    QKfG = nc.dram_tensor("QKfG", [4 * 256, QW], BF16)
    AFBG = nc.dram_tensor("AFBG", [4 * 32, QW], BF16)
    VtmG = nc.dram_tensor("VtmG", [SEQ, 256], BF16)
    UdG = nc.dram_tensor("UdG", [4 * 256, QW], BF16)
    YF = nc.dram_tensor("YF", [256, NT], BF16)
    ML = [nc.dram_tensor("ML%d" % i, [128, 24], F32) for i in range(DEPTH)]
    MG = [nc.dram_tensor("MG%d" % i, [4 * 128, 24], F32) for i in range(DEPTH)]
    ApL = nc.dram_tensor("ApL", [128, 2 * 16 * 64], BF16)
    ApG = nc.dram_tensor("ApG", [4 * 128, 2 * 16 * 64], BF16)
    EpL = nc.dram_tensor("EpL", [128, 32], F32)
    EpG = nc.dram_tensor("EpG", [4 * 128, 32], F32)

    def xq_tile_ap(t0, tw):
        return xq.ap()[:, :, t0:t0 + tw].rearrange("c p t -> p c t")
    QKp = nc.dram_tensor("QKp", [4, 128, NT], BF16)
    Sd = nc.dram_tensor("Sd", [128, 2, 67 * 64], BF16)
    PH = ["p0", "s1", "s2a", "s2b", "s3a", "s3b"]

    def skip_phase(l, name):
        if stop_after is None:
            return False
        sl, sn = stop_after
        return (l, PH.index(name)) > (sl, PH.index(sn))

    S = Sched(nc)
    ES = contextlib.ExitStack()
    DRAMK = ("QKf", "Gs", "Ud", "Ymc", "Ymp", "AFB", "Vtm", "Ymg", "Ymf", "Ymfc", "x", "out", "H2", "xq", "QKp", "Sd", "L2", "YF")

    _uid = [0]

    def sb(st, name, shape, dt):
        _uid[0] += 1
        return st.enter_context(nc.sbuf_tensor("sb%d_%s" % (_uid[0], name), list(shape), dt))

    def MM(out, lhsT, rhs, start, stop, R, W):
        S.op("pe", lambda e: e.matmul(out, lhsT=lhsT, rhs=rhs, start=start, stop=stop), R, W)

    def TR(out, in_, ident, R, W):
        S.op("pe", lambda e: e.transpose(out=out, in_=in_, identity=ident), R, W)

    def ACT(out, in_, func, R, W, bias=None, scale=None):
        kw = {}
        if bias is not None:
            kw["bias"] = bias
        if scale is not None:
            kw["scale"] = scale
        S.op("act", lambda e: e.activation(out=out, in_=in_, func=func, **kw), R, W)

    def TT(eng, out, in0, in1, op, R, W):
        S.op(eng, lambda e: e.tensor_tensor(out=out, in0=in0, in1=in1, op=op), R, W)

    def TS(eng, out, in0, s1, s2, op0, op1, R, W):
        if op1 is None:
            S.op(eng, lambda e: e.tensor_scalar(out=out, in0=in0, scalar1=s1, scalar2=None, op0=op0), R, W)
        else:
            S.op(eng, lambda e: e.tensor_scalar(out=out, in0=in0, scalar1=s1, scalar2=s2, op0=op0, op1=op1), R, W)

    def STT(out, in0, scalar, in1, op0, op1, R, W):
        S.op("dve", lambda e: e.scalar_tensor_tensor(out=out, in0=in0, scalar=scalar, in1=in1, op0=op0, op1=op1), R, W)

    def CP(eng, out, in_, R, W):
        if eng == "act":
            S.op(eng, lambda e: e.activation(out=out, in_=in_, func=AF.Copy), R, W)
        else:
            S.op(eng, lambda e: e.tensor_copy(out=out, in_=in_), R, W)

    def MEMSET(eng, ap, val, W):
        S.op(eng, lambda e: e.memset(ap, val), (), W)

    def DMA(q, out, in_, R, W):
        to_dram = not isinstance(W[0], str) and W[0][0] in DRAMK or (isinstance(W[0], str) and W[0] in DRAMK)
        if q == "pool" and out.dtype == in_.dtype:
            q = "sp"
        S.dma(q, out, in_, R, W, semkey="st" if to_dram else None)

    rs = ES.enter_context(nc.sync.register("rs_qoff"))
    S.op("sp", lambda e: e.reg_load(rs, qoff_d.ap()[0:1, 0:1]), (), ())
    _snap = {}

    def qv(e):
        if "v" not in _snap:
            _snap["v"] = e.snap(rs, min_val=0, max_val=SEQ - QW)
        return _snap["v"]

    rs2 = ES.enter_context(nc.sync.register("rs_qoff2"))
    S.op("sp", lambda e: e.reg_load(rs2, qoff2_d.ap()[0:1, 0:1]), (), ())

    def qv2(e):
        if "v2" not in _snap:
            _snap["v2"] = e.snap(rs2, min_val=0, max_val=3 * 1024)
        return _snap["v2"]

    def DYN(out, apfn, W):
        o_ = _Op()
        o_.eng = "sp"
        o_.is_dma = True
        o_.fn = lambda e: e.dma_start(out=out, in_=apfn(e))
        o_.semkey = W[0]
        o_.inc = 16
        S.dma_cnt[o_.semkey] = S.dma_cnt.get(o_.semkey, 0) + 16
        o_.cum = S.dma_cnt[o_.semkey]
        oid = S._add(o_, [], W)
        S.last_dma[o_.semkey] = oid

    def DMA_DYN(out, src_t, t0, tw, W, rows=None):
        r0, r1 = rows if rows is not None else (0, src_t.shape[0])
        DYN(out, lambda e: src_t.ap()[r0:r1, bass.ds(qv(e) + t0, tw)].rearrange("(c p) t -> p c t", p=128), W)

    PSB = [ES.enter_context(nc.psum_tensor("psb%d" % i, [128, 512], F32)) for i in range(7)]
    _pu = [0]

    def psum8(st_, dt, cols):
        _pu[0] += 1
        return st_.enter_context(nc.psum_tensor("ps8_%d" % _pu[0], [128, cols], dt))
    _bank = [0]

    def bank():
        i = _bank[0] % 7
        _bank[0] += 1
        return PSB[i], ("ps", i)

    def K16(name, rows=128, st=None):
        o, n = OFF16[name]
        t = sb(st, "k_" + name + "_%d" % len(S.ops), [rows, n], BF16)
        DMA("pool", t[:, :], cst16_d.ap()[0:rows, o:o + n], (), ["c16"])
        return t[:, :]

    def K32(name, rows=128, st=None, cols=None):
        o, n = OFF32[name]
        if cols is not None:
            n = cols
        t = sb(st, "k_" + name + "_%d" % len(S.ops), [rows, n], F32)
        DMA("sp", t[:, :], cst32_d.ap()[0:rows, o:o + n], (), ["c32"])
        return t[:, :]

    EPSB = sb(ES, "epsb", [128, 1], F32)
    MEMSET("pool", EPSB[:, :], EPS, ["epsb"])
    ones_bf = sb(ES, "ones_bf", [128, 128], BF16)
    MEMSET("pool", ones_bf[:, :], 1.0, ["ones"])
    svec = sb(ES, "svec", [128, 8, 2], F32)
    DMA("sp", svec[:, :, :], c_fm.ap(), (), ["svec"])
    ACT(svec[:, :, :], svec[:, :, :], AF.Silu, ["svec"], ["svec"])
    svec_bf = sb(ES, "svec_bf", [128, 8, 2], BF16)
    CP("dve", svec_bf[:, :, :], svec[:, :, :], ["svec"], ["svec"])
    gfin = sb(ES, "gfin", [128, 8], F32)
    DMA("sp", gfin[:, :], gfin_d.ap(), (), ["gfin"])
    modv_l = [sb(ES, "modv%d" % i, [128, 48, 2], F32) for i in range(DEPTH)]
    bmod_l = [sb(ES, "bmod%d" % i, [128, 12], F32) for i in range(DEPTH)]
    n1g_l = [sb(ES, "n1g%d" % i, [128, 8], F32) for i in range(DEPTH)]
    n2g_l = [sb(ES, "n2g%d" % i, [128, 8], F32) for i in range(DEPTH)]
    A1_l = [sb(ES, "A1_%d" % i, [128, 8, 2], F32) for i in range(DEPTH)]
    A2_l = [sb(ES, "A2_%d" % i, [128, 8, 2], F32) for i in range(DEPTH)]
    modv, A1, A2 = modv_l[0], A1_l[0], A2_l[0]

    class ModCalc:
        def __init__(self, lq, st_):
            self.lq = lq
            self.mk = ("mod", lq)
            self.wm = [sb(st_, "wm%d" % i, [128, 8, 512], BF16) for i in range(2)]
            self.wmf = sb(st_, "wmf", [128, 8, 512], F32)
            self.modq = sb(st_, "modq", [128, 12, 2], F32)
            self.pbm = psum8(st_, F32, 512)
            self.pkm = ("psmod", lq)
            DMA("sp", bmod_l[lq][:, :], bmod_d.ap()[lq], (), [("bmod", lq)])
            DMA("sp", n1g_l[lq][:, :], n1g_d.ap()[lq], (), [("ng", lq)])
            DMA("sp", n2g_l[lq][:, :], n2g_d.ap()[lq], (), [("ng", lq)])
            self.n = 0

        def piece(self):
            i = self.n
            self.n += 1
            w = self.wm[i % 2]
            wk = ("wm", i % 2)
            src_ = wmod_d.ap()[self.lq, i]
            if i % 2 == 0:
                DMA("pool", w[:, :, :], src_, (), [wk])
            else:
                DMA("sp", self.wmf[:, :, :], src_, (), ["wmf"])
                CP("dve", w[:, 0:4, :], self.wmf[:, 0:4, :], ["wmf"], [wk])
                CP("act", w[:, 4:8, :], self.wmf[:, 4:8, :], ["wmf"], [wk])
            for m in range(4):
                col = (i * 4 + m) * 2
                for kc in range(8):
                    MM(self.pbm[:, col:col + 2], w[:, kc, m * 128:(m + 1) * 128], svec_bf[:, kc, :], kc == 0, kc == 7,
                       [wk, "svec"], [self.pkm])

        def pieces(self, k):
            for _ in range(k):
                if self.n < 3:
                    self.piece()

        def finish(self):
            self.pieces(3)
            lq = self.lq
            TT("dve", self.modq[:, :, :], self.pbm[:, 0:24].rearrange("p (c i) -> p c i", i=2),
               bmod_l[lq][:, :].unsqueeze(2).broadcast_to([128, 12, 2]), ALU.add, [self.pkm, ("bmod", lq)], [("modq", lq)])
            DMA("sp", ML[lq].ap(), self.modq[:, :, :].rearrange("p c i -> p (c i)"), [("modq", lq)], [("L2", 8 + lq)])

    def mod_cc(lq):
        S.cc(lambda e: e.collective_compute("AllGather", ALU.bypass, replica_groups=[[0, 1, 2, 3], [4, 5, 6, 7]],
                                            ins=[ML[lq].ap().opt()], outs=[MG[lq].ap().opt()]))

    def mod_load(lq):
        mk = ("mod", lq)
        mv = modv_l[lq]
        for r_ in range(4):
            DMA("sp", mv[:, 12 * r_:12 * r_ + 12, :], MG[lq].ap()[128 * r_:128 * (r_ + 1), :].rearrange("p (c i) -> p c i", i=2), [], [mk])
        for (Aout, ng, kind) in ((A1_l[lq], n1g_l[lq], 1), (A2_l[lq], n2g_l[lq], 4)):
            TS("dve", Aout[:, :, :], mv[:, kind * 8:(kind + 1) * 8, :], 1.0, None, ALU.add, None, [mk], [mk])
            TT("dve", Aout[:, :, :], Aout[:, :, :], ng[:, :].unsqueeze(2).broadcast_to([128, 8, 2]), ALU.mult,
               [mk, ("ng", lq)], [mk])

    def phase0_mod(lq):
        with contextlib.ExitStack() as st_:
            mc = ModCalc(lq, st_)
            mc.finish()
            S.barrier()
            mod_cc(lq)

    wa2 = sb(ES, "wa2", [16, 2, 128], BF16)
    nba2 = sb(ES, "nba2", [128, 2], F32)
    glag = sb(ES, "glag", [64, 1], F32)
    fftw = sb(ES, "fftw", [64, 4, 64], F32)
    ABg = sb(ES, "ABg", [64, 4, 128], BF16)
    convw = sb(ES, "convw", [128, 2, 3], F32)
    convb = sb(ES, "convb", [128, 2], F32)
    poolw = sb(ES, "poolw", [64, 4, 64], BF16)
    poolsc = sb(ES, "poolsc", [64, 4], F32)
    fcw = sb(ES, "fcw", [128, NJ, 3], F32)
    fcb = sb(ES, "fcb", [128, NJ], F32)

    def modcol(kind, c, i):
        return modv[:, kind * 8 + c, i:i + 1]

    def tile_list(tw):
        tl = [(t * tw, tw, 0, 64) for t in range(SEQ // tw)]
        tl.append((SEQ, CTX, 1, CTX))
        return tl

    def xsrc(l):
        return xT_in if l == 0 else xs

    def x_tile_ap(t, t0, tw):
        return t.ap()[:, t0:t0 + tw].rearrange("(c p) t -> p c t", p=128)

    def norm_mod(xt, sq, tmp, rstd, hx, tw, Acol, Bcol, Rx, Wkeys, tag):
        ACT(sq[:, :, 0:tw], xt[:, :, 0:tw], AF.Square, Rx, [tag + "sq"])
        pb, pk = bank()
        for c in range(8):
            MM(pb[:, 0:tw], ones_bf[:, :], sq[:, c, 0:tw], c == 0, c == 7, [tag + "sq", "ones"], [pk])
        ACT(rstd[:, 0:tw], pb[:, 0:tw], AF.Ln, [pk], [tag + "rstd"], bias=EPSB[:, 0:1], scale=1.0 / D)
        ACT(rstd[:, 0:tw], rstd[:, 0:tw], AF.Exp, [tag + "rstd"], [tag + "rstd"], scale=-0.5)
        TT("dve", tmp[:, :, 0:tw], xt[:, :, 0:tw], rstd[:, 0:tw].unsqueeze(1).broadcast_to([128, 8, tw]), ALU.mult,
           Rx + [tag + "rstd"], [tag + "tmp"])
        for c in range(8):
            if Bcol is None:
                if c % 2 == 0:
                    TS("dve", hx[:, c, 0:tw], tmp[:, c, 0:tw], Acol(c), None, ALU.mult, None, [tag + "tmp"], Wkeys)
                else:
                    ACT(hx[:, c, 0:tw], tmp[:, c, 0:tw], AF.Copy, [tag + "tmp"], Wkeys, scale=Acol(c))
            else:
                ACT(hx[:, c, 0:tw], tmp[:, c, 0:tw], AF.Identity, [tag + "tmp", "params"], Wkeys, bias=Bcol(c), scale=Acol(c))

    cs_p = K32("C64S64", 64, ES)

    def load_params_early(lq):
        DMA("pool", wa2[:, :, :], wa2_d.ap()[lq], (), ["params"])
        DMA("sp", nba2[:, :], ba2_d.ap()[lq], (), ["params"])
        DMA("sp", glag[:, :], glag_d.ap()[lq], (), ["params"])
        DMA("sp", fftw[:, :, :], fftw_d.ap()[lq], (), ["params"])
        DMA("sp", convw[:, :, :], convw_d.ap()[lq], (), ["params"])
        DMA("sp", convb[:, :], convb_d.ap()[lq], (), ["params"])
        DMA("pool", poolw[:, :, :], poolw_d.ap()[lq], (), ["params"])
        DMA("sp", poolsc[:, :], poolsc_d.ap()[lq], (), ["params"])
        TS("dve", nba2[:, :], nba2[:, :], -1.0, None, ALU.mult, None, ["params"], ["params"])
        for g in range(4):
            pb, pk = bank()
            rhs = fftw[:, g, :].rearrange("m (dl dh) -> m dh dl", dl=2)
            MM(pb[0:64, 0:64], cs_p[:, 0:64], rhs, True, True, ["c32", "params"], [pk])
            MM(pb[0:64, 64:128], cs_p[:, 64:128], rhs, True, True, ["c32", "params"], [pk])
            CP("dve", ABg[:, g, :], pb[0:64, 0:128], [pk], ["params"])

    def load_params_ffn(lq):
        DMA("sp", fcw[:, :, :], fcw_d.ap()[lq], (), ["params"])
        DMA("sp", fcb[:, :], fcb_d.ap()[lq], (), ["params"])

    phase0_mod(0)
    for l in range(n_layers):
        last = l == DEPTH - 1
        modv, A1, A2 = modv_l[l], A1_l[l], A2_l[l]
        with _Phase(skip_phase(l, 'p0')) as st:
            if st is None:
                raise _SkipPhase()
            if l > 0:
                mod_load(l)
                load_params_ffn(l)
            else:
                load_params_early(0)
                load_params_ffn(0)
            if l == 0:
                S.barrier()
                mod_load(0)
            S.barrier()

        with _Phase(skip_phase(l, 's1')) as st:
            if st is None:
                raise _SkipPhase()
            win = sb(st, "win", [128, 8, 2080], BF16)
            if l + 1 < n_layers:
                DMA("pool", win[:, :, :], win_d.ap()[l], (), ["win"])
            else:
                winf = sb(st, "winf", [128, 4, 2080], F32)
                DMA("pool", win[:, 0:4, :], win_d.ap()[l][:, 0:4, :], (), ["win"])
                DMA("sp", winf[:, :, :], win_d.ap()[l][:, 4:8, :], (), ["winf"])
                CP("dve", win[:, 4:6, :], winf[:, 0:2, :], ["winf"], ["win"])
                CP("act", win[:, 6:8, :], winf[:, 2:4, :], ["winf"], ["win"])
            TW = 512
            xt2 = [sb(st, "xt%d" % i, [128, 8, TW], F32) for i in range(2)]
            sq = sb(st, "sq", [128, 8, TW], BF16)
            tmp = sb(st, "tmp", [128, 8, TW], F32)
            rstd = sb(st, "rstd", [128, TW], F32)
            hx2 = [sb(st, "hx%d" % i, [128, 8, TW], BF16) for i in range(2)]
            stg2 = [sb(st, "stg%d" % i, [128, 8, TW], BF16) for i in range(2)]
            stp2 = [sb(st, "stp%d" % i, [64, 4, TW], BF16) for i in range(2)]
            afb2 = [sb(st, "afb%d" % i, [32, TW], BF16) for i in range(2)]
            vst2 = [sb(st, "vst%d" % i, [128, 4, 256], BF16) for i in range(2)]
            ptm = sb(st, "ptm", [128, 4, 256], BF16)
            hs = sb(st, "hs", [128, TW], F32)
            mcv = sb(st, "mcv", [128, TW], F32)
            acc = sb(st, "acc", [128, TW], F32)
            pooled = sb(st, "pooled", [64, 4, TW], BF16)
            tiles = tile_list(TW)
            tiles = tiles[:QW // TW] + tiles[-1:]
            src = xsrc(l)
            PMk = K16("PM", 128, st)
            PMCk = K16("PMC", 128, st)

            def load_x(ti):
                t0, tw, ci, rl = tiles[ti]
                if ci == 0:
                    if l == 0:
                        DMA_DYN(xt2[ti % 2][:, :, 0:tw], src, t0, tw, [("xt", ti % 2)])
                    else:
                        DMA("sp", xt2[ti % 2][:, :, 0:tw], xq_tile_ap(t0, tw), [], [("xt", ti % 2)])
                    return
                DMA("sp", xt2[ti % 2][:, :, 0:tw], x_tile_ap(src, t0, tw), [("x", t0 // 256 + k) for k in range(tw // 256)],
                    [("xt", ti % 2)])

            if _NT1 < 99:
                tiles = tiles[:_NT1] + tiles[-1:]
            def do_norm(ti):
                t0_, tw_, ci_, rl_ = tiles[ti]
                p_ = ti % 2
                norm_mod(xt2[p_], sq, tmp, rstd, hx2[p_], tw_, lambda c: A1[:, c, ci_:ci_ + 1], lambda c: modcol(0, c, ci_),
                         [("xt", p_)], [("hx", p_)], "n1")

            mc_next = ModCalc(l + 1, st) if (l + 1 < n_layers) else None
            load_x(0)
            if len(tiles) > 1:
                load_x(1)
            do_norm(0)
            for ti, (t0, tw, ci, rl) in enumerate(tiles):
                p = ti % 2
                xt, hx, stg, stp, afb, vst = xt2[p], hx2[p], stg2[p], stp2[p], afb2[p], vst2[p]
                hk = ("hx", p)

                def proj(col0, m, pb, pk, off=0):
                    for kc in range(8):
                        MM(pb[0:m, off:off + tw], win[:, kc, col0:col0 + m], hx[:, kc, 0:tw], kc == 0, kc == 7, ["win", hk], [pk])

                if _LVL < 2:
                    continue
                for slot, col0 in ((0, 416), (1, 0)):
                    pb, pk = bank()
                    proj(col0, 128, pb, pk)
                    CP("act" if slot == 0 else "dve", stg[:, slot, 0:tw], pb[:, 0:tw], [pk], [("stg", p)])
                pb, pk = bank()
                proj(384, 32, pb, pk)
                CP("act", afb[:, 0:tw], pb[0:32, 0:tw], [pk], [("afb", p)])
                for j in range(2):
                    pb, pk = bank()
                    proj(544 + 128 * j, 128, pb, pk)
                    ACT(stg[:, 2 + j, 0:tw], pb[:, 0:tw], AF.Silu, [pk], [("stg", p)])
                for j in range(2):
                    pb, pk = bank()
                    proj(800 + 128 * j, 128, pb, pk)
                    CP("dve", stg[:, 4 + j, 0:tw], pb[:, 0:tw], [pk], [("stg", p)])
                if ti + 1 < len(tiles):
                    do_norm(ti + 1)
                if ti + 2 < len(tiles):
                    load_x(ti + 2)
                for j in range(2 if _LVL >= 3 else 0):
                    pbh, pkh = bank()
                    proj(1056 + 128 * j, 128, pbh, pkh)
                    pbB, pkB = bank()
                    proj(1312 + 128 * j, 128, pbB, pkB)
                    pbC, pkC = bank()
                    proj(1568 + 128 * j, 128, pbC, pkC)
                    CP("act", hs[:, 0:tw], pbh[:, 0:tw], [pkh], ["hs"])
                    TT("dve", mcv[:, 0:tw], pbC[:, 0:tw], hs[:, 0:tw], ALU.mult, [pkC, "hs"], ["mcv"])
                    ACT(acc[:, 0:tw], mcv[:, 0:tw], AF.Identity, ["mcv", "params"], ["acc"], bias=convb[:, j:j + 1],
                        scale=convw[:, j, 1:2])
                    a3 = acc[:, 0:tw].rearrange("p (r l) -> p r l", l=rl)
                    m3 = mcv[:, 0:tw].rearrange("p (r l) -> p r l", l=rl)
                    STT(a3[:, :, 1:rl], m3[:, :, 0:rl - 1], convw[:, j, 0:1], a3[:, :, 1:rl], ALU.mult, ALU.add,
                        ["mcv", "acc", "params"], ["acc"])
                    STT(a3[:, :, 0:rl - 1], m3[:, :, 1:rl], convw[:, j, 2:3], a3[:, :, 0:rl - 1], ALU.mult, ALU.add,
                        ["mcv", "acc", "params"], ["acc"])
                    TT("dve", stg[:, 6 + j, 0:tw], acc[:, 0:tw], pbB[:, 0:tw], ALU.mult, ["acc", pkB], [("stg", p)])
                nsub = tw // 128
                for i in range(nsub if _LVL >= 4 else 0):
                    pb, pk = bank()
                    for kc in range(8):
                        b_ = win[:, kc, 128:384]
                        rhs2 = bass.AP(b_.tensor, b_.offset, [list(b_.ap[0]), [1696, 2], [1, 256]])
                        MM(pb[:, 0:512], hx[:, kc, i * 128:(i + 1) * 128], rhs2, kc == 0, kc == 7, ["win", hk], [pk])
                    CP("dve", vst[:, i, :], pb[:, 0:256], [pk], [("vst", p)])
                    CP("dve", ptm[:, i, :], pb[:, 256:512], [pk], ["ptm"])
                for g in range(4 if _LVL >= 5 else 0):
                    pb, pk = bank()
                    if ci == 0:
                        pm = PMk.rearrange("p (g t) -> p g t", g=4)
                        for i in range(nsub):
                            MM(pb[0:64, i * 128:(i + 1) * 128], ptm[:, i, 64 * g:64 * g + 64], pm[:, g, :], True, True,
                               ["ptm", "c16"], [pk])
                    else:
                        pmc = PMCk.rearrange("p (g k t) -> p g k t", g=4, k=2)
                        for kc in range(2):
                            MM(pb[0:64, 0:256], ptm[:, kc, 64 * g:64 * g + 64], pmc[:, g, kc, :], kc == 0, kc == 1,
                               ["ptm", "c16"], [pk])
                    CP("act", pooled[:, g, 0:tw], pb[0:64, 0:tw], [pk], [("pooled", g)])
                    pb2, pk2 = bank()
                    MM(pb2[0:64, 0:tw], poolw[:, g, :], pooled[:, g, 0:tw], True, True, [("pooled", g), "params"], [pk2])
                    TS("dve", stp[:, g, 0:tw], pb2[0:64, 0:tw], poolsc[:, g:g + 1], None, ALU.mult, None, [pk2, "params"],
                       [("stp", p)])
                if mc_next is not None:
                    mc_next.pieces(2 if ti < len(tiles) - 1 else 12)
                if _LVL < 6:
                    continue
                ts_ = slice(t0, t0 + tw)
                DMA("pool", QKf.ap()[:, ts_].rearrange("(s p) t -> p s t", p=128), stg[:, 0:2, 0:tw], [("stg", p)], [("QKf", ti)])
                DMA("pool", Gs.ap()[:, ts_].rearrange("(s p) t -> p s t", p=128), stg[:, 2:4, 0:tw], [("stg", p)], [("Gs", ti)])
                DMA("pool", (UdL if ci == 0 else Ud).ap()[:, ts_].rearrange("(s p) t -> p s t", p=128), stg[:, 4:6, 0:tw], [("stg", p)],
                    [("Ud", ti)])
                DMA("pool", Ymix.ap()[512:768, ts_].rearrange("(s p) t -> p s t", p=128), stg[:, 6:8, 0:tw], [("stg", p)],
                    [("Ymc", ti)])
                DMA("pool", Ymix.ap()[768:1024, ts_].rearrange("(g p) t -> p g t", p=64), stp[:, :, 0:tw], [("stp", p)],
                    [("Ymp", ti)])
                DMA("pool", AFB.ap()[:, ts_], afb[:, 0:tw], [("afb", p)], [("AFB", ti)])
                DMA("pool", Vtm.ap()[ts_, :].rearrange("(i p) c -> p i c", p=128), vst[:, 0:nsub, :], [("vst", p)], [("Vtm", ti)])
            if mc_next is not None:
                mc_next.finish()
            S.barrier()
            S.cc(lambda e: e.collective_compute("AllGather", ALU.bypass, replica_groups=[[0, 1, 2, 3], [4, 5, 6, 7]],
                                                ins=[UdL.ap().opt()], outs=[UdG.ap().opt()]))
            if mc_next is not None:
                mod_cc(l + 1)
            s1_done = True

        if not skip_phase(l, 's1'):
            S.barrier(skip_cc=True)
        NS1 = 17
        allk = lambda nm: [(nm, ti) for ti in range(NS1)]

        def run_fft():
          with contextlib.ExitStack() as st:
            usb1 = sb(st, "usb", [64, NT], BF16)
            usb2 = [usb1, usb1]
            Gsb2 = [sb(st, "Gsb%d" % i, [128, 128, 64], BF16) for i in range(2)]
            Zp2 = [sb(st, "Zp%d" % i, [128, 2, 32, 128], BF16) for i in range(2)]
            m1_2 = [sb(st, "m1_%d" % i, [128, 512], F32) for i in range(2)]
            m2_2 = [sb(st, "m2_%d" % i, [128, 512], F32) for i in range(2)]
            Ysb1 = sb(st, "Ysb", [128, 32, 128], BF16)
            Ysb2 = [Ysb1, Ysb1]
            Gc = sb(st, "Gc", [128, 2, 128], BF16)
            Yc = sb(st, "Yc", [64, 256], BF16)
            CS1, CS2 = K16("CS1", 128, st), K16("CS2", 128, st)
            TT1 = K32("TT1", 128, st).rearrange("p (a k) -> p a k", a=2)
            TT2 = K32("TT2", 128, st).rearrange("p (a k) -> p a k", a=2)
            BDC, BDS = K16("BDC", 128, st), K16("BDS", 128, st)
            C256 = K16("C256", 128, st).rearrange("p (k t) -> p k t", k=2)
            NS256 = K16("NS256", 128, st).rearrange("p (k t) -> p k t", k=2)
            scale = 1.0 / np.sqrt(float(SEQ * 64))
            def fkeys(g):
                return ("usb", 0), ("Gsb", g % 2), ("Zp", g % 2), ("Ysb", 0)

            def f_load(g):
                usb = usb2[g % 2]
                KU, KG, KZ, KY = fkeys(g)
                for r_ in range(4):
                    DMA("sp", usb[:, r_ * QW:(r_ + 1) * QW], UdG.ap()[r_ * 256 + 64 * g:r_ * 256 + 64 * g + 64, :], [], [KU])
                DMA("sp", usb[:, SEQ:NT], Ud.ap()[64 * g:64 * g + 64, SEQ:NT], [], [KU])

            def f_step1(g):
                usb, Gsb = usb2[g % 2], Gsb2[g % 2]
                KU, KG, KZ, KY = fkeys(g)
                for q4 in range(16):
                    pb, pk = bank()
                    for i in range(4):
                        n2_ = q4 * 4 + i
                        MM(pb[:, i * 128:(i + 1) * 128], usb[:, n2_:SEQ:64], ABg[:, g, :], True, True, [KU, "params"], [pk])
                    CP("act" if q4 % 2 == 0 else "dve", Gsb[:, :, q4 * 4:q4 * 4 + 4], pb[:, :].rearrange("p (n c) -> p c n", n=4),
                       [pk], [KG])
                if not last:
                    pb, pk = bank()
                    for j in range(2):
                        MM(pb[:, j * 128:(j + 1) * 128], usb[:, SEQ + 128 * j:SEQ + 128 * (j + 1)], ABg[:, g, :], True, True,
                           [KU, "params"], [pk])
                    CP("act", Gc[:, :, :], pb[:, 0:256].rearrange("p (j c) -> p j c", j=2), [pk], ["Gc"])
                    pb, pk = bank()
                    for j in range(2):
                        MM(pb[0:64, 0:256], Gc[:, j, 0:64], C256[:, j, :], j == 0, False, ["Gc", "c16"], [pk])
                    for j in range(2):
                        MM(pb[0:64, 0:256], Gc[:, j, 64:128], NS256[:, j, :], False, j == 1, ["Gc", "c16"], [pk])
                    ACT(Yc[:, :], pb[0:64, 0:256], AF.Copy, [pk], ["Yc"], scale=1.0 / 128.0)
                    for dl in range(2):
                        r0 = 256 + 64 * g + 32 * dl
                        DMA("pool", Ymix.ap()[r0:r0 + 32, SEQ:SEQ + CTX], Yc[dl:64:2, :], ["Yc"], [("Ymfc", g, dl)])

            def f_stepA(g):
                Gsb, Zp = Gsb2[g % 2], Zp2[g % 2]
                KU, KG, KZ, KY = fkeys(g)
                for dp in range(16):
                    pb, pk = bank()
                    for e_ in range(2):
                        dh = dp * 2 + e_
                        oc = pb[:, e_ * 256:(e_ + 1) * 256]
                        MM(oc, Gsb[:, 2 * dh:2 * dh + 2, :].rearrange("p a n -> p (a n)"), CS1, True, False, [KG, "c16"], [pk])
                        MM(oc, Gsb[:, 64 + 2 * dh:64 + 2 * dh + 2, :].rearrange("p a n -> p (a n)"), CS2, False, True,
                           [KG, "c16"], [pk])
                    p4 = pb[:, :].rearrange("p (e a k) -> p e a k", e=2, a=2)
                    m1, m2 = m1_2[dp % 2], m2_2[dp % 2]
                    k1_, k2_ = ("m1", dp % 2), ("m2", dp % 2)
                    m1v = m1[:, :].rearrange("p (e a k) -> p e a k", e=2, a=2)
                    m2v = m2[:, :].rearrange("p (e a k) -> p e a k", e=2, a=2)
                    TT("dve", m1v, p4, TT1.unsqueeze(1).broadcast_to([128, 2, 2, 128]), ALU.mult, [pk, "c32"], [k1_])
                    TT("dve", m2v, p4, TT2.unsqueeze(1).broadcast_to([128, 2, 2, 128]), ALU.mult, [pk, "c32"], [k2_])
                    TT("dve", Zp[:, 0, 2 * dp:2 * dp + 2, :], m1v[:, :, 0, :], m1v[:, :, 1, :], ALU.subtract, [k1_], [(KZ, 0)])
                    TT("pool", Zp[:, 1, 2 * dp:2 * dp + 2, :], m2v[:, :, 0, :], m2v[:, :, 1, :], ALU.add, [k2_], [(KZ, 1)])

            def f_stepC(g):
                Zp, Ysb = Zp2[g % 2], Ysb2[g % 2]
                KU, KG, KZ, KY = fkeys(g)
                for q8 in range(8):
                    pb, pk = bank()
                    MM(pb[:, :], BDC, Zp[:, 0, 4 * q8:4 * q8 + 4, :].rearrange("p a k -> p (a k)"), True, False, [(KZ, 0), "c16"], [pk])
                    MM(pb[:, :], BDS, Zp[:, 1, 4 * q8:4 * q8 + 4, :].rearrange("p a k -> p (a k)"), False, True, [(KZ, 1), "c16"], [pk])
                    ACT(Ysb[:, 4 * q8:4 * q8 + 4, :], pb[:, :].rearrange("p (a k) -> p a k", a=4), AF.Copy, [pk], [KY], scale=scale)
                for dl in range(2):
                    r0 = 64 * g + 32 * dl
                    dst = bass.AP(YF, r0 * NT, [[128, 64], [NT, 32], [1, 128]])
                    DMA("pool", dst, Ysb[dl * 64:(dl + 1) * 64, :, :], [KY], [("Ymf", g, dl)])

            f_load(0)
            for g in range(4):
                f_step1(g)
                if g + 1 < 4:
                    f_load(g + 1)
                if g > 0:
                    f_stepC(g - 1)
                f_stepA(g)
            f_stepC(3)
            S.barrier()


        with _Phase(skip_phase(l, 's2a')) as st:
            if st is None:
                raise _SkipPhase()
            NCH = 66
            GW = 512
            ident = K16("IDENT", 128, st)
            mask = K32("MASK", 128, st)
            rmask = K32("RMASK", 128, st, cols=GW)
            maskBD = K32("MASKBD", 128, st)
            groups = [(g * GW, GW, 4, [g]) for g in range(4)] + [(SEQ, CTX, 2, [16])]

            def idxS(d, c):
                if d == 0:
                    return c + 2 if c < 64 else (0 if c == 64 else 1)
                return c if c < 64 else (64 if c == 64 else 65)

            def idxA(d, c):
                if d == 0:
                    return c + 3 if c < 64 else (1 if c == 64 else 2)
                return c - 1 if (1 <= c < 64) else (66 if c == 0 else (63 if c == 64 else 64))

            with contextlib.ExitStack() as sa_outer:
                Sall = sb(sa_outer, "Sall", [128, 2, 67, 64], F32)
                Sbf = sb(sa_outer, "Sbf", [128, 2, 67, 64], BF16)
                Eall = sb(sa_outer, "Eall", [128, 2, NCH], F32)
                AL = sb(sa_outer, "AL", [128, 2, 16, 64], F32)
                EL = sb(sa_outer, "EL", [128, 2, 16], F32)
                sa = contextlib.ExitStack()
                qraw2 = [sb(sa, "qraw%d" % i, [128, 2, GW], BF16) for i in range(2)]
                afr2 = [sb(sa, "afr%d" % i, [16, 2, GW], BF16) for i in range(2)]
                vg2 = [sb(sa, "vga%d" % i, [128, 4, 256], BF16) for i in range(2)]
                qkst2 = [sb(sa, "qkst%d" % i, [128, 4, GW], BF16) for i in range(2)]
                ktm2 = [sb(sa, "ktm%d" % i, [128, 4, 2, 128], BF16) for i in range(2)]
                lfb = sb(sa, "lfb", [128, 2, GW], F32)
                cum = sb(sa, "cum", [128, 2, GW], F32)
                tot = sb(sa, "tot", [128, 2, 4], F32)
                eE = sb(sa, "eE", [128, 4, GW], F32)
                tmpA2 = [sb(sa, "tmpA%d" % i, [128, 256], F32) for i in range(2)]
                PST = psum8(sa, BF16, 1024)
                MEMSET("pool", Sall[:, 0, 0, :], 0.0, ["Sall0"])
                MEMSET("pool", Sall[:, 1, 65, :], 0.0, ["Sall1"])
                nA = 0
                def loadA(gi):
                    g0, gw, nchk, s1t = groups[gi]
                    p = gi % 2
                    qraw, afr, vg = qraw2[p], afr2[p], vg2[p]
                    qsrc, ksrc = QKf.ap()[0:128, g0:g0 + gw], QKf.ap()[128:256, g0:g0 + gw]
                    afs, abs_ = AFB.ap()[0:16, g0:g0 + gw], AFB.ap()[16:32, g0:g0 + gw]
                    vsrc = Vtm.ap()[g0:g0 + gw, :]
                    DMA("sp", qraw[:, 0, 0:gw], qsrc, [], [("qraw", p)])
                    DMA("sp", qraw[:, 1, 0:gw], ksrc, [], [("qraw", p)])
                    DMA("sp", afr[:, 0, 0:gw], afs, [], [("afr", p)])
                    DMA("sp", afr[:, 1, 0:gw], abs_, [], [("afr", p)])
                    DMA("sp", vg[:, 0:nchk, :], vsrc.rearrange("(c p) d -> p c d", p=128), [], [("vga", p)])

                loadA(0)
                for gi, (g0, gw, nchk, s1t) in enumerate(groups):
                    if gi + 1 < len(groups):
                        loadA(gi + 1)
                    p = gi % 2
                    qraw, afr, vg, qkst, ktm = qraw2[p], afr2[p], vg2[p], qkst2[p], ktm2[p]
                    own = g0 < SEQ
                    for d in range(2):
                        pb, pk = bank()
                        MM(pb[:, 0:gw], wa2[:, d, :], afr[:, d, 0:gw], True, True, [("afr", p), "params"], [pk])
                        ACT(lfb[:, d, 0:gw], pb[:, 0:gw], AF.Exp, [pk, "params"], [("lfb", d)], bias=nba2[:, d:d + 1], scale=-1.0)
                        ACT(lfb[:, d, 0:gw], lfb[:, d, 0:gw], AF.Ln, [("lfb", d)], [("lfb", d)], bias=1.0, scale=1.0)
                        S.op("dve", (lambda o_, d0, d1: (lambda e: e.tensor_tensor_scan(out=o_, data0=d0, data1=d1, initial=0.0,
                                                                                         op0=ALU.mult, op1=ALU.add)))(
                            cum[:, d, 0:gw], rmask[:, 0:gw], lfb[:, d, 0:gw]), [("lfb", d), "c32"], [("cum", d)])
                        S.op("dve", (lambda o_, i_: (lambda e: e.tensor_reduce(out=o_, in_=i_, axis=AX.X, op=ALU.add)))(
                            tot[:, d, 0:nchk], lfb[:, d, 0:gw].rearrange("p (c t) -> p c t", t=128)), [("lfb", d)], [("tot", d)])
                    TT("dve", cum[:, 1, 0:gw], lfb[:, 1, 0:gw], cum[:, 1, 0:gw], ALU.subtract, [("lfb", 1), ("cum", 1)], [("cum", 1)])
                    TT("dve", cum[:, 1, 0:gw].rearrange("p (c t) -> p c t", t=128), cum[:, 1, 0:gw].rearrange("p (c t) -> p c t", t=128),
                       tot[:, 1, 0:nchk].unsqueeze(2).broadcast_to([128, nchk, 128]), ALU.add, [("cum", 1), ("tot", 1)], [("cum", 1)])
                    c0 = g0 // 128
                    for d in range(2):
                        Edst = EL[:, d, c0:c0 + nchk] if own else Eall[:, d, c0:c0 + nchk]
                        ACT(Edst, tot[:, d, 0:nchk], AF.Exp, [("tot", d)], [("Eall", d)], scale=-1.0 / 16)
                        ACT(eE[:, 2 * d, 0:gw], cum[:, d, 0:gw], AF.Exp, [("cum", d)], [("eE", 2 * d)], scale=-1.0 / 16)
                        ACT(eE[:, 2 * d + 1, 0:gw], cum[:, d, 0:gw], AF.Exp, [("cum", d)], [("eE", 2 * d + 1)], scale=1.0 / 16)
                        STT(qkst[:, 2 * d, 0:gw], qraw[:, 0, 0:gw], 32.0 ** -0.5, eE[:, 2 * d, 0:gw], ALU.mult, ALU.mult,
                            [("qraw", p), ("eE", 2 * d)], [("qkst", p)])
                        TT("pool", qkst[:, 2 * d + 1, 0:gw], qraw[:, 1, 0:gw], eE[:, 2 * d + 1, 0:gw], ALU.mult,
                           [("qraw", p), ("eE", 2 * d + 1)], [("qkst", p)])
                    DMA("pool", QKp.ap()[:, :, g0:g0 + gw].rearrange("v p t -> p v t"), qkst[:, :, 0:gw], [("qkst", p)], [("QKp", gi)])
                    for cc in range(nchk):
                        for d in range(2):
                            o_ = (cc * 2 + d) * 128
                            TR(PST[:, o_:o_ + 128], qkst[:, 2 * d + 1, cc * 128:(cc + 1) * 128], ident, [("qkst", p), "c16"], ["pst"])
                    CP("act", ktm[:, 0:nchk, :, :], PST[:, 0:nchk * 256].rearrange("p (c d k) -> p c d k", d=2, k=128), ["pst"],
                       [("ktm", p)])
                    for cc in range(nchk):
                        c = c0 + cc
                        pb, pk = bank()
                        for d in range(2):
                            MM(pb[:, d * 256:(d + 1) * 256], ktm[:, cc, d, :], vg[:, cc, :], True, True, [("ktm", p), ("vga", p)], [pk])
                        for d in range(2):
                            tA = tmpA2[nA % 2]
                            tk_ = ("tmpA", nA % 2)
                            nA += 1
                            esc = EL[:, d, c:c + 1] if own else Eall[:, d, c:c + 1]
                            adst = AL[:, d, c, :] if own else Sall[:, d, idxA(d, c), :]
                            STT(tA[:, :], pb[:, d * 256:(d + 1) * 256], esc, maskBD, ALU.mult, ALU.mult,
                                [pk, ("Eall", d), "c32"], [tk_])
                            S.op("dve", (lambda o_, i_: (lambda e: e.tensor_reduce(out=o_, in_=i_, axis=AX.X, op=ALU.add)))(
                                adst, tA[:, :].rearrange("p (h v) -> p v h", h=4)), [tk_], ["Sall%d" % d])
                DMA("pool", ApL.ap(), AL[:, :, :, :].rearrange("p d c v -> p (d c v)"), ["Sall0", "Sall1"], [("L2", 5)])
                DMA("pool", EpL.ap(), EL[:, :, :].rearrange("p d c -> p (d c)"), [("Eall", 0), ("Eall", 1)], [("L2", 6)])
                S.barrier()
                sa.close()
                S.cc(lambda e: e.collective_compute("AllGather", ALU.bypass, replica_groups=[[0, 1, 2, 3], [4, 5, 6, 7]],
                                                    ins=[ApL.ap().opt()], outs=[ApG.ap().opt()]))
                S.cc(lambda e: e.collective_compute("AllGather", ALU.bypass, replica_groups=[[0, 1, 2, 3], [4, 5, 6, 7]],
                                                    ins=[EpL.ap().opt()], outs=[EpG.ap().opt()]))
                run_fft()
                for r_ in range(4):
                    rows = slice(r_ * 128, (r_ + 1) * 128)
                    for d in range(2):
                        DMA("sp", Eall[:, d, 16 * r_:16 * r_ + 16], EpG.ap()[rows, d * 16:(d + 1) * 16], [], [("Eall", d)])
                        src3 = ApG.ap()[rows, d * 1024:(d + 1) * 1024].rearrange("p (c v) -> p c v", v=64)
                        if d == 0:
                            DMA("sp", Sbf[:, 0, 16 * r_ + 3:16 * r_ + 19, :], src3, [], [("Sbf", 0)])
                        elif r_ == 0:
                            DMA("sp", Sbf[:, 1, 66:67, :], src3[:, 0:1, :], [], [("Sbf", 1)])
                            DMA("sp", Sbf[:, 1, 0:15, :], src3[:, 1:16, :], [], [("Sbf", 1)])
                        else:
                            DMA("sp", Sbf[:, 1, 16 * r_ - 1:16 * r_ + 15, :], src3, [], [("Sbf", 1)])
                orderF = [64, 65] + list(range(0, 63))
                orderB = [65, 64] + list(range(63, 0, -1))
                for i in range(len(orderF)):
                    for d, c in ((0, orderF[i]), (1, orderB[i])):
                        asrc = Sall[:, d, idxA(d, c), :] if c >= 64 else Sbf[:, d, idxA(d, c), :]
                        STT(Sall[:, d, idxA(d, c), :], Sall[:, d, idxS(d, c), :], Eall[:, d, c:c + 1], asrc,
                            ALU.mult, ALU.add, ["Sall%d" % d, ("Eall", d), ("Sbf", d)], ["Sall%d" % d])
                CP("act", Sbf[:, 0, :, :], Sall[:, 0, :, :], ["Sall0"], [("Sbf", 0)])
                CP("dve", Sbf[:, 1, :, :], Sall[:, 1, :, :], ["Sall1"], [("Sbf", 1)])
                DMA("pool", Sd.ap(), Sbf[:, :, :, :].rearrange("p d s v -> p d (s v)"), [("Sbf", 0), ("Sbf", 1)], [("Sd", 0)])
                S.barrier()
            with contextlib.ExitStack() as sb_:
                qhO = sb(sb_, "qhO", [32, 4, 4, QW], BF16)
                shO = sb(sb_, "shO", [32, 2, 4, 16 * 64], BF16)
                vbO = sb(sb_, "vbO", [128, 16, 256], BF16)
                gsO = sb(sb_, "gsO", [64, 4, QW], BF16)
                qhC = sb(sb_, "qhC", [32, 4, 4, CTX], BF16)
                shC = sb(sb_, "shC", [32, 2, 4, 2 * 64], BF16)
                vbC = sb(sb_, "vbC", [128, 2, 256], BF16)
                gsC = sb(sb_, "gsC", [64, 4, CTX], BF16)
                for v_ in range(4):
                    DMA("sp", qhO[:, v_, :, :], QKp.ap()[v_, :, 0:QW].rearrange("(h k) t -> k h t", k=32), [], ["qhO"])
                for d in range(2):
                    koff = 2 * 64 if d == 0 else 0
                    DYN(shO[:, d, :, :], (lambda d_, k_: (lambda e: Sd.ap()[:, d_, bass.ds(qv2(e) + k_, 16 * 64)].rearrange("(h k) w -> k h w", k=32)))(d, koff),
                        ["shO"])
                DMA("sp", vbO[:, :, :], Vtm.ap()[0:QW, :].rearrange("(c p) d -> p c d", p=128), [], ["vbO"])
                DMA("sp", gsO[:, :, :], Gs.ap()[:, 0:QW].rearrange("(h p) t -> p h t", p=64), [], ["gsO"])
                if not last:
                    for v_ in range(4):
                        DMA("sp", qhC[:, v_, :, :], QKp.ap()[v_, :, SEQ:NT].rearrange("(h k) t -> k h t", k=32), [], ["qhC"])
                    for d in range(2):
                        s_ = idxS(d, 64)
                        DMA("sp", shC[:, d, :, :], Sd.ap()[:, d, s_ * 64:(s_ + 2) * 64].rearrange("(h k) w -> k h w", k=32), [], ["shC"])
                    DMA("sp", vbC[:, :, :], Vtm.ap()[SEQ:NT, :].rearrange("(c p) d -> p c d", p=128), [], ["vbC"])
                    DMA("sp", gsC[:, :, :], Gs.ap()[:, SEQ:NT].rearrange("(h p) t -> p h t", p=64), [], ["gsC"])
                bufs = {"O": (qhO, shO, vbO, gsO, "qhO", "shO", "vbO", "gsO"), "C": (qhC, shC, vbC, gsC, "qhC", "shC", "vbC", "gsC")}
                items = [("O", cl, cl * 128) for cl in range(16)] + ([] if last else [("C", 0, SEQ), ("C", 1, SEQ + 128)])
                NP = 3
                AT1 = [sb(sb_, "AT1p_%d" % i, [128, 4, 128], F32) for i in range(NP)]
                AT2 = [sb(sb_, "AT2p_%d" % i, [128, 4, 128], F32) for i in range(NP)]
                ATb = [sb(sb_, "ATbp_%d" % i, [128, 4, 128], BF16) for i in range(NP)]
                osb2 = [sb(sb_, "osbp%d" % i, [64, 512], F32) for i in range(NP)]
                osq2 = [sb(sb_, "osqp%d" % i, [64, 512], BF16) for i in range(NP)]
                orst2 = [sb(sb_, "orstp%d" % i, [64, 512], F32) for i in range(NP)]
                yg2 = [sb(sb_, "ygp%d" % i, [64, 4, 128], BF16) for i in range(NP)]
                pbo_of = {}

                def st1(i):
                    bk, cc, t0 = items[i]
                    qh, sh, vg, gsb, kq, ks, kv, kg = bufs[bk]
                    pa = i % NP
                    cs_ = slice(cc * 128, (cc + 1) * 128)
                    pbF, pkF = bank()
                    for h in range(4):
                        MM(pbF[:, h * 128:(h + 1) * 128], qh[:, 1, h, cs_], qh[:, 0, h, cs_], True, True, [kq], [pkF])
                    pbB, pkB = bank()
                    for h in range(4):
                        MM(pbB[:, h * 128:(h + 1) * 128], qh[:, 3, h, cs_], qh[:, 2, h, cs_], True, True, [kq], [pkB])
                    TT("dve", AT1[pa][:, :, :], pbF[:, :].rearrange("p (h i) -> p h i", h=4),
                       mask[:, 0:128].unsqueeze(1).broadcast_to([128, 4, 128]), ALU.mult, [pkF, "c32"], [("AT1", pa)])
                    TT("dve", AT2[pa][:, :, :], pbB[:, :].rearrange("p (h i) -> p h i", h=4),
                       mask[:, 128:256].unsqueeze(1).broadcast_to([128, 4, 128]), ALU.mult, [pkB, "c32"], [("AT2", pa)])
                    TT("dve", ATb[pa][:, :, :], AT1[pa][:, :, :], AT2[pa][:, :, :], ALU.add, [("AT1", pa), ("AT2", pa)], [("ATb", pa)])

                def st2(i):
                    bk, cc, t0 = items[i]
                    qh, sh, vg, gsb, kq, ks, kv, kg = bufs[bk]
                    pa = i % NP
                    cs_ = slice(cc * 128, (cc + 1) * 128)
                    pbo, pko = bank()
                    pbo_of[i] = (pbo, pko)
                    for h in range(4):
                        oc = pbo[0:64, h * 128:(h + 1) * 128]
                        MM(oc, vg[:, cc, 64 * h:64 * h + 64], ATb[pa][:, h, :], True, False, [kv, ("ATb", pa)], [pko])
                        MM(oc, sh[:, 0, h, cc * 64:(cc + 1) * 64], qh[:, 0, h, cs_], False, False, [ks, kq], [pko])
                        MM(oc, sh[:, 1, h, cc * 64:(cc + 1) * 64], qh[:, 2, h, cs_], False, True, [ks, kq], [pko])
                    CP("act", osb2[pa][:, :], pbo[0:64, :], [pko], [("osb", pa)])
                    ACT(osq2[pa][:, :], pbo[0:64, :], AF.Square, [pko], [("osq", pa)])

                def st3(i):
                    bk, cc, t0 = items[i]
                    qh, sh, vg, gsb, kq, ks, kv, kg = bufs[bk]
                    pa = i % NP
                    cs_ = slice(cc * 128, (cc + 1) * 128)
                    osb, osq, orst, yg = osb2[pa], osq2[pa], orst2[pa], yg2[pa]
                    pb, pk = bank()
                    MM(pb[0:64, :], ones_bf[0:64, 0:64], osq[:, :], True, True, [("osq", pa), "ones"], [pk])
                    ACT(orst[:, :], pb[0:64, :], AF.Ln, [pk], [("orst", pa)], bias=EPSB[0:64, 0:1], scale=1.0 / 64)
                    ACT(orst[:, :], orst[:, :], AF.Exp, [("orst", pa)], [("orst", pa)], scale=-0.5)
                    TT("dve", osb[:, :], osb[:, :], orst[:, :], ALU.mult, [("osb", pa), ("orst", pa)], [("osb", pa)])
                    STT(yg[:, :, :], osb[:, :].rearrange("p (h i) -> p h i", h=4), glag[:, 0:1], gsb[:, :, cs_], ALU.mult, ALU.mult,
                        [("osb", pa), kg, "params"], [("yg", pa)])
                    DMA("pool", Ymix.ap()[0:256, t0:t0 + 128].rearrange("(h p) t -> p h t", p=64), yg[:, :, :], [("yg", pa)],
                        [("Ymg", i)])

                n_it = len(items)
                for i in range(n_it + 2):
                    if i < n_it:
                        st1(i)
                    if 0 <= i - 1 < n_it:
                        st2(i - 1)
                    if 0 <= i - 2 < n_it:
                        st3(i - 2)
            S.barrier()

        if stop_after is None and l + 1 < n_layers:
            load_params_early(l + 1)
        pref_w = stop_after is None
        if pref_w:
            sw = contextlib.ExitStack()
            wup_o = sb(sw, "wup_o", [128, 8, 2 * DFF], BF16)
        with _Phase(skip_phase(l, 's3a')) as st:
            if st is None:
                raise _SkipPhase()
            wout = sb(st, "wout", [128, 8, 1024], BF16)
            DMA("pool", wout[:, :, :], wout_d.ap()[l], (), ["wout"])
            if pref_w:
                DMA("pool", wup_o[:, :, :], wup_d.ap()[l], (), ["wup"])
            TW = 512
            xt2 = [sb(st, "xu%d" % i, [128, 8, TW], F32) for i in range(2)]
            ym2 = [sb(st, "ym%d" % i, [128, 8, TW], BF16) for i in range(2)]
            sq = sb(st, "sq3", [128, 8, TW], BF16)
            tmp = sb(st, "tmp3", [128, 8, TW], F32)
            rstd = sb(st, "rstd3", [128, TW], F32)
            h22 = [sb(st, "h2_%d" % i, [128, 8, TW], BF16) for i in range(2)]
            tiles = tile_list(TW)
            tiles = tiles[:QW // TW] + ([] if last else tiles[-1:])
            src = xsrc(l)

            def load3(ti):
                t0, tw, ci, rl = tiles[ti]
                if ci == 0:
                    if l == 0:
                        DMA_DYN(xt2[ti % 2][:, :, 0:tw], src, t0, tw, [("xu", ti % 2)])
                    else:
                        DMA("sp", xt2[ti % 2][:, :, 0:tw], xq_tile_ap(t0, tw), [], [("xu", ti % 2)])
                    ymt = ym2[ti % 2]
                    DMA("sp", ymt[:, 0:2, 0:tw], Ymix.ap()[0:256, t0:t0 + tw].rearrange("(c p) t -> p c t", p=128), [], [("ym", ti % 2)])
                    DMA("sp", ymt[:, 4:8, 0:tw], Ymix.ap()[512:1024, t0:t0 + tw].rearrange("(c p) t -> p c t", p=128), [], [("ym", ti % 2)])
                    DMA_DYN(ymt[:, 2:4, 0:tw], YF, t0, tw, [("ym", ti % 2)])
                    return
                DMA("sp", xt2[ti % 2][:, :, 0:tw], x_tile_ap(src, t0, tw), [("x", t0 // 256 + k) for k in range(tw // 256)],
                    [("xu", ti % 2)])
                DMA("sp", ym2[ti % 2][:, :, 0:tw], Ymix.ap()[:, t0:t0 + tw].rearrange("(c p) t -> p c t", p=128), [], [("ym", ti % 2)])

            load3(0)
            for ti, (t0, tw, ci, rl) in enumerate(tiles):
                if ti + 1 < len(tiles):
                    load3(ti + 1)
                p = ti % 2
                xt, ym, h2 = xt2[p], ym2[p], h22[p]
                xk = ("xu", p)
                for m in range(8):
                    pb, pk = bank()
                    for kc in range(8):
                        MM(pb[:, 0:tw], wout[:, kc, m * 128:(m + 1) * 128], ym[:, kc, 0:tw], kc == 0, kc == 7, ["wout", ("ym", p)], [pk])
                    STT(xt[:, m, 0:tw], pb[:, 0:tw], modcol(2, m, ci), xt[:, m, 0:tw], ALU.mult, ALU.add, [pk, xk, "params"], [xk])
                norm_mod(xt, sq, tmp, rstd, h2, tw, lambda c: A2[:, c, ci:ci + 1], lambda c: modcol(3, c, ci), [xk], [("h2", p)], "n2")
                if ci == 0:
                    DMA("pool", xq_tile_ap(t0, tw), xt[:, :, 0:tw], [xk], [("xq", t0 // 256 + k) for k in range(tw // 256)])
                else:
                    DMA("pool", x_tile_ap(xs, t0, tw), xt[:, :, 0:tw], [xk], [("x", t0 // 256 + k) for k in range(tw // 256)])
                DMA("pool", H2.ap()[:, t0:t0 + tw].rearrange("(c p) t -> p c t", p=128), h2[:, :, 0:tw], [("h2", p)], [("H2", ti)])
            S.barrier()

        with _Phase(skip_phase(l, 's3b')) as st:
            if st is None:
                raise _SkipPhase()
            if pref_w:
                wup = wup_o
            else:
                wup = sb(st, "wup", [128, 8, 2 * DFF], BF16)
                DMA("pool", wup[:, :, :], wup_d.ap()[l], (), ["wup"])
            wdn = sb(st, "wdn", [128, NJ, 1024], BF16)
            DMA("pool", wdn[:, :, :], wdn_d.ap()[l], (), ["wdn"])
            TW = 512
            xt = sb(st, "xv", [128, 8, TW], F32)
            h22 = [sb(st, "hv%d" % i, [128, 8, TW], BF16) for i in range(2)]
            hid = sb(st, "hid", [128, NJ, TW], BF16)
            tcv2 = [sb(st, "tcv%d" % i, [128, TW], F32) for i in range(2)]
            scv2 = [sb(st, "scv%d" % i, [128, TW], F32) for i in range(2)]
            tiles = tile_list(TW)
            tiles = tiles[:QW // TW] + ([] if last else tiles[-1:])
            xk = "xv"

            def xio_ap(t0, tw, ci):
                return xq_tile_ap(t0, tw) if ci == 0 else x_tile_ap(xs, t0, tw)

            def load4(ti):
                t0, tw, ci, rl = tiles[ti]
                DMA("sp", h22[ti % 2][:, :, 0:tw], H2.ap()[:, t0:t0 + tw].rearrange("(c p) t -> p c t", p=128), [], [("hv", ti % 2)])

            load4(0)
            for ti, (t0, tw, ci, rl) in enumerate(tiles):
                if ti + 1 < len(tiles):
                    load4(ti + 1)
                p = ti % 2
                h2 = h22[p]
                hk = ("hv", p)
                DMA("sp", xt[:, :, 0:tw], xio_ap(t0, tw, ci), [], [xk])
                for j in range(NJ):
                    pb, pk = bank()
                    for kc in range(8):
                        MM(pb[:, 0:tw], wup[:, kc, j * 128:(j + 1) * 128], h2[:, kc, 0:tw], kc == 0, kc == 7, ["wup", hk], [pk])
                    pbu, pku = bank()
                    for kc in range(8):
                        MM(pbu[:, 0:tw], wup[:, kc, DFF + j * 128:DFF + (j + 1) * 128], h2[:, kc, 0:tw], kc == 0, kc == 7,
                           ["wup", hk], [pku])
                    tcv, scv = tcv2[j % 2], scv2[j % 2]
                    tk, sk = ("tcv", j % 2), ("scv", j % 2)
                    ACT(tcv[:, 0:tw], pb[:, 0:tw], AF.Identity, [pk, "params"], [tk], bias=fcb[:, j:j + 1], scale=fcw[:, j, 1:2])
                    a3 = tcv[:, 0:tw].rearrange("p (r l) -> p r l", l=rl)
                    p3 = pb[:, 0:tw].rearrange("p (r l) -> p r l", l=rl)
                    STT(a3[:, :, 1:rl], p3[:, :, 0:rl - 1], fcw[:, j, 0:1], a3[:, :, 1:rl], ALU.mult, ALU.add, [pk, tk, "params"], [tk])
                    STT(a3[:, :, 0:rl - 1], p3[:, :, 1:rl], fcw[:, j, 2:3], a3[:, :, 0:rl - 1], ALU.mult, ALU.add, [pk, tk, "params"], [tk])
                    ACT(scv[:, 0:tw], tcv[:, 0:tw], AF.Silu, [tk], [sk])
                    TT("dve", hid[:, j, 0:tw], scv[:, 0:tw], pbu[:, 0:tw], ALU.mult, [sk, pku], [("hid", j)])
                for m in range(8):
                    pb, pk = bank()
                    for j in range(NJ):
                        MM(pb[:, 0:tw], wdn[:, j, m * 128:(m + 1) * 128], hid[:, j, 0:tw], j == 0, j == NJ - 1, ["wdn", ("hid", j)], [pk])
                    STT(xt[:, m, 0:tw], pb[:, 0:tw], modcol(5, m, ci), xt[:, m, 0:tw], ALU.mult, ALU.add, [pk, xk, "params"], [xk])
                DMA("pool", xio_ap(t0, tw, ci), xt[:, :, 0:tw], [xk], [("xq" if ci == 0 else "x", t0 // 256)])
            S.barrier()

        if pref_w:
            sw.close()

    with contextlib.ExitStack() as st:
        TW = 512
        xt2 = [sb(st, "xf%d" % i, [128, 8, TW], F32) for i in range(2)]
        sq = sb(st, "sqf", [128, 8, TW], BF16)
        tmp = sb(st, "tmpf", [128, 8, TW], F32)
        rstd = sb(st, "rstdf", [128, TW], F32)
        ot2 = [sb(st, "of%d" % i, [128, 8, TW], F32) for i in range(2)]
        full = n_layers == DEPTH and stop_after is None
        src = xq if full else (xs if (n_layers > 0 and (stop_after[0], PH.index(stop_after[1])) >= (0, 4)) else xT_in)
        nt = QW // TW

        def loadf(ti):
            DMA("sp", xt2[ti % 2][:, :, :], xq_tile_ap(ti * TW, TW) if full else x_tile_ap(src, ti * TW, TW), [], [("xf", ti % 2)])

        loadf(0)
        for ti in range(nt):
            if ti + 1 < nt:
                loadf(ti + 1)
            p = ti % 2
            norm_mod(xt2[p], sq, tmp, rstd, ot2[p], TW, lambda c: gfin[:, c:c + 1], None, [("xf", p)], [("of", p)], "nf")
            DMA("sp", x_tile_ap(outT, ti * TW, TW), ot2[p][:, :, :], [("of", p)], [("out", ti)])
    S.emit()
    ES.close()
    return nc


def _pm(a, kc):
    n = a.shape[-1]
    return np.ascontiguousarray(a.reshape(kc, 128, n).transpose(1, 0, 2))


def _layout_inputs(inp):
    f = lambda a: np.ascontiguousarray(np.asarray(a, dtype=np.float32))
    L = DEPTH
    shared = {}
    shared["n1g"] = f(np.stack([inp["norm1_g"][l].reshape(8, 128).T for l in range(L)]))
    shared["n2g"] = f(np.stack([inp["norm2_g"][l].reshape(8, 128).T for l in range(L)]))
    wmod_q, bmod_q = [], []
    for r in range(4):
        cs_ = slice(r * 1536, (r + 1) * 1536)
        wmod_q.append(f(np.stack([np.asarray(inp["w_mod"][l])[:, cs_].reshape(8, 128, 3, 512).transpose(2, 1, 0, 3) for l in range(L)])))
        bmod_q.append(f(np.stack([np.asarray(inp["b_mod"][l])[cs_].reshape(12, 128).T for l in range(L)])))
    shared["win"] = f(np.stack([_pm(np.asarray(inp["w_in"][l]), 8) for l in range(L)]))
    shared["wa2"] = f(np.stack([np.asarray(inp["gla_w_a2"][l]).transpose(1, 0, 2) for l in range(L)]))
    shared["ba2"] = f(np.stack([np.asarray(inp["gla_b_a2"][l]).T for l in range(L)]))
    shared["glag"] = f(np.stack([np.asarray(inp["gla_norm_g"][l]).reshape(64, 1) for l in range(L)]))
    shared["fftw"] = f(np.stack([np.asarray(inp["fft_w"][l]).transpose(1, 0, 2) for l in range(L)]))
    shared["convw"] = f(np.stack([np.asarray(inp["conv_w"][l]).reshape(3, 2, 128).transpose(2, 1, 0) for l in range(L)]))
    shared["convb"] = f(np.stack([np.asarray(inp["conv_b"][l]).reshape(2, 128).T for l in range(L)]))
    shared["poolw"] = f(np.stack([np.asarray(inp["pool_w"][l]).transpose(1, 0, 2) for l in range(L)]))
    shared["poolsc"] = f(np.stack([np.asarray(inp["pool_scale"][l]).reshape(4, 64).T for l in range(L)]))
    shared["wout"] = f(np.stack([_pm(np.asarray(inp["w_out"][l]), 8) for l in range(L)]))
    shared["wup"] = f(np.stack([_pm(np.asarray(inp["ffn_w_up"][l]), 8) for l in range(L)]))
    shared["fcw"] = f(np.stack([np.asarray(inp["ffn_conv_w"][l]).reshape(3, NJ, 128).transpose(2, 1, 0) for l in range(L)]))
    shared["fcb"] = f(np.stack([np.asarray(inp["ffn_conv_b"][l]).reshape(NJ, 128).T for l in range(L)]))
    shared["wdn"] = f(np.stack([_pm(np.asarray(inp["ffn_w_down"][l]), NJ) for l in range(L)]))
    shared["gfin"] = f(np.asarray(inp["final_norm_g"]).reshape(8, 128).T)
    shared["cst16"] = CST16
    shared["cst32"] = CST32
    maps = []
    x = np.asarray(inp["x"], dtype=np.float32)
    ctx = np.asarray(inp["ctx"], dtype=np.float32)
    c = np.asarray(inp["c"], dtype=np.float32)
    cc = np.asarray(inp["c_ctx"], dtype=np.float32)
    per_b = []
    for b in range(2):
        xT = np.ascontiguousarray(np.concatenate([x[b], ctx[b]], 0).T)
        cf = np.ascontiguousarray(np.stack([c[b].reshape(8, 128).T, cc.reshape(8, 128).T], -1))
        per_b.append((xT, cf))
    for core in range(8):
        m = dict(shared)
        m["xT"], m["c_fm"] = per_b[core // 4]
        m["wmod"], m["bmod"] = wmod_q[core % 4], bmod_q[core % 4]
        m["qoff"] = np.array([[(core % 4) * (SEQ // 4)]], dtype=np.int32)
        m["qoff2"] = np.array([[(core % 4) * 1024]], dtype=np.int32)
        maps.append(m)
    return maps


_NC = {}


def kernel(**inputs):
    if "nc" not in _NC:
        _NC["nc"] = build_program()
    nc = _NC["nc"]
    maps = _layout_inputs(inputs)
    res = run_bass_kernel_spmd(nc, maps, core_ids=list(range(8)))
    out = np.empty((2, SEQ, D), dtype=np.float32)
    q = SEQ // 4
    for core in range(8):
        b, r = core // 4, core % 4
        oT = res.results[core]["outT"]
        out[b, r * q:(r + 1) * q, :] = oT.T
    return out
```

```python
import contextlib
import os
import numpy as np
_LVL = int(os.environ.get('S1_LEVEL', '9'))
_NT1 = int(os.environ.get('S1_TILES', '99'))
import concourse.bass as bass
import concourse.mybir as mybir
from concourse.bass_utils import run_bass_kernel_spmd

F32 = mybir.dt.float32
BF16 = mybir.dt.bfloat16
AF = mybir.ActivationFunctionType
ALU = mybir.AluOpType
AX = mybir.AxisListType

D = 1024
SEQ = 8192
CTX = 256
NT = SEQ + CTX
DEPTH = 2
DFF = 2816
NJ = 22
EPS = 1e-6
COMPUTE = ("pe", "act", "dve", "pool")


class _Op:
    __slots__ = ("eng", "fn", "deps", "seq", "is_dma", "semkey", "cum", "sig", "inc")


class Sched:
    def __init__(self, nc):
        self.nc = nc
        self.ops = []
        self.q = {e: [] for e in ("pe", "act", "dve", "pool", "sp")}
        self.last_w = {}
        self.readers = {}
        self.dma_cnt = {}
        self.last_dma = {}
        self.same_eng_window = 1

    def _deps(self, reads, writes):
        deps = set()
        for r in reads:
            w = self.last_w.get(r)
            if w is not None:
                deps.add(w)
        for r in writes:
            w = self.last_w.get(r)
            if w is not None:
                deps.add(w)
            for x in self.readers.get(r, ()):
                deps.add(x)
        return deps

    def _commit(self, oid, reads, writes):
        o = self.ops[oid]
        for r in reads:
            lst = self.readers.setdefault(r, [])
            if not o.is_dma:
                lst[:] = [x for x in lst if self.ops[x].is_dma or self.ops[x].eng != o.eng]
            lst.append(oid)
        for r in writes:
            self.last_w[r] = oid
            self.readers[r] = []

    def _add(self, o, reads, writes):
        o.deps = self._deps(reads, writes)
        o.seq = len(self.q[o.eng])
        oid = len(self.ops)
        self.ops.append(o)
        self.q[o.eng].append(oid)
        self._commit(oid, reads, writes)
        return oid

    def op(self, eng, fn, reads=(), writes=()):
        o = _Op()
        o.eng = eng
        o.fn = fn
        o.is_dma = False
        return self._add(o, reads, writes)

    def dma(self, eng, out, in_, reads=(), writes=(), semkey=None):
        o = _Op()
        o.eng = eng
        o.is_dma = True
        o.fn = lambda e: e.dma_start(out=out, in_=in_)
        o.semkey = semkey if semkey is not None else writes[0]
        o.inc = 16
        self.dma_cnt[o.semkey] = self.dma_cnt.get(o.semkey, 0) + 16
        o.cum = self.dma_cnt[o.semkey]
        oid = self._add(o, reads, writes)
        self.last_dma[o.semkey] = oid
        return oid

    def cc(self, fn, semkey="cc"):
        o = _Op()
        o.eng = "pool"
        o.is_dma = True
        o.fn = fn
        o.semkey = semkey
        o.inc = 1
        self.dma_cnt[semkey] = self.dma_cnt.get(semkey, 0) + 1
        o.cum = self.dma_cnt[semkey]
        oid = self._add(o, [], [])
        self.last_dma[semkey] = oid
        return oid

    def barrier(self, skip_cc=False):
        lasts = set(q[-1] for q in self.q.values() if q) | set(v for k, v in self.last_dma.items() if not (skip_cc and k == "cc"))
        if skip_cc:
            lasts = set(x for x in lasts if not (self.ops[x].is_dma and self.ops[x].semkey == "cc"))
        for e in self.q:
            o = _Op()
            o.eng = e
            o.fn = None
            o.is_dma = False
            o.deps = set(lasts)
            o.seq = len(self.q[e])
            oid = len(self.ops)
            self.ops.append(o)
            self.q[e].append(oid)

    def emit(self):
        nc = self.nc
        with contextlib.ExitStack() as st:
            esem = {e: st.enter_context(nc.semaphore("s_" + e)) for e in COMPUTE}
            dsem = {}
            for i, k in enumerate(self.dma_cnt):
                dsem[k] = st.enter_context(nc.semaphore("d%d" % i))
            block = st.enter_context(nc.Block())
            ops = self.ops
            win = self.same_eng_window

            def skip(o, po):
                if po.eng != o.eng or o.is_dma or o.fn is None:
                    return False
                if po.eng == "pe":
                    return True
                return o.seq - po.seq > win

            need = set()
            for o in ops:
                for d in o.deps:
                    po = ops[d]
                    if po.is_dma or po.fn is None:
                        continue
                    if skip(o, po):
                        continue
                    need.add(d)
            cnt = {e: 0 for e in COMPUTE}
            for o in ops:
                o.sig = 0
            for e in COMPUTE:
                for oid in self.q[e]:
                    if oid in need:
                        cnt[e] += 1
                        ops[oid].sig = cnt[e]
            self.sig_counts = cnt

            def run(ename, eng):
                waited = {}
                for oid in self.q[ename]:
                    o = ops[oid]
                    for d in sorted(o.deps):
                        po = ops[d]
                        if po.fn is None:
                            continue
                        if po.is_dma:
                            key = ("d", po.semkey)
                            if waited.get(key, 0) >= po.cum:
                                continue
                            waited[key] = po.cum
                            eng.wait_ge(dsem[po.semkey], po.cum)
                        else:
                            if d not in need or skip(o, po):
                                continue
                            key = ("e", po.eng)
                            if waited.get(key, 0) >= po.sig:
                                continue
                            waited[key] = po.sig
                            eng.wait_ge(esem[po.eng], po.sig)
                    if o.fn is None:
                        continue
                    ins = o.fn(eng)
                    if o.is_dma:
                        ins.then_inc(dsem[o.semkey], o.inc)
                    elif o.sig:
                        ins.then_inc(esem[ename], 1)
                if ename == "sp":
                    for k, v in self.dma_cnt.items():
                        eng.wait_ge(dsem[k], v)

            @block.sync
            def _(e):
                run("sp", e)

            @block.tensor
            def _(e):
                run("pe", e)

            @block.scalar
            def _(e):
                run("act", e)

            @block.vector
            def _(e):
                run("dve", e)

            @block.gpsimd
            def _(e):
                run("pool", e)


def _pool_mat(w, n):
    t = np.arange(n)
    lo = np.clip(t - w // 2, 0, n - 1)
    hi = np.clip(t + w // 2 - 1, 0, n - 1)
    cnt = (hi - lo + 1).astype(np.float64)
    m = np.zeros((n, n))
    for tt in range(n):
        m[lo[tt]:hi[tt] + 1, tt] = 1.0 / cnt[tt]
    m -= np.eye(n)
    return m


C16 = {}
C32 = {}


def _consts():
    c16 = {}
    c32 = {}
    a = np.arange(128)
    ang = 2 * np.pi * np.outer(a, a) / 128.0
    C128, S128 = np.cos(ang), np.sin(ang)
    c16["CS1"] = np.concatenate([C128, S128], 1)
    c16["CS2"] = np.concatenate([-S128, C128], 1)
    n2 = np.arange(128) % 64
    tw = 2 * np.pi * np.outer(n2, np.arange(128)) / 8192.0
    c32["TT1"] = np.concatenate([np.cos(tw), np.sin(tw)], 1)
    c32["TT2"] = np.concatenate([np.sin(tw), np.cos(tw)], 1)
    b = np.arange(64)
    a64 = 2 * np.pi * np.outer(b, b) / 64.0
    C64, S64 = np.cos(a64), np.sin(a64)
    bdc = np.zeros((128, 128))
    bds = np.zeros((128, 128))
    for e in range(2):
        bdc[e * 64:(e + 1) * 64, e * 64:(e + 1) * 64] = C64
        bds[e * 64:(e + 1) * 64, e * 64:(e + 1) * 64] = -S64
    c16["BDC"] = bdc
    c16["BDS"] = bds
    cs = np.zeros((128, 128))
    cs[:64, :64] = C64
    cs[:64, 64:] = S64
    c32["C64S64"] = cs
    k = np.arange(256)
    a256 = 2 * np.pi * np.outer(k, k) / 256.0
    c16["C256"] = np.cos(a256).reshape(2, 128, 256).transpose(1, 0, 2).reshape(128, 512)
    c16["NS256"] = (-np.sin(a256)).reshape(2, 128, 256).transpose(1, 0, 2).reshape(128, 512)
    pm = np.zeros((128, 4, 128))
    pmc = np.zeros((128, 4, 2, 256))
    for g, w in enumerate((2, 4, 8, 16)):
        m64 = _pool_mat(w, 64)
        pm[:64, g, :64] = m64
        pm[64:, g, 64:] = m64
        mc = _pool_mat(w, 256)
        pmc[:, g, :, :] = mc.reshape(2, 128, 256).transpose(1, 0, 2)
    c16["PM"] = pm.reshape(128, 512)
    c16["PMC"] = pmc.reshape(128, 2048)
    c16["IDENT"] = np.eye(128)
    j = np.arange(128)
    mf = (j[:, None] <= j[None, :]).astype(np.float64)
    mb = (j[:, None] >= j[None, :]).astype(np.float64)
    c32["MASK"] = np.concatenate([mf, mb], 1)
    pp = np.arange(128)[:, None] // 32
    cc_ = np.arange(256)[None, :] // 64
    c32["MASKBD"] = (pp == cc_).astype(np.float64)
    rm = np.ones((128, 1024))
    rm[:, ::128] = 0.0
    c32["RMASK"] = rm
    return c16, c32


def _pack(d):
    offs = {}
    o = 0
    arrs = []
    for k, v in d.items():
        offs[k] = (o, v.shape[1])
        o += v.shape[1]
        arrs.append(v)
    return offs, np.ascontiguousarray(np.concatenate(arrs, 1).astype(np.float32))


_C16, _C32 = _consts()
OFF16, CST16 = _pack(_C16)
OFF32, CST32 = _pack(_C32)


class _SkipPhase(Exception):
    pass


class _Phase:
    def __init__(self, skip):
        self.skip = skip
        self.st = None

    def __enter__(self):
        if self.skip:
            return None
        self.st = contextlib.ExitStack()
        return self.st.__enter__()

    def __exit__(self, et, ev, tb):
        if self.st is not None:
            self.st.__exit__(et, ev, tb)
        return et is _SkipPhase


def build_program(n_layers=DEPTH, dbg=False, stop_after=None):
    nc = bass.Bass("TRN2", target_bir_lowering=False)

    def din(name, shape):
        return nc.dram_tensor(name, list(shape), F32, kind="ExternalInput")

    xT_in = din("xT", [D, NT])
    c_fm = din("c_fm", [128, 8, 2])
    n1g_d = din("n1g", [DEPTH, 128, 8])
    n2g_d = din("n2g", [DEPTH, 128, 8])
    wmod_d = din("wmod", [DEPTH, 3, 128, 8, 512])
    bmod_d = din("bmod", [DEPTH, 128, 12])
    win_d = din("win", [DEPTH, 128, 8, 2080])
    wa2_d = din("wa2", [DEPTH, 16, 2, 128])
    ba2_d = din("ba2", [DEPTH, 128, 2])
    glag_d = din("glag", [DEPTH, 64, 1])
    fftw_d = din("fftw", [DEPTH, 64, 4, 64])
    convw_d = din("convw", [DEPTH, 128, 2, 3])
    convb_d = din("convb", [DEPTH, 128, 2])
    poolw_d = din("poolw", [DEPTH, 64, 4, 64])
    poolsc_d = din("poolsc", [DEPTH, 64, 4])
    wout_d = din("wout", [DEPTH, 128, 8, 1024])
    wup_d = din("wup", [DEPTH, 128, 8, 2 * DFF])
    fcw_d = din("fcw", [DEPTH, 128, NJ, 3])
    fcb_d = din("fcb", [DEPTH, 128, NJ])
    wdn_d = din("wdn", [DEPTH, 128, NJ, 1024])
    gfin_d = din("gfin", [128, 8])
    cst16_d = din("cst16", list(CST16.shape))
    cst32_d = din("cst32", list(CST32.shape))
    qoff_d = nc.dram_tensor("qoff", [1, 1], mybir.dt.int32, kind="ExternalInput")
    qoff2_d = nc.dram_tensor("qoff2", [1, 1], mybir.dt.int32, kind="ExternalInput")
    QW = SEQ // 4
    outT = nc.dram_tensor("outT", [D, QW], F32, kind="ExternalOutput")

    kw = {"kind": "ExternalOutput"} if dbg else {}
    xs = nc.dram_tensor("xs", [D, NT], F32, **kw)
    QKf = nc.dram_tensor("QKf", [256, NT], BF16, **kw)
    AFB = nc.dram_tensor("AFB", [32, NT], BF16, **kw)
    Vtm = nc.dram_tensor("Vtm", [NT, 256], BF16, **kw)
    Ud = nc.dram_tensor("Ud", [256, NT], BF16, **kw)
    Gs = nc.dram_tensor("Gs", [256, NT], BF16, **kw)
    Ymix = nc.dram_tensor("Ymix", [D, NT], BF16, **kw)
    H2 = nc.dram_tensor("H2", [D, NT], BF16, **kw)
    xq = nc.dram_tensor("xq", [8, 128, QW], F32)
    QKfL = nc.dram_tensor("QKfL", [256, QW], BF16)
    AFBL = nc.dram_tensor("AFBL", [32, QW], BF16)
    VtmL = nc.dram_tensor("VtmL", [QW, 256], BF16)
    UdL = nc.dram_tensor("UdL", [256, QW], BF16)
    QKfG = nc.dram_tensor("QKfG", [4 * 256, QW], BF16)
    AFBG = nc.dram_tensor("AFBG", [4 * 32, QW], BF16)
    VtmG = nc.dram_tensor("VtmG", [SEQ, 256], BF16)
    UdG = nc.dram_tensor("UdG", [4 * 256, QW], BF16)
    YF = nc.dram_tensor("YF", [256, NT], BF16)
    ML = [nc.dram_tensor("ML%d" % i, [128, 24], F32) for i in range(DEPTH)]
    MG = [nc.dram_tensor("MG%d" % i, [4 * 128, 24], F32) for i in range(DEPTH)]
    ApL = nc.dram_tensor("ApL", [128, 2 * 16 * 64], BF16)
    ApG = nc.dram_tensor("ApG", [4 * 128, 2 * 16 * 64], BF16)
    EpL = nc.dram_tensor("EpL", [128, 32], F32)
    EpG = nc.dram_tensor("EpG", [4 * 128, 32], F32)

    def xq_tile_ap(t0, tw):
        return xq.ap()[:, :, t0:t0 + tw].rearrange("c p t -> p c t")
    QKp = nc.dram_tensor("QKp", [4, 128, NT], BF16)
    Sd = nc.dram_tensor("Sd", [128, 2, 67 * 64], BF16)
    PH = ["p0", "s1", "s2a", "s2b", "s3a", "s3b"]

    def skip_phase(l, name):
        if stop_after is None:
            return False
        sl, sn = stop_after
        return (l, PH.index(name)) > (sl, PH.index(sn))

    S = Sched(nc)
    ES = contextlib.ExitStack()
    DRAMK = ("QKf", "Gs", "Ud", "Ymc", "Ymp", "AFB", "Vtm", "Ymg", "Ymf", "Ymfc", "x", "out", "H2", "xq", "QKp", "Sd", "L2", "YF")

    _uid = [0]

    def sb(st, name, shape, dt):
        _uid[0] += 1
        return st.enter_context(nc.sbuf_tensor("sb%d_%s" % (_uid[0], name), list(shape), dt))

    def MM(out, lhsT, rhs, start, stop, R, W):
        S.op("pe", lambda e: e.matmul(out, lhsT=lhsT, rhs=rhs, start=start, stop=stop), R, W)

    def TR(out, in_, ident, R, W):
        S.op("pe", lambda e: e.transpose(out=out, in_=in_, identity=ident), R, W)

    def ACT(out, in_, func, R, W, bias=None, scale=None):
        kw = {}
        if bias is not None:
            kw["bias"] = bias
        if scale is not None:
            kw["scale"] = scale
        S.op("act", lambda e: e.activation(out=out, in_=in_, func=func, **kw), R, W)

    def TT(eng, out, in0, in1, op, R, W):
        S.op(eng, lambda e: e.tensor_tensor(out=out, in0=in0, in1=in1, op=op), R, W)

    def TS(eng, out, in0, s1, s2, op0, op1, R, W):
        if op1 is None:
            S.op(eng, lambda e: e.tensor_scalar(out=out, in0=in0, scalar1=s1, scalar2=None, op0=op0), R, W)
        else:
            S.op(eng, lambda e: e.tensor_scalar(out=out, in0=in0, scalar1=s1, scalar2=s2, op0=op0, op1=op1), R, W)

    def STT(out, in0, scalar, in1, op0, op1, R, W):
        S.op("dve", lambda e: e.scalar_tensor_tensor(out=out, in0=in0, scalar=scalar, in1=in1, op0=op0, op1=op1), R, W)

    def CP(eng, out, in_, R, W):
        if eng == "act":
            S.op(eng, lambda e: e.activation(out=out, in_=in_, func=AF.Copy), R, W)
        else:
            S.op(eng, lambda e: e.tensor_copy(out=out, in_=in_), R, W)

    def MEMSET(eng, ap, val, W):
        S.op(eng, lambda e: e.memset(ap, val), (), W)

    def DMA(q, out, in_, R, W):
        to_dram = not isinstance(W[0], str) and W[0][0] in DRAMK or (isinstance(W[0], str) and W[0] in DRAMK)
        if q == "pool" and out.dtype == in_.dtype:
            q = "sp"
        S.dma(q, out, in_, R, W, semkey="st" if to_dram else None)

    rs = ES.enter_context(nc.sync.register("rs_qoff"))
    S.op("sp", lambda e: e.reg_load(rs, qoff_d.ap()[0:1, 0:1]), (), ())
    _snap = {}

    def qv(e):
        if "v" not in _snap:
            _snap["v"] = e.snap(rs, min_val=0, max_val=SEQ - QW)
        return _snap["v"]

    rs2 = ES.enter_context(nc.sync.register("rs_qoff2"))
    S.op("sp", lambda e: e.reg_load(rs2, qoff2_d.ap()[0:1, 0:1]), (), ())

    def qv2(e):
        if "v2" not in _snap:
            _snap["v2"] = e.snap(rs2, min_val=0, max_val=3 * 1024)
        return _snap["v2"]

    def DYN(out, apfn, W):
        o_ = _Op()
        o_.eng = "sp"
        o_.is_dma = True
        o_.fn = lambda e: e.dma_start(out=out, in_=apfn(e))
        o_.semkey = W[0]
        o_.inc = 16
        S.dma_cnt[o_.semkey] = S.dma_cnt.get(o_.semkey, 0) + 16
        o_.cum = S.dma_cnt[o_.semkey]
        oid = S._add(o_, [], W)
        S.last_dma[o_.semkey] = oid

    def DMA_DYN(out, src_t, t0, tw, W, rows=None):
        r0, r1 = rows if rows is not None else (0, src_t.shape[0])
        DYN(out, lambda e: src_t.ap()[r0:r1, bass.ds(qv(e) + t0, tw)].rearrange("(c p) t -> p c t", p=128), W)

    PSB = [ES.enter_context(nc.psum_tensor("psb%d" % i, [128, 512], F32)) for i in range(7)]
    _pu = [0]

    def psum8(st_, dt, cols):
        _pu[0] += 1
        return st_.enter_context(nc.psum_tensor("ps8_%d" % _pu[0], [128, cols], dt))
    _bank = [0]

    def bank():
        i = _bank[0] % 7
        _bank[0] += 1
        return PSB[i], ("ps", i)

    def K16(name, rows=128, st=None):
        o, n = OFF16[name]
        t = sb(st, "k_" + name + "_%d" % len(S.ops), [rows, n], BF16)
        DMA("pool", t[:, :], cst16_d.ap()[0:rows, o:o + n], (), ["c16"])
        return t[:, :]

    def K32(name, rows=128, st=None, cols=None):
        o, n = OFF32[name]
        if cols is not None:
            n = cols
        t = sb(st, "k_" + name + "_%d" % len(S.ops), [rows, n], F32)
        DMA("sp", t[:, :], cst32_d.ap()[0:rows, o:o + n], (), ["c32"])
        return t[:, :]

    EPSB = sb(ES, "epsb", [128, 1], F32)
    MEMSET("pool", EPSB[:, :], EPS, ["epsb"])
    ones_bf = sb(ES, "ones_bf", [128, 128], BF16)
    MEMSET("pool", ones_bf[:, :], 1.0, ["ones"])
    svec = sb(ES, "svec", [128, 8, 2], F32)
    DMA("sp", svec[:, :, :], c_fm.ap(), (), ["svec"])
    ACT(svec[:, :, :], svec[:, :, :], AF.Silu, ["svec"], ["svec"])
    svec_bf = sb(ES, "svec_bf", [128, 8, 2], BF16)
    CP("dve", svec_bf[:, :, :], svec[:, :, :], ["svec"], ["svec"])
    gfin = sb(ES, "gfin", [128, 8], F32)
    DMA("sp", gfin[:, :], gfin_d.ap(), (), ["gfin"])
    modv_l = [sb(ES, "modv%d" % i, [128, 48, 2], F32) for i in range(DEPTH)]
    bmod_l = [sb(ES, "bmod%d" % i, [128, 12], F32) for i in range(DEPTH)]
    n1g_l = [sb(ES, "n1g%d" % i, [128, 8], F32) for i in range(DEPTH)]
    n2g_l = [sb(ES, "n2g%d" % i, [128, 8], F32) for i in range(DEPTH)]
    A1_l = [sb(ES, "A1_%d" % i, [128, 8, 2], F32) for i in range(DEPTH)]
    A2_l = [sb(ES, "A2_%d" % i, [128, 8, 2], F32) for i in range(DEPTH)]
    modv, A1, A2 = modv_l[0], A1_l[0], A2_l[0]

    class ModCalc:
        def __init__(self, lq, st_):
            self.lq = lq
            self.mk = ("mod", lq)
            self.wm = [sb(st_, "wm%d" % i, [128, 8, 512], BF16) for i in range(2)]
            self.wmf = sb(st_, "wmf", [128, 8, 512], F32)
            self.modq = sb(st_, "modq", [128, 12, 2], F32)
            self.pbm = psum8(st_, F32, 512)
            self.pkm = ("psmod", lq)
            DMA("sp", bmod_l[lq][:, :], bmod_d.ap()[lq], (), [("bmod", lq)])
            DMA("sp", n1g_l[lq][:, :], n1g_d.ap()[lq], (), [("ng", lq)])
            DMA("sp", n2g_l[lq][:, :], n2g_d.ap()[lq], (), [("ng", lq)])
            self.n = 0

        def piece(self):
            i = self.n
            self.n += 1
            w = self.wm[i % 2]
            wk = ("wm", i % 2)
            src_ = wmod_d.ap()[self.lq, i]
            if i % 2 == 0:
                DMA("pool", w[:, :, :], src_, (), [wk])
            else:
                DMA("sp", self.wmf[:, :, :], src_, (), ["wmf"])
                CP("dve", w[:, 0:4, :], self.wmf[:, 0:4, :], ["wmf"], [wk])
                CP("act", w[:, 4:8, :], self.wmf[:, 4:8, :], ["wmf"], [wk])
            for m in range(4):
                col = (i * 4 + m) * 2
                for kc in range(8):
                    MM(self.pbm[:, col:col + 2], w[:, kc, m * 128:(m + 1) * 128], svec_bf[:, kc, :], kc == 0, kc == 7,
                       [wk, "svec"], [self.pkm])

        def pieces(self, k):
            for _ in range(k):
                if self.n < 3:
                    self.piece()

        def finish(self):
            self.pieces(3)
            lq = self.lq
            TT("dve", self.modq[:, :, :], self.pbm[:, 0:24].rearrange("p (c i) -> p c i", i=2),
               bmod_l[lq][:, :].unsqueeze(2).broadcast_to([128, 12, 2]), ALU.add, [self.pkm, ("bmod", lq)], [("modq", lq)])
            DMA("sp", ML[lq].ap(), self.modq[:, :, :].rearrange("p c i -> p (c i)"), [("modq", lq)], [("L2", 8 + lq)])

    def mod_cc(lq):
        S.cc(lambda e: e.collective_compute("AllGather", ALU.bypass, replica_groups=[[0, 1, 2, 3], [4, 5, 6, 7]],
                                            ins=[ML[lq].ap().opt()], outs=[MG[lq].ap().opt()]))

    def mod_load(lq):
        mk = ("mod", lq)
        mv = modv_l[lq]
        for r_ in range(4):
            DMA("sp", mv[:, 12 * r_:12 * r_ + 12, :], MG[lq].ap()[128 * r_:128 * (r_ + 1), :].rearrange("p (c i) -> p c i", i=2), [], [mk])
        for (Aout, ng, kind) in ((A1_l[lq], n1g_l[lq], 1), (A2_l[lq], n2g_l[lq], 4)):
            TS("dve", Aout[:, :, :], mv[:, kind * 8:(kind + 1) * 8, :], 1.0, None, ALU.add, None, [mk], [mk])
            TT("dve", Aout[:, :, :], Aout[:, :, :], ng[:, :].unsqueeze(2).broadcast_to([128, 8, 2]), ALU.mult,
               [mk, ("ng", lq)], [mk])

    def phase0_mod(lq):
        with contextlib.ExitStack() as st_:
            mc = ModCalc(lq, st_)
            mc.finish()
            S.barrier()
            mod_cc(lq)

    wa2 = sb(ES, "wa2", [16, 2, 128], BF16)
    nba2 = sb(ES, "nba2", [128, 2], F32)
    glag = sb(ES, "glag", [64, 1], F32)
    fftw = sb(ES, "fftw", [64, 4, 64], F32)
    ABg = sb(ES, "ABg", [64, 4, 128], BF16)
    convw = sb(ES, "convw", [128, 2, 3], F32)
    convb = sb(ES, "convb", [128, 2], F32)
    poolw = sb(ES, "poolw", [64, 4, 64], BF16)
    poolsc = sb(ES, "poolsc", [64, 4], F32)
    fcw = sb(ES, "fcw", [128, NJ, 3], F32)
    fcb = sb(ES, "fcb", [128, NJ], F32)

    def modcol(kind, c, i):
        return modv[:, kind * 8 + c, i:i + 1]

    def tile_list(tw):
        tl = [(t * tw, tw, 0, 64) for t in range(SEQ // tw)]
        tl.append((SEQ, CTX, 1, CTX))
        return tl

    def xsrc(l):
        return xT_in if l == 0 else xs

    def x_tile_ap(t, t0, tw):
        return t.ap()[:, t0:t0 + tw].rearrange("(c p) t -> p c t", p=128)

    def norm_mod(xt, sq, tmp, rstd, hx, tw, Acol, Bcol, Rx, Wkeys, tag):
        ACT(sq[:, :, 0:tw], xt[:, :, 0:tw], AF.Square, Rx, [tag + "sq"])
        pb, pk = bank()
        for c in range(8):
            MM(pb[:, 0:tw], ones_bf[:, :], sq[:, c, 0:tw], c == 0, c == 7, [tag + "sq", "ones"], [pk])
        ACT(rstd[:, 0:tw], pb[:, 0:tw], AF.Ln, [pk], [tag + "rstd"], bias=EPSB[:, 0:1], scale=1.0 / D)
        ACT(rstd[:, 0:tw], rstd[:, 0:tw], AF.Exp, [tag + "rstd"], [tag + "rstd"], scale=-0.5)
        TT("dve", tmp[:, :, 0:tw], xt[:, :, 0:tw], rstd[:, 0:tw].unsqueeze(1).broadcast_to([128, 8, tw]), ALU.mult,
           Rx + [tag + "rstd"], [tag + "tmp"])
        for c in range(8):
            if Bcol is None:
                if c % 2 == 0:
                    TS("dve", hx[:, c, 0:tw], tmp[:, c, 0:tw], Acol(c), None, ALU.mult, None, [tag + "tmp"], Wkeys)
                else:
                    ACT(hx[:, c, 0:tw], tmp[:, c, 0:tw], AF.Copy, [tag + "tmp"], Wkeys, scale=Acol(c))
            else:
                ACT(hx[:, c, 0:tw], tmp[:, c, 0:tw], AF.Identity, [tag + "tmp", "params"], Wkeys, bias=Bcol(c), scale=Acol(c))

    cs_p = K32("C64S64", 64, ES)

    def load_params_early(lq):
        DMA("pool", wa2[:, :, :], wa2_d.ap()[lq], (), ["params"])
        DMA("sp", nba2[:, :], ba2_d.ap()[lq], (), ["params"])
        DMA("sp", glag[:, :], glag_d.ap()[lq], (), ["params"])
        DMA("sp", fftw[:, :, :], fftw_d.ap()[lq], (), ["params"])
        DMA("sp", convw[:, :, :], convw_d.ap()[lq], (), ["params"])
        DMA("sp", convb[:, :], convb_d.ap()[lq], (), ["params"])
        DMA("pool", poolw[:, :, :], poolw_d.ap()[lq], (), ["params"])
        DMA("sp", poolsc[:, :], poolsc_d.ap()[lq], (), ["params"])
        TS("dve", nba2[:, :], nba2[:, :], -1.0, None, ALU.mult, None, ["params"], ["params"])
        for g in range(4):
            pb, pk = bank()
            rhs = fftw[:, g, :].rearrange("m (dl dh) -> m dh dl", dl=2)
            MM(pb[0:64, 0:64], cs_p[:, 0:64], rhs, True, True, ["c32", "params"], [pk])
            MM(pb[0:64, 64:128], cs_p[:, 64:128], rhs, True, True, ["c32", "params"], [pk])
            CP("dve", ABg[:, g, :], pb[0:64, 0:128], [pk], ["params"])

    def load_params_ffn(lq):
        DMA("sp", fcw[:, :, :], fcw_d.ap()[lq], (), ["params"])
        DMA("sp", fcb[:, :], fcb_d.ap()[lq], (), ["params"])

    phase0_mod(0)
    for l in range(n_layers):
        last = l == DEPTH - 1
        modv, A1, A2 = modv_l[l], A1_l[l], A2_l[l]
        with _Phase(skip_phase(l, 'p0')) as st:
            if st is None:
                raise _SkipPhase()
            if l > 0:
                mod_load(l)
                load_params_ffn(l)
            else:
                load_params_early(0)
                load_params_ffn(0)
            if l == 0:
                S.barrier()
                mod_load(0)
            S.barrier()

        with _Phase(skip_phase(l, 's1')) as st:
            if st is None:
                raise _SkipPhase()
            win = sb(st, "win", [128, 8, 2080], BF16)
            if l + 1 < n_layers:
                DMA("pool", win[:, :, :], win_d.ap()[l], (), ["win"])
            else:
                winf = sb(st, "winf", [128, 4, 2080], F32)
                DMA("pool", win[:, 0:4, :], win_d.ap()[l][:, 0:4, :], (), ["win"])
                DMA("sp", winf[:, :, :], win_d.ap()[l][:, 4:8, :], (), ["winf"])
                CP("dve", win[:, 4:6, :], winf[:, 0:2, :], ["winf"], ["win"])
                CP("act", win[:, 6:8, :], winf[:, 2:4, :], ["winf"], ["win"])
            TW = 512
            xt2 = [sb(st, "xt%d" % i, [128, 8, TW], F32) for i in range(2)]
            sq = sb(st, "sq", [128, 8, TW], BF16)
            tmp = sb(st, "tmp", [128, 8, TW], F32)
            rstd = sb(st, "rstd", [128, TW], F32)
            hx2 = [sb(st, "hx%d" % i, [128, 8, TW], BF16) for i in range(2)]
            stg2 = [sb(st, "stg%d" % i, [128, 8, TW], BF16) for i in range(2)]
            stp2 = [sb(st, "stp%d" % i, [64, 4, TW], BF16) for i in range(2)]
            afb2 = [sb(st, "afb%d" % i, [32, TW], BF16) for i in range(2)]
            vst2 = [sb(st, "vst%d" % i, [128, 4, 256], BF16) for i in range(2)]
            ptm = sb(st, "ptm", [128, 4, 256], BF16)
            hs = sb(st, "hs", [128, TW], F32)
            mcv = sb(st, "mcv", [128, TW], F32)
            acc = sb(st, "acc", [128, TW], F32)
            pooled = sb(st, "pooled", [64, 4, TW], BF16)
            tiles = tile_list(TW)
            tiles = tiles[:QW // TW] + tiles[-1:]
            src = xsrc(l)
            PMk = K16("PM", 128, st)
            PMCk = K16("PMC", 128, st)

            def load_x(ti):
                t0, tw, ci, rl = tiles[ti]
                if ci == 0:
                    if l == 0:
                        DMA_DYN(xt2[ti % 2][:, :, 0:tw], src, t0, tw, [("xt", ti % 2)])
                    else:
                        DMA("sp", xt2[ti % 2][:, :, 0:tw], xq_tile_ap(t0, tw), [], [("xt", ti % 2)])
                    return
                DMA("sp", xt2[ti % 2][:, :, 0:tw], x_tile_ap(src, t0, tw), [("x", t0 // 256 + k) for k in range(tw // 256)],
                    [("xt", ti % 2)])

            if _NT1 < 99:
                tiles = tiles[:_NT1] + tiles[-1:]
            def do_norm(ti):
                t0_, tw_, ci_, rl_ = tiles[ti]
                p_ = ti % 2
                norm_mod(xt2[p_], sq, tmp, rstd, hx2[p_], tw_, lambda c: A1[:, c, ci_:ci_ + 1], lambda c: modcol(0, c, ci_),
                         [("xt", p_)], [("hx", p_)], "n1")

            mc_next = ModCalc(l + 1, st) if (l + 1 < n_layers) else None
            load_x(0)
            if len(tiles) > 1:
                load_x(1)
            do_norm(0)
            for ti, (t0, tw, ci, rl) in enumerate(tiles):
                p = ti % 2
                xt, hx, stg, stp, afb, vst = xt2[p], hx2[p], stg2[p], stp2[p], afb2[p], vst2[p]
                hk = ("hx", p)

                def proj(col0, m, pb, pk, off=0):
                    for kc in range(8):
                        MM(pb[0:m, off:off + tw], win[:, kc, col0:col0 + m], hx[:, kc, 0:tw], kc == 0, kc == 7, ["win", hk], [pk])

                if _LVL < 2:
                    continue
                for slot, col0 in ((0, 416), (1, 0)):
                    pb, pk = bank()
                    proj(col0, 128, pb, pk)
                    CP("act" if slot == 0 else "dve", stg[:, slot, 0:tw], pb[:, 0:tw], [pk], [("stg", p, slot)])
                pb, pk = bank()
                proj(384, 32, pb, pk)
                CP("act", afb[:, 0:tw], pb[0:32, 0:tw], [pk], [("afb", p)])
                for j in range(2):
                    pb, pk = bank()
                    proj(544 + 128 * j, 128, pb, pk)
                    ACT(stg[:, 2 + j, 0:tw], pb[:, 0:tw], AF.Silu, [pk], [("stg", p, 2 + j)])
                for j in range(2):
                    pb, pk = bank()
                    proj(800 + 128 * j, 128, pb, pk)
                    CP("dve", stg[:, 4 + j, 0:tw], pb[:, 0:tw], [pk], [("stg", p, 4 + j)])
                if ti + 1 < len(tiles):
                    do_norm(ti + 1)
                if ti + 2 < len(tiles):
                    load_x(ti + 2)
                for j in range(2 if _LVL >= 3 else 0):
                    pbh, pkh = bank()
                    proj(1056 + 128 * j, 128, pbh, pkh)
                    pbB, pkB = bank()
                    proj(1312 + 128 * j, 128, pbB, pkB)
                    pbC, pkC = bank()
                    proj(1568 + 128 * j, 128, pbC, pkC)
                    CP("act", hs[:, 0:tw], pbh[:, 0:tw], [pkh], ["hs"])
                    TT("dve", mcv[:, 0:tw], pbC[:, 0:tw], hs[:, 0:tw], ALU.mult, [pkC, "hs"], ["mcv"])
                    ACT(acc[:, 0:tw], mcv[:, 0:tw], AF.Identity, ["mcv", "params"], ["acc"], bias=convb[:, j:j + 1],
                        scale=convw[:, j, 1:2])
                    a3 = acc[:, 0:tw].rearrange("p (r l) -> p r l", l=rl)
                    m3 = mcv[:, 0:tw].rearrange("p (r l) -> p r l", l=rl)
                    STT(a3[:, :, 1:rl], m3[:, :, 0:rl - 1], convw[:, j, 0:1], a3[:, :, 1:rl], ALU.mult, ALU.add,
                        ["mcv", "acc", "params"], ["acc"])
                    STT(a3[:, :, 0:rl - 1], m3[:, :, 1:rl], convw[:, j, 2:3], a3[:, :, 0:rl - 1], ALU.mult, ALU.add,
                        ["mcv", "acc", "params"], ["acc"])
                    TT("dve", stg[:, 6 + j, 0:tw], acc[:, 0:tw], pbB[:, 0:tw], ALU.mult, ["acc", pkB], [("stg", p, 6 + j)])
                nsub = tw // 128
                for i in range(nsub if _LVL >= 4 else 0):
                    pb, pk = bank()
                    for kc in range(8):
                        b_ = win[:, kc, 128:384]
                        rhs2 = bass.AP(b_.tensor, b_.offset, [list(b_.ap[0]), [1696, 2], [1, 256]])
                        MM(pb[:, 0:512], hx[:, kc, i * 128:(i + 1) * 128], rhs2, kc == 0, kc == 7, ["win", hk], [pk])
                    CP("dve", vst[:, i, :], pb[:, 0:256], [pk], [("vst", p)])
                    CP("dve", ptm[:, i, :], pb[:, 256:512], [pk], ["ptm"])
                for g in range(4 if _LVL >= 5 else 0):
                    pb, pk = bank()
                    if ci == 0:
                        pm = PMk.rearrange("p (g t) -> p g t", g=4)
                        for i in range(nsub):
                            MM(pb[0:64, i * 128:(i + 1) * 128], ptm[:, i, 64 * g:64 * g + 64], pm[:, g, :], True, True,
                               ["ptm", "c16"], [pk])
                    else:
                        pmc = PMCk.rearrange("p (g k t) -> p g k t", g=4, k=2)
                        for kc in range(2):
                            MM(pb[0:64, 0:256], ptm[:, kc, 64 * g:64 * g + 64], pmc[:, g, kc, :], kc == 0, kc == 1,
                               ["ptm", "c16"], [pk])
                    CP("act", pooled[:, g, 0:tw], pb[0:64, 0:tw], [pk], [("pooled", g)])
                    pb2, pk2 = bank()
                    MM(pb2[0:64, 0:tw], poolw[:, g, :], pooled[:, g, 0:tw], True, True, [("pooled", g), "params"], [pk2])
                    TS("dve", stp[:, g, 0:tw], pb2[0:64, 0:tw], poolsc[:, g:g + 1], None, ALU.mult, None, [pk2, "params"],
                       [("stp", p)])
                if mc_next is not None:
                    mc_next.pieces(2 if ti < len(tiles) - 1 else 12)
                if _LVL < 6:
                    continue
                ts_ = slice(t0, t0 + tw)
                DMA("pool", QKf.ap()[:, ts_].rearrange("(s p) t -> p s t", p=128), stg[:, 0:2, 0:tw], [("stg", p, 0), ("stg", p, 1)], [("QKf", ti)])
                DMA("pool", Gs.ap()[:, ts_].rearrange("(s p) t -> p s t", p=128), stg[:, 2:4, 0:tw], [("stg", p, 2), ("stg", p, 3)], [("Gs", ti)])
                DMA("pool", (UdL if ci == 0 else Ud).ap()[:, ts_].rearrange("(s p) t -> p s t", p=128), stg[:, 4:6, 0:tw], [("stg", p, 4), ("stg", p, 5)],
                    [("Ud", ti)])
                DMA("pool", Ymix.ap()[512:768, ts_].rearrange("(s p) t -> p s t", p=128), stg[:, 6:8, 0:tw], [("stg", p, 6), ("stg", p, 7)],
                    [("Ymc", ti)])
                DMA("pool", Ymix.ap()[768:1024, ts_].rearrange("(g p) t -> p g t", p=64), stp[:, :, 0:tw], [("stp", p)],
                    [("Ymp", ti)])
                DMA("pool", AFB.ap()[:, ts_], afb[:, 0:tw], [("afb", p)], [("AFB", ti)])
                DMA("pool", Vtm.ap()[ts_, :].rearrange("(i p) c -> p i c", p=128), vst[:, 0:nsub, :], [("vst", p)], [("Vtm", ti)])
            if mc_next is not None:
                mc_next.finish()
            S.barrier()
            S.cc(lambda e: e.collective_compute("AllGather", ALU.bypass, replica_groups=[[0, 1, 2, 3], [4, 5, 6, 7]],
                                                ins=[UdL.ap().opt()], outs=[UdG.ap().opt()]))
            if mc_next is not None:
                mod_cc(l + 1)
            s1_done = True

        if not skip_phase(l, 's1'):
            S.barrier(skip_cc=True)
        NS1 = 17
        allk = lambda nm: [(nm, ti) for ti in range(NS1)]

        def run_fft():
          with contextlib.ExitStack() as st:
            usb1 = sb(st, "usb", [64, NT], BF16)
            usb2 = [usb1, usb1]
            Gsb2 = [sb(st, "Gsb%d" % i, [128, 128, 64], BF16) for i in range(2)]
            Zp2 = [sb(st, "Zp%d" % i, [128, 2, 32, 128], BF16) for i in range(2)]
            m1_2 = [sb(st, "m1_%d" % i, [128, 512], F32) for i in range(2)]
            m2_2 = [sb(st, "m2_%d" % i, [128, 512], F32) for i in range(2)]
            Ysb1 = sb(st, "Ysb", [128, 32, 128], BF16)
            Ysb2 = [Ysb1, Ysb1]
            Gc = sb(st, "Gc", [128, 2, 128], BF16)
            Yc = sb(st, "Yc", [64, 256], BF16)
            CS1, CS2 = K16("CS1", 128, st), K16("CS2", 128, st)
            TT1 = K32("TT1", 128, st).rearrange("p (a k) -> p a k", a=2)
            TT2 = K32("TT2", 128, st).rearrange("p (a k) -> p a k", a=2)
            BDC, BDS = K16("BDC", 128, st), K16("BDS", 128, st)
            C256 = K16("C256", 128, st).rearrange("p (k t) -> p k t", k=2)
            NS256 = K16("NS256", 128, st).rearrange("p (k t) -> p k t", k=2)
            scale = 1.0 / np.sqrt(float(SEQ * 64))
            def fkeys(g):
                return ("usb", 0), ("Gsb", g % 2), ("Zp", g % 2), ("Ysb", 0)

            def f_load(g):
                usb = usb2[g % 2]
                KU, KG, KZ, KY = fkeys(g)
                for r_ in range(4):
                    DMA("sp", usb[:, r_ * QW:(r_ + 1) * QW], UdG.ap()[r_ * 256 + 64 * g:r_ * 256 + 64 * g + 64, :], [], [KU])
                DMA("sp", usb[:, SEQ:NT], Ud.ap()[64 * g:64 * g + 64, SEQ:NT], [], [KU])

            def f_step1(g):
                usb, Gsb = usb2[g % 2], Gsb2[g % 2]
                KU, KG, KZ, KY = fkeys(g)
                for q4 in range(16):
                    pb, pk = bank()
                    for i in range(4):
                        n2_ = q4 * 4 + i
                        MM(pb[:, i * 128:(i + 1) * 128], usb[:, n2_:SEQ:64], ABg[:, g, :], True, True, [KU, "params"], [pk])
                    CP("act" if q4 % 2 == 0 else "dve", Gsb[:, :, q4 * 4:q4 * 4 + 4], pb[:, :].rearrange("p (n c) -> p c n", n=4),
                       [pk], [KG])
                if not last:
                    pb, pk = bank()
                    for j in range(2):
                        MM(pb[:, j * 128:(j + 1) * 128], usb[:, SEQ + 128 * j:SEQ + 128 * (j + 1)], ABg[:, g, :], True, True,
                           [KU, "params"], [pk])
                    CP("act", Gc[:, :, :], pb[:, 0:256].rearrange("p (j c) -> p j c", j=2), [pk], ["Gc"])
                    pb, pk = bank()
                    for j in range(2):
                        MM(pb[0:64, 0:256], Gc[:, j, 0:64], C256[:, j, :], j == 0, False, ["Gc", "c16"], [pk])
                    for j in range(2):
                        MM(pb[0:64, 0:256], Gc[:, j, 64:128], NS256[:, j, :], False, j == 1, ["Gc", "c16"], [pk])
                    ACT(Yc[:, :], pb[0:64, 0:256], AF.Copy, [pk], ["Yc"], scale=1.0 / 128.0)
                    for dl in range(2):
                        r0 = 256 + 64 * g + 32 * dl
                        DMA("pool", Ymix.ap()[r0:r0 + 32, SEQ:SEQ + CTX], Yc[dl:64:2, :], ["Yc"], [("Ymfc", g, dl)])

            def f_stepA(g):
                Gsb, Zp = Gsb2[g % 2], Zp2[g % 2]
                KU, KG, KZ, KY = fkeys(g)
                for dp in range(16):
                    pb, pk = bank()
                    for e_ in range(2):
                        dh = dp * 2 + e_
                        oc = pb[:, e_ * 256:(e_ + 1) * 256]
                        MM(oc, Gsb[:, 2 * dh:2 * dh + 2, :].rearrange("p a n -> p (a n)"), CS1, True, False, [KG, "c16"], [pk])
                        MM(oc, Gsb[:, 64 + 2 * dh:64 + 2 * dh + 2, :].rearrange("p a n -> p (a n)"), CS2, False, True,
                           [KG, "c16"], [pk])
                    p4 = pb[:, :].rearrange("p (e a k) -> p e a k", e=2, a=2)
                    m1, m2 = m1_2[dp % 2], m2_2[dp % 2]
                    k1_, k2_ = ("m1", dp % 2), ("m2", dp % 2)
                    m1v = m1[:, :].rearrange("p (e a k) -> p e a k", e=2, a=2)
                    m2v = m2[:, :].rearrange("p (e a k) -> p e a k", e=2, a=2)
                    TT("dve", m1v, p4, TT1.unsqueeze(1).broadcast_to([128, 2, 2, 128]), ALU.mult, [pk, "c32"], [k1_])
                    TT("dve", m2v, p4, TT2.unsqueeze(1).broadcast_to([128, 2, 2, 128]), ALU.mult, [pk, "c32"], [k2_])
                    TT("dve", Zp[:, 0, 2 * dp:2 * dp + 2, :], m1v[:, :, 0, :], m1v[:, :, 1, :], ALU.subtract, [k1_], [(KZ, 0)])
                    TT("pool", Zp[:, 1, 2 * dp:2 * dp + 2, :], m2v[:, :, 0, :], m2v[:, :, 1, :], ALU.add, [k2_], [(KZ, 1)])

            def f_stepC(g):
                Zp, Ysb = Zp2[g % 2], Ysb2[g % 2]
                KU, KG, KZ, KY = fkeys(g)
                for q8 in range(8):
                    pb, pk = bank()
                    MM(pb[:, :], BDC, Zp[:, 0, 4 * q8:4 * q8 + 4, :].rearrange("p a k -> p (a k)"), True, False, [(KZ, 0), "c16"], [pk])
                    MM(pb[:, :], BDS, Zp[:, 1, 4 * q8:4 * q8 + 4, :].rearrange("p a k -> p (a k)"), False, True, [(KZ, 1), "c16"], [pk])
                    ACT(Ysb[:, 4 * q8:4 * q8 + 4, :], pb[:, :].rearrange("p (a k) -> p a k", a=4), AF.Copy, [pk], [KY], scale=scale)
                for dl in range(2):
                    r0 = 64 * g + 32 * dl
                    dst = bass.AP(YF, r0 * NT, [[128, 64], [NT, 32], [1, 128]])
                    DMA("pool", dst, Ysb[dl * 64:(dl + 1) * 64, :, :], [KY], [("Ymf", g, dl)])

            f_load(0)
            for g in range(4):
                f_step1(g)
                if g + 1 < 4:
                    f_load(g + 1)
                if g > 0:
                    f_stepC(g - 1)
                f_stepA(g)
            f_stepC(3)
            S.barrier()


        with _Phase(skip_phase(l, 's2a')) as st:
            if st is None:
                raise _SkipPhase()
            NCH = 66
            GW = 512
            ident = K16("IDENT", 128, st)
            mask = K32("MASK", 128, st)
            rmask = K32("RMASK", 128, st, cols=GW)
            maskBD = K32("MASKBD", 128, st)
            groups = [(g * GW, GW, 4, [g]) for g in range(4)] + [(SEQ, CTX, 2, [16])]

            def idxS(d, c):
                if d == 0:
                    return c + 2 if c < 64 else (0 if c == 64 else 1)
                return c if c < 64 else (64 if c == 64 else 65)

            def idxA(d, c):
                if d == 0:
                    return c + 3 if c < 64 else (1 if c == 64 else 2)
                return c - 1 if (1 <= c < 64) else (66 if c == 0 else (63 if c == 64 else 64))

            with contextlib.ExitStack() as sa_outer:
                Sall = sb(sa_outer, "Sall", [128, 2, 67, 64], F32)
                Sbf = sb(sa_outer, "Sbf", [128, 2, 67, 64], BF16)
                Eall = sb(sa_outer, "Eall", [128, 2, NCH], F32)
                AL = sb(sa_outer, "AL", [128, 2, 16, 64], F32)
                EL = sb(sa_outer, "EL", [128, 2, 16], F32)
                sa = contextlib.ExitStack()
                qraw2 = [sb(sa, "qraw%d" % i, [128, 2, GW], BF16) for i in range(2)]
                afr2 = [sb(sa, "afr%d" % i, [16, 2, GW], BF16) for i in range(2)]
                vg2 = [sb(sa, "vga%d" % i, [128, 4, 256], BF16) for i in range(2)]
                qkst2 = [sb(sa, "qkst%d" % i, [128, 4, GW], BF16) for i in range(2)]
                ktm2 = [sb(sa, "ktm%d" % i, [128, 4, 2, 128], BF16) for i in range(2)]
                lfb = sb(sa, "lfb", [128, 2, GW], F32)
                cum = sb(sa, "cum", [128, 2, GW], F32)
                tot = sb(sa, "tot", [128, 2, 4], F32)
                eE = sb(sa, "eE", [128, 4, GW], F32)
                tmpA2 = [sb(sa, "tmpA%d" % i, [128, 256], F32) for i in range(2)]
                PST = psum8(sa, BF16, 1024)
                MEMSET("pool", Sall[:, 0, 0, :], 0.0, ["Sall0"])
                MEMSET("pool", Sall[:, 1, 65, :], 0.0, ["Sall1"])
                nA = 0
                def loadA(gi):
                    g0, gw, nchk, s1t = groups[gi]
                    p = gi % 2
                    qraw, afr, vg = qraw2[p], afr2[p], vg2[p]
                    qsrc, ksrc = QKf.ap()[0:128, g0:g0 + gw], QKf.ap()[128:256, g0:g0 + gw]
                    afs, abs_ = AFB.ap()[0:16, g0:g0 + gw], AFB.ap()[16:32, g0:g0 + gw]
                    vsrc = Vtm.ap()[g0:g0 + gw, :]
                    DMA("sp", qraw[:, 0, 0:gw], qsrc, [], [("qraw", p)])
                    DMA("sp", qraw[:, 1, 0:gw], ksrc, [], [("qraw", p)])
                    DMA("sp", afr[:, 0, 0:gw], afs, [], [("afr", p)])
                    DMA("sp", afr[:, 1, 0:gw], abs_, [], [("afr", p)])
                    DMA("sp", vg[:, 0:nchk, :], vsrc.rearrange("(c p) d -> p c d", p=128), [], [("vga", p)])

                loadA(0)
                for gi, (g0, gw, nchk, s1t) in enumerate(groups):
                    if gi + 1 < len(groups):
                        loadA(gi + 1)
                    p = gi % 2
                    qraw, afr, vg, qkst, ktm = qraw2[p], afr2[p], vg2[p], qkst2[p], ktm2[p]
                    own = g0 < SEQ
                    for d in range(2):
                        pb, pk = bank()
                        MM(pb[:, 0:gw], wa2[:, d, :], afr[:, d, 0:gw], True, True, [("afr", p), "params"], [pk])
                        ACT(lfb[:, d, 0:gw], pb[:, 0:gw], AF.Exp, [pk, "params"], [("lfb", d)], bias=nba2[:, d:d + 1], scale=-1.0)
                        ACT(lfb[:, d, 0:gw], lfb[:, d, 0:gw], AF.Ln, [("lfb", d)], [("lfb", d)], bias=1.0, scale=1.0)
                        S.op("dve", (lambda o_, d0, d1: (lambda e: e.tensor_tensor_scan(out=o_, data0=d0, data1=d1, initial=0.0,
                                                                                         op0=ALU.mult, op1=ALU.add)))(
                            cum[:, d, 0:gw], rmask[:, 0:gw], lfb[:, d, 0:gw]), [("lfb", d), "c32"], [("cum", d)])
                        S.op("dve", (lambda o_, i_: (lambda e: e.tensor_reduce(out=o_, in_=i_, axis=AX.X, op=ALU.add)))(
                            tot[:, d, 0:nchk], lfb[:, d, 0:gw].rearrange("p (c t) -> p c t", t=128)), [("lfb", d)], [("tot", d)])
                    TT("dve", cum[:, 1, 0:gw], lfb[:, 1, 0:gw], cum[:, 1, 0:gw], ALU.subtract, [("lfb", 1), ("cum", 1)], [("cum", 1)])
                    TT("dve", cum[:, 1, 0:gw].rearrange("p (c t) -> p c t", t=128), cum[:, 1, 0:gw].rearrange("p (c t) -> p c t", t=128),
                       tot[:, 1, 0:nchk].unsqueeze(2).broadcast_to([128, nchk, 128]), ALU.add, [("cum", 1), ("tot", 1)], [("cum", 1)])
                    c0 = g0 // 128
                    for d in range(2):
                        Edst = EL[:, d, c0:c0 + nchk] if own else Eall[:, d, c0:c0 + nchk]
                        ACT(Edst, tot[:, d, 0:nchk], AF.Exp, [("tot", d)], [("Eall", d)], scale=-1.0 / 16)
                        ACT(eE[:, 2 * d, 0:gw], cum[:, d, 0:gw], AF.Exp, [("cum", d)], [("eE", 2 * d)], scale=-1.0 / 16)
                        ACT(eE[:, 2 * d + 1, 0:gw], cum[:, d, 0:gw], AF.Exp, [("cum", d)], [("eE", 2 * d + 1)], scale=1.0 / 16)
                        STT(qkst[:, 2 * d, 0:gw], qraw[:, 0, 0:gw], 32.0 ** -0.5, eE[:, 2 * d, 0:gw], ALU.mult, ALU.mult,
                            [("qraw", p), ("eE", 2 * d)], [("qkst", p)])
                        TT("pool", qkst[:, 2 * d + 1, 0:gw], qraw[:, 1, 0:gw], eE[:, 2 * d + 1, 0:gw], ALU.mult,
                           [("qraw", p), ("eE", 2 * d + 1)], [("qkst", p)])
                    DMA("pool", QKp.ap()[:, :, g0:g0 + gw].rearrange("v p t -> p v t"), qkst[:, :, 0:gw], [("qkst", p)], [("QKp", gi)])
                    for cc in range(nchk):
                        for d in range(2):
                            o_ = (cc * 2 + d) * 128
                            TR(PST[:, o_:o_ + 128], qkst[:, 2 * d + 1, cc * 128:(cc + 1) * 128], ident, [("qkst", p), "c16"], ["pst"])
                    CP("act", ktm[:, 0:nchk, :, :], PST[:, 0:nchk * 256].rearrange("p (c d k) -> p c d k", d=2, k=128), ["pst"],
                       [("ktm", p)])
                    for cc in range(nchk):
                        c = c0 + cc
                        pb, pk = bank()
                        for d in range(2):
                            MM(pb[:, d * 256:(d + 1) * 256], ktm[:, cc, d, :], vg[:, cc, :], True, True, [("ktm", p), ("vga", p)], [pk])
                        for d in range(2):
                            tA = tmpA2[nA % 2]
                            tk_ = ("tmpA", nA % 2)
                            nA += 1
                            esc = EL[:, d, c:c + 1] if own else Eall[:, d, c:c + 1]
                            adst = AL[:, d, c, :] if own else Sall[:, d, idxA(d, c), :]
                            STT(tA[:, :], pb[:, d * 256:(d + 1) * 256], esc, maskBD, ALU.mult, ALU.mult,
                                [pk, ("Eall", d), "c32"], [tk_])
                            S.op("dve", (lambda o_, i_: (lambda e: e.tensor_reduce(out=o_, in_=i_, axis=AX.X, op=ALU.add)))(
                                adst, tA[:, :].rearrange("p (h v) -> p v h", h=4)), [tk_], ["Sall%d" % d])
                DMA("pool", ApL.ap(), AL[:, :, :, :].rearrange("p d c v -> p (d c v)"), ["Sall0", "Sall1"], [("L2", 5)])
                DMA("pool", EpL.ap(), EL[:, :, :].rearrange("p d c -> p (d c)"), [("Eall", 0), ("Eall", 1)], [("L2", 6)])
                S.barrier()
                sa.close()
                S.cc(lambda e: e.collective_compute("AllGather", ALU.bypass, replica_groups=[[0, 1, 2, 3], [4, 5, 6, 7]],
                                                    ins=[ApL.ap().opt()], outs=[ApG.ap().opt()]))
                S.cc(lambda e: e.collective_compute("AllGather", ALU.bypass, replica_groups=[[0, 1, 2, 3], [4, 5, 6, 7]],
                                                    ins=[EpL.ap().opt()], outs=[EpG.ap().opt()]))
                run_fft()
                for r_ in range(4):
                    rows = slice(r_ * 128, (r_ + 1) * 128)
                    for d in range(2):
                        DMA("sp", Eall[:, d, 16 * r_:16 * r_ + 16], EpG.ap()[rows, d * 16:(d + 1) * 16], [], [("Eall", d)])
                        src3 = ApG.ap()[rows, d * 1024:(d + 1) * 1024].rearrange("p (c v) -> p c v", v=64)
                        if d == 0:
                            DMA("sp", Sbf[:, 0, 16 * r_ + 3:16 * r_ + 19, :], src3, [], [("Sbf", 0)])
                        elif r_ == 0:
                            DMA("sp", Sbf[:, 1, 66:67, :], src3[:, 0:1, :], [], [("Sbf", 1)])
                            DMA("sp", Sbf[:, 1, 0:15, :], src3[:, 1:16, :], [], [("Sbf", 1)])
                        else:
                            DMA("sp", Sbf[:, 1, 16 * r_ - 1:16 * r_ + 15, :], src3, [], [("Sbf", 1)])
                orderF = [64, 65] + list(range(0, 63))
                orderB = [65, 64] + list(range(63, 0, -1))
                for i in range(len(orderF)):
                    for d, c in ((0, orderF[i]), (1, orderB[i])):
                        asrc = Sall[:, d, idxA(d, c), :] if c >= 64 else Sbf[:, d, idxA(d, c), :]
                        STT(Sall[:, d, idxA(d, c), :], Sall[:, d, idxS(d, c), :], Eall[:, d, c:c + 1], asrc,
                            ALU.mult, ALU.add, ["Sall%d" % d, ("Eall", d), ("Sbf", d)], ["Sall%d" % d])
                CP("act", Sbf[:, 0, :, :], Sall[:, 0, :, :], ["Sall0"], [("Sbf", 0)])
                CP("dve", Sbf[:, 1, :, :], Sall[:, 1, :, :], ["Sall1"], [("Sbf", 1)])
                DMA("pool", Sd.ap(), Sbf[:, :, :, :].rearrange("p d s v -> p d (s v)"), [("Sbf", 0), ("Sbf", 1)], [("Sd", 0)])
                S.barrier()
            with contextlib.ExitStack() as sb_:
                qhO = sb(sb_, "qhO", [32, 4, 4, QW], BF16)
                shO = sb(sb_, "shO", [32, 2, 4, 16 * 64], BF16)
                vbO = sb(sb_, "vbO", [128, 16, 256], BF16)
                gsO = sb(sb_, "gsO", [64, 4, QW], BF16)
                qhC = sb(sb_, "qhC", [32, 4, 4, CTX], BF16)
                shC = sb(sb_, "shC", [32, 2, 4, 2 * 64], BF16)
                vbC = sb(sb_, "vbC", [128, 2, 256], BF16)
                gsC = sb(sb_, "gsC", [64, 4, CTX], BF16)
                for v_ in range(4):
                    DMA("sp", qhO[:, v_, :, :], QKp.ap()[v_, :, 0:QW].rearrange("(h k) t -> k h t", k=32), [], ["qhO"])
                for d in range(2):
                    koff = 2 * 64 if d == 0 else 0
                    DYN(shO[:, d, :, :], (lambda d_, k_: (lambda e: Sd.ap()[:, d_, bass.ds(qv2(e) + k_, 16 * 64)].rearrange("(h k) w -> k h w", k=32)))(d, koff),
                        ["shO"])
                DMA("sp", vbO[:, :, :], Vtm.ap()[0:QW, :].rearrange("(c p) d -> p c d", p=128), [], ["vbO"])
                DMA("sp", gsO[:, :, :], Gs.ap()[:, 0:QW].rearrange("(h p) t -> p h t", p=64), [], ["gsO"])
                if not last:
                    for v_ in range(4):
                        DMA("sp", qhC[:, v_, :, :], QKp.ap()[v_, :, SEQ:NT].rearrange("(h k) t -> k h t", k=32), [], ["qhC"])
                    for d in range(2):
                        s_ = idxS(d, 64)
                        DMA("sp", shC[:, d, :, :], Sd.ap()[:, d, s_ * 64:(s_ + 2) * 64].rearrange("(h k) w -> k h w", k=32), [], ["shC"])
                    DMA("sp", vbC[:, :, :], Vtm.ap()[SEQ:NT, :].rearrange("(c p) d -> p c d", p=128), [], ["vbC"])
                    DMA("sp", gsC[:, :, :], Gs.ap()[:, SEQ:NT].rearrange("(h p) t -> p h t", p=64), [], ["gsC"])
                bufs = {"O": (qhO, shO, vbO, gsO, "qhO", "shO", "vbO", "gsO"), "C": (qhC, shC, vbC, gsC, "qhC", "shC", "vbC", "gsC")}
                items = [("O", cl, cl * 128) for cl in range(16)] + ([] if last else [("C", 0, SEQ), ("C", 1, SEQ + 128)])
                NP = 3
                AT1 = [sb(sb_, "AT1p_%d" % i, [128, 4, 128], F32) for i in range(NP)]
                AT2 = [sb(sb_, "AT2p_%d" % i, [128, 4, 128], F32) for i in range(NP)]
                ATb = [sb(sb_, "ATbp_%d" % i, [128, 4, 128], BF16) for i in range(NP)]
                osb2 = [sb(sb_, "osbp%d" % i, [64, 512], F32) for i in range(NP)]
                osq2 = [sb(sb_, "osqp%d" % i, [64, 512], BF16) for i in range(NP)]
                orst2 = [sb(sb_, "orstp%d" % i, [64, 512], F32) for i in range(NP)]
                yg2 = [sb(sb_, "ygp%d" % i, [64, 4, 128], BF16) for i in range(NP)]
                pbo_of = {}

                def st1(i):
                    bk, cc, t0 = items[i]
                    qh, sh, vg, gsb, kq, ks, kv, kg = bufs[bk]
                    pa = i % NP
                    cs_ = slice(cc * 128, (cc + 1) * 128)
                    pbF, pkF = bank()
                    for h in range(4):
                        MM(pbF[:, h * 128:(h + 1) * 128], qh[:, 1, h, cs_], qh[:, 0, h, cs_], True, True, [kq], [pkF])
                    pbB, pkB = bank()
                    for h in range(4):
                        MM(pbB[:, h * 128:(h + 1) * 128], qh[:, 3, h, cs_], qh[:, 2, h, cs_], True, True, [kq], [pkB])
                    TT("dve", AT1[pa][:, :, :], pbF[:, :].rearrange("p (h i) -> p h i", h=4),
                       mask[:, 0:128].unsqueeze(1).broadcast_to([128, 4, 128]), ALU.mult, [pkF, "c32"], [("AT1", pa)])
                    TT("dve", AT2[pa][:, :, :], pbB[:, :].rearrange("p (h i) -> p h i", h=4),
                       mask[:, 128:256].unsqueeze(1).broadcast_to([128, 4, 128]), ALU.mult, [pkB, "c32"], [("AT2", pa)])
                    TT("dve", ATb[pa][:, :, :], AT1[pa][:, :, :], AT2[pa][:, :, :], ALU.add, [("AT1", pa), ("AT2", pa)], [("ATb", pa)])

                def st2(i):
                    bk, cc, t0 = items[i]
                    qh, sh, vg, gsb, kq, ks, kv, kg = bufs[bk]
                    pa = i % NP
                    cs_ = slice(cc * 128, (cc + 1) * 128)
                    pbo, pko = bank()
                    pbo_of[i] = (pbo, pko)
                    for h in range(4):
                        oc = pbo[0:64, h * 128:(h + 1) * 128]
                        MM(oc, vg[:, cc, 64 * h:64 * h + 64], ATb[pa][:, h, :], True, False, [kv, ("ATb", pa)], [pko])
                        MM(oc, sh[:, 0, h, cc * 64:(cc + 1) * 64], qh[:, 0, h, cs_], False, False, [ks, kq], [pko])
                        MM(oc, sh[:, 1, h, cc * 64:(cc + 1) * 64], qh[:, 2, h, cs_], False, True, [ks, kq], [pko])
                    CP("act", osb2[pa][:, :], pbo[0:64, :], [pko], [("osb", pa)])
                    ACT(osq2[pa][:, :], pbo[0:64, :], AF.Square, [pko], [("osq", pa)])

                def st3(i):
                    bk, cc, t0 = items[i]
                    qh, sh, vg, gsb, kq, ks, kv, kg = bufs[bk]
                    pa = i % NP
                    cs_ = slice(cc * 128, (cc + 1) * 128)
                    osb, osq, orst, yg = osb2[pa], osq2[pa], orst2[pa], yg2[pa]
                    pb, pk = bank()
                    MM(pb[0:64, :], ones_bf[0:64, 0:64], osq[:, :], True, True, [("osq", pa), "ones"], [pk])
                    ACT(orst[:, :], pb[0:64, :], AF.Ln, [pk], [("orst", pa)], bias=EPSB[0:64, 0:1], scale=1.0 / 64)
                    ACT(orst[:, :], orst[:, :], AF.Exp, [("orst", pa)], [("orst", pa)], scale=-0.5)
                    TT("dve", osb[:, :], osb[:, :], orst[:, :], ALU.mult, [("osb", pa), ("orst", pa)], [("osb", pa)])
                    STT(yg[:, :, :], osb[:, :].rearrange("p (h i) -> p h i", h=4), glag[:, 0:1], gsb[:, :, cs_], ALU.mult, ALU.mult,
                        [("osb", pa), kg, "params"], [("yg", pa)])
                    DMA("pool", Ymix.ap()[0:256, t0:t0 + 128].rearrange("(h p) t -> p h t", p=64), yg[:, :, :], [("yg", pa)],
                        [("Ymg", i)])

                n_it = len(items)
                for i in range(n_it + 2):
                    if i < n_it:
                        st1(i)
                    if 0 <= i - 1 < n_it:
                        st2(i - 1)
                    if 0 <= i - 2 < n_it:
                        st3(i - 2)
            S.barrier()

        if stop_after is None and l + 1 < n_layers:
            load_params_early(l + 1)
        pref_w = stop_after is None
        if pref_w:
            sw = contextlib.ExitStack()
            wup_o = sb(sw, "wup_o", [128, 8, 2 * DFF], BF16)
        with _Phase(skip_phase(l, 's3a')) as st:
            if st is None:
                raise _SkipPhase()
            wout = sb(st, "wout", [128, 8, 1024], BF16)
            DMA("pool", wout[:, :, :], wout_d.ap()[l], (), ["wout"])
            if pref_w:
                DMA("pool", wup_o[:, :, :], wup_d.ap()[l], (), ["wup"])
            TW = 512
            xt2 = [sb(st, "xu%d" % i, [128, 8, TW], F32) for i in range(2)]
            ym2 = [sb(st, "ym%d" % i, [128, 8, TW], BF16) for i in range(2)]
            sq = sb(st, "sq3", [128, 8, TW], BF16)
            tmp = sb(st, "tmp3", [128, 8, TW], F32)
            rstd = sb(st, "rstd3", [128, TW], F32)
            h22 = [sb(st, "h2_%d" % i, [128, 8, TW], BF16) for i in range(2)]
            tiles = tile_list(TW)
            tiles = tiles[:QW // TW] + ([] if last else tiles[-1:])
            src = xsrc(l)

            def load3(ti):
                t0, tw, ci, rl = tiles[ti]
                if ci == 0:
                    if l == 0:
                        DMA_DYN(xt2[ti % 2][:, :, 0:tw], src, t0, tw, [("xu", ti % 2)])
                    else:
                        DMA("sp", xt2[ti % 2][:, :, 0:tw], xq_tile_ap(t0, tw), [], [("xu", ti % 2)])
                    ymt = ym2[ti % 2]
                    DMA("sp", ymt[:, 0:2, 0:tw], Ymix.ap()[0:256, t0:t0 + tw].rearrange("(c p) t -> p c t", p=128), [], [("ym", ti % 2)])
                    DMA("sp", ymt[:, 4:8, 0:tw], Ymix.ap()[512:1024, t0:t0 + tw].rearrange("(c p) t -> p c t", p=128), [], [("ym", ti % 2)])
                    DMA_DYN(ymt[:, 2:4, 0:tw], YF, t0, tw, [("ym", ti % 2)])
                    return
                DMA("sp", xt2[ti % 2][:, :, 0:tw], x_tile_ap(src, t0, tw), [("x", t0 // 256 + k) for k in range(tw // 256)],
                    [("xu", ti % 2)])
                DMA("sp", ym2[ti % 2][:, :, 0:tw], Ymix.ap()[:, t0:t0 + tw].rearrange("(c p) t -> p c t", p=128), [], [("ym", ti % 2)])

            load3(0)
            for ti, (t0, tw, ci, rl) in enumerate(tiles):
                if ti + 1 < len(tiles):
                    load3(ti + 1)
                p = ti % 2
                xt, ym, h2 = xt2[p], ym2[p], h22[p]
                xk = ("xu", p)
                for m in range(8):
                    pb, pk = bank()
                    for kc in range(8):
                        MM(pb[:, 0:tw], wout[:, kc, m * 128:(m + 1) * 128], ym[:, kc, 0:tw], kc == 0, kc == 7, ["wout", ("ym", p)], [pk])
                    STT(xt[:, m, 0:tw], pb[:, 0:tw], modcol(2, m, ci), xt[:, m, 0:tw], ALU.mult, ALU.add, [pk, xk, "params"], [xk])
                norm_mod(xt, sq, tmp, rstd, h2, tw, lambda c: A2[:, c, ci:ci + 1], lambda c: modcol(3, c, ci), [xk], [("h2", p)], "n2")
                if ci == 0:
                    DMA("pool", xq_tile_ap(t0, tw), xt[:, :, 0:tw], [xk], [("xq", t0 // 256 + k) for k in range(tw // 256)])
                else:
                    DMA("pool", x_tile_ap(xs, t0, tw), xt[:, :, 0:tw], [xk], [("x", t0 // 256 + k) for k in range(tw // 256)])
                DMA("pool", H2.ap()[:, t0:t0 + tw].rearrange("(c p) t -> p c t", p=128), h2[:, :, 0:tw], [("h2", p)], [("H2", ti)])
            S.barrier()

        with _Phase(skip_phase(l, 's3b')) as st:
            if st is None:
                raise _SkipPhase()
            if pref_w:
                wup = wup_o
            else:
                wup = sb(st, "wup", [128, 8, 2 * DFF], BF16)
                DMA("pool", wup[:, :, :], wup_d.ap()[l], (), ["wup"])
            wdn = sb(st, "wdn", [128, NJ, 1024], BF16)
            DMA("pool", wdn[:, :, :], wdn_d.ap()[l], (), ["wdn"])
            TW = 512
            xt = sb(st, "xv", [128, 8, TW], F32)
            h22 = [sb(st, "hv%d" % i, [128, 8, TW], BF16) for i in range(2)]
            hid = sb(st, "hid", [128, NJ, TW], BF16)
            tcv2 = [sb(st, "tcv%d" % i, [128, TW], F32) for i in range(2)]
            scv2 = [sb(st, "scv%d" % i, [128, TW], F32) for i in range(2)]
            tiles = tile_list(TW)
            tiles = tiles[:QW // TW] + ([] if last else tiles[-1:])
            xk = "xv"

            def xio_ap(t0, tw, ci):
                return xq_tile_ap(t0, tw) if ci == 0 else x_tile_ap(xs, t0, tw)

            def load4(ti):
                t0, tw, ci, rl = tiles[ti]
                DMA("sp", h22[ti % 2][:, :, 0:tw], H2.ap()[:, t0:t0 + tw].rearrange("(c p) t -> p c t", p=128), [], [("hv", ti % 2)])

            load4(0)
            for ti, (t0, tw, ci, rl) in enumerate(tiles):
                if ti + 1 < len(tiles):
                    load4(ti + 1)
                p = ti % 2
                h2 = h22[p]
                hk = ("hv", p)
                DMA("sp", xt[:, :, 0:tw], xio_ap(t0, tw, ci), [], [xk])
                for j in range(NJ):
                    pb, pk = bank()
                    for kc in range(8):
                        MM(pb[:, 0:tw], wup[:, kc, j * 128:(j + 1) * 128], h2[:, kc, 0:tw], kc == 0, kc == 7, ["wup", hk], [pk])
                    pbu, pku = bank()
                    for kc in range(8):
                        MM(pbu[:, 0:tw], wup[:, kc, DFF + j * 128:DFF + (j + 1) * 128], h2[:, kc, 0:tw], kc == 0, kc == 7,
                           ["wup", hk], [pku])
                    tcv, scv = tcv2[j % 2], scv2[j % 2]
                    tk, sk = ("tcv", j % 2), ("scv", j % 2)
                    ACT(tcv[:, 0:tw], pb[:, 0:tw], AF.Identity, [pk, "params"], [tk], bias=fcb[:, j:j + 1], scale=fcw[:, j, 1:2])
                    a3 = tcv[:, 0:tw].rearrange("p (r l) -> p r l", l=rl)
                    p3 = pb[:, 0:tw].rearrange("p (r l) -> p r l", l=rl)
                    STT(a3[:, :, 1:rl], p3[:, :, 0:rl - 1], fcw[:, j, 0:1], a3[:, :, 1:rl], ALU.mult, ALU.add, [pk, tk, "params"], [tk])
                    STT(a3[:, :, 0:rl - 1], p3[:, :, 1:rl], fcw[:, j, 2:3], a3[:, :, 0:rl - 1], ALU.mult, ALU.add, [pk, tk, "params"], [tk])
                    ACT(scv[:, 0:tw], tcv[:, 0:tw], AF.Silu, [tk], [sk])
                    TT("dve", hid[:, j, 0:tw], scv[:, 0:tw], pbu[:, 0:tw], ALU.mult, [sk, pku], [("hid", j)])
                for m in range(8):
                    pb, pk = bank()
                    for j in range(NJ):
                        MM(pb[:, 0:tw], wdn[:, j, m * 128:(m + 1) * 128], hid[:, j, 0:tw], j == 0, j == NJ - 1, ["wdn", ("hid", j)], [pk])
                    STT(xt[:, m, 0:tw], pb[:, 0:tw], modcol(5, m, ci), xt[:, m, 0:tw], ALU.mult, ALU.add, [pk, xk, "params"], [xk])
                DMA("pool", xio_ap(t0, tw, ci), xt[:, :, 0:tw], [xk], [("xq" if ci == 0 else "x", t0 // 256)])
            S.barrier()

        if pref_w:
            sw.close()

    with contextlib.ExitStack() as st:
        TW = 512
        xt2 = [sb(st, "xf%d" % i, [128, 8, TW], F32) for i in range(2)]
        sq = sb(st, "sqf", [128, 8, TW], BF16)
        tmp = sb(st, "tmpf", [128, 8, TW], F32)
        rstd = sb(st, "rstdf", [128, TW], F32)
        ot2 = [sb(st, "of%d" % i, [128, 8, TW], F32) for i in range(2)]
        full = n_layers == DEPTH and stop_after is None
        src = xq if full else (xs if (n_layers > 0 and (stop_after[0], PH.index(stop_after[1])) >= (0, 4)) else xT_in)
        nt = QW // TW

        def loadf(ti):
            DMA("sp", xt2[ti % 2][:, :, :], xq_tile_ap(ti * TW, TW) if full else x_tile_ap(src, ti * TW, TW), [], [("xf", ti % 2)])

        loadf(0)
        for ti in range(nt):
            if ti + 1 < nt:
                loadf(ti + 1)
            p = ti % 2
            norm_mod(xt2[p], sq, tmp, rstd, ot2[p], TW, lambda c: gfin[:, c:c + 1], None, [("xf", p)], [("of", p)], "nf")
            DMA("sp", x_tile_ap(outT, ti * TW, TW), ot2[p][:, :, :], [("of", p)], [("out", ti)])
    S.emit()
    ES.close()
    return nc


def _pm(a, kc):
    n = a.shape[-1]
    return np.ascontiguousarray(a.reshape(kc, 128, n).transpose(1, 0, 2))


def _layout_inputs(inp):
    f = lambda a: np.ascontiguousarray(np.asarray(a, dtype=np.float32))
    L = DEPTH
    shared = {}
    shared["n1g"] = f(np.stack([inp["norm1_g"][l].reshape(8, 128).T for l in range(L)]))
    shared["n2g"] = f(np.stack([inp["norm2_g"][l].reshape(8, 128).T for l in range(L)]))
    wmod_q, bmod_q = [], []
    for r in range(4):
        cs_ = slice(r * 1536, (r + 1) * 1536)
        wmod_q.append(f(np.stack([np.asarray(inp["w_mod"][l])[:, cs_].reshape(8, 128, 3, 512).transpose(2, 1, 0, 3) for l in range(L)])))
        bmod_q.append(f(np.stack([np.asarray(inp["b_mod"][l])[cs_].reshape(12, 128).T for l in range(L)])))
    shared["win"] = f(np.stack([_pm(np.asarray(inp["w_in"][l]), 8) for l in range(L)]))
    shared["wa2"] = f(np.stack([np.asarray(inp["gla_w_a2"][l]).transpose(1, 0, 2) for l in range(L)]))
    shared["ba2"] = f(np.stack([np.asarray(inp["gla_b_a2"][l]).T for l in range(L)]))
    shared["glag"] = f(np.stack([np.asarray(inp["gla_norm_g"][l]).reshape(64, 1) for l in range(L)]))
    shared["fftw"] = f(np.stack([np.asarray(inp["fft_w"][l]).transpose(1, 0, 2) for l in range(L)]))
    shared["convw"] = f(np.stack([np.asarray(inp["conv_w"][l]).reshape(3, 2, 128).transpose(2, 1, 0) for l in range(L)]))
    shared["convb"] = f(np.stack([np.asarray(inp["conv_b"][l]).reshape(2, 128).T for l in range(L)]))
    shared["poolw"] = f(np.stack([np.asarray(inp["pool_w"][l]).transpose(1, 0, 2) for l in range(L)]))
    shared["poolsc"] = f(np.stack([np.asarray(inp["pool_scale"][l]).reshape(4, 64).T for l in range(L)]))
    shared["wout"] = f(np.stack([_pm(np.asarray(inp["w_out"][l]), 8) for l in range(L)]))
    shared["wup"] = f(np.stack([_pm(np.asarray(inp["ffn_w_up"][l]), 8) for l in range(L)]))
    shared["fcw"] = f(np.stack([np.asarray(inp["ffn_conv_w"][l]).reshape(3, NJ, 128).transpose(2, 1, 0) for l in range(L)]))
    shared["fcb"] = f(np.stack([np.asarray(inp["ffn_conv_b"][l]).reshape(NJ, 128).T for l in range(L)]))
    shared["wdn"] = f(np.stack([_pm(np.asarray(inp["ffn_w_down"][l]), NJ) for l in range(L)]))
    shared["gfin"] = f(np.asarray(inp["final_norm_g"]).reshape(8, 128).T)
    shared["cst16"] = CST16
    shared["cst32"] = CST32
    maps = []
    x = np.asarray(inp["x"], dtype=np.float32)
    ctx = np.asarray(inp["ctx"], dtype=np.float32)
    c = np.asarray(inp["c"], dtype=np.float32)
    cc = np.asarray(inp["c_ctx"], dtype=np.float32)
    per_b = []
    for b in range(2):
        xT = np.ascontiguousarray(np.concatenate([x[b], ctx[b]], 0).T)
        cf = np.ascontiguousarray(np.stack([c[b].reshape(8, 128).T, cc.reshape(8, 128).T], -1))
        per_b.append((xT, cf))
    for core in range(8):
        m = dict(shared)
        m["xT"], m["c_fm"] = per_b[core // 4]
        m["wmod"], m["bmod"] = wmod_q[core % 4], bmod_q[core % 4]
        m["qoff"] = np.array([[(core % 4) * (SEQ // 4)]], dtype=np.int32)
        m["qoff2"] = np.array([[(core % 4) * 1024]], dtype=np.int32)
        maps.append(m)
    return maps


_NC = {}


def kernel(**inputs):
    if "nc" not in _NC:
        _NC["nc"] = build_program()
    nc = _NC["nc"]
    maps = _layout_inputs(inputs)
    res = run_bass_kernel_spmd(nc, maps, core_ids=list(range(8)))
    out = np.empty((2, SEQ, D), dtype=np.float32)
    q = SEQ // 4
    for core in range(8):
        b, r = core // 4, core % 4
        oT = res.results[core]["outT"]
        out[b, r * q:(r + 1) * q, :] = oT.T
    return out
```
